# Optimizing a Trainium2 kernel written in Bass

```python
import math
import jax, jax.numpy as jnp
from jax import lax
import numpy as np

D_MODEL = 1024
BATCH = 32
SEQ = 2048
DEPTH = 2

GRID_W = 64
CTX_LEN = 256
EPS = 1e-6
N_BRANCH = 3
D_CONV_MIX = D_MODEL
SSD_D_INNER = D_MODEL
SSD_HEADDIM = 64
SSD_HEADS = SSD_D_INNER // SSD_HEADDIM
SSD_GROUPS = 2
SSD_STATE = 128
SSD_CHUNK = 128
XBC_DIM = SSD_D_INNER + 2 * SSD_GROUPS * SSD_STATE
NA_HEADS = 16
NA_HEAD_DIM = 64
NA_WIDTH = NA_HEADS * NA_HEAD_DIM
NA_KH = 8
NA_KW = 16
ROPE_BASE = 10000.0
D_FF = 4 * D_MODEL
IN_SPLITS = (D_CONV_MIX, D_CONV_MIX, D_CONV_MIX, SSD_D_INNER, XBC_DIM, 2 * SSD_HEADS, NA_WIDTH, NA_WIDTH, NA_WIDTH, N_BRANCH * D_MODEL)
IN_COLS = sum(IN_SPLITS)
IN_NAMES = ('conv_b', 'conv_c', 'conv_x', 'ssd_z', 'ssd_xbc', 'ssd_dt', 'na_q', 'na_k', 'na_v', 'gate')

kernel_name = 'hybrid_dit_conv_ssd_na_block'


def rmsnorm(x, w):
    xf = x.astype(jnp.float32)
    y = xf * lax.rsqrt(jnp.mean(xf * xf, axis=-1, keepdims=True) + EPS)
    return (y * w.astype(jnp.float32)).astype(x.dtype)


def modulate(x, shift, scale):
    return x * (1 + scale) + shift


def split_in(u):
    offs = [int(o) for o in np.cumsum(IN_SPLITS)[:-1]]
    return dict(zip(IN_NAMES, jnp.split(u, offs, axis=-1)))


def heads(t, n):
    return t.reshape(t.shape[0], t.shape[1], n, -1)


def dwconv3(x, w, b=None):
    xp = jnp.pad(x, ((0, 0), (1, 1), (0, 0)))
    y = xp[:, :-2] * w[0] + xp[:, 1:-1] * w[1] + xp[:, 2:] * w[2]
    return y if b is None else y + b


def _rev(t, d):
    return jnp.flip(t, axis=1) if d else t


def ssd_scan(X, a, Bm, Cm, h0, with_output):
    b, L, G, R, P = X.shape
    N = Bm.shape[-1]
    Q = SSD_CHUNK
    nc = L // Q
    X = X.reshape(b, nc, Q, G, R, P)
    Bm = Bm.reshape(b, nc, Q, G, N)
    Cm = Cm.reshape(b, nc, Q, G, N)
    a_cs = jnp.cumsum(a.astype(jnp.float32).reshape(b, nc, Q, G, R), axis=2)
    decay_states = jnp.exp(a_cs[:, :, -1:] - a_cs)
    states = jnp.einsum('bclgn,bclgrp->bcgrpn', Bm, X * decay_states[..., None])
    chunk_decay = jnp.exp(a_cs[:, :, -1])

    def step(h, inp):
        s_c, d_c = inp
        return h * d_c[..., None, None] + s_c, h

    h_final, h_before = lax.scan(step, h0.astype(jnp.float32), (jnp.swapaxes(states, 0, 1), jnp.swapaxes(chunk_decay, 0, 1)))
    if not with_output:
        return None, h_final
    h_before = jnp.swapaxes(h_before, 0, 1)
    seg = a_cs[:, :, :, None] - a_cs[:, :, None]
    tri = jnp.tril(jnp.ones((Q, Q), dtype=bool))[None, None, :, :, None, None]
    Lmat = jnp.exp(jnp.where(tri, seg, -jnp.inf))
    scores = jnp.einsum('bclgn,bcsgn->bclsg', Cm, Bm)
    y_diag = jnp.einsum('bclsgr,bcsgrp->bclgrp', scores[..., None] * Lmat, X)
    y_off = jnp.einsum('bclgn,bcgrpn->bclgrp', Cm, h_before) * jnp.exp(a_cs)[..., None]
    return (y_diag + y_off).reshape(b, L, G, R, P), h_final


def ssd_mix(xbc, z, dt_raw, conv_w, conv_b, a_log, dt_bias, d_skip, norm_w, h0, with_output):
    b, L, _ = xbc.shape
    G, R, P, N = SSD_GROUPS, SSD_HEADS // SSD_GROUPS, SSD_HEADDIM, SSD_STATE
    xbc = jax.nn.silu(dwconv3(xbc, conv_w, conv_b))
    xs, Bm, Cm = jnp.split(xbc, [SSD_D_INNER, SSD_D_INNER + G * N], axis=-1)
    xs = xs.reshape(b, L, G, R, P)
    Bm = Bm.reshape(b, L, G, N)
    Cm = Cm.reshape(b, L, G, N)
    dt = jax.nn.softplus(dt_raw.astype(jnp.float32).reshape(b, L, 2, G, R) + dt_bias.astype(jnp.float32).reshape(2, G, R))
    A = -jnp.exp(a_log.astype(jnp.float32)).reshape(2, G, R)
    ys, hs = [], []
    for d in range(2):
        dt_d = dt[:, :, d]
        y_d, h_d = ssd_scan(_rev(xs * dt_d[..., None], d), _rev(dt_d * A[d], d), _rev(Bm, d), _rev(Cm, d), h0[d], with_output)
        hs.append(h_d)
        if with_output:
            ys.append(_rev(y_d, d).astype(xs.dtype) + xs * d_skip[d].reshape(G, R)[:, :, None])
    if not with_output:
        return None, (hs[0], hs[1])
    y = (ys[0] + ys[1]).reshape(b, L, SSD_D_INNER) * jax.nn.silu(z)
    y = rmsnorm(y.reshape(b, L, G, -1), norm_w.reshape(G, -1)).reshape(b, L, SSD_D_INNER)
    return y, (hs[0], hs[1])


def axial_rope(L, dtype):
    t = jnp.arange(L, dtype=jnp.int32)
    row = (t // GRID_W).astype(jnp.float32)
    col = (t % GRID_W).astype(jnp.float32)
    half = NA_HEAD_DIM // 2
    inv = ROPE_BASE ** (-jnp.arange(0, half, 2, dtype=jnp.float32) / half)
    ang_r = row[:, None] * inv
    ang_c = col[:, None] * inv
    ang = jnp.concatenate([ang_r, ang_r, ang_c, ang_c], axis=-1)
    return jnp.cos(ang).astype(dtype), jnp.sin(ang).astype(dtype)


def apply_rope(x, cos, sin):
    def rot_half(u):
        u1, u2 = jnp.split(u, 2, axis=-1)
        return jnp.concatenate([-u2, u1], axis=-1)
    xr, xc = jnp.split(x, 2, axis=-1)
    rot = jnp.concatenate([rot_half(xr), rot_half(xc)], axis=-1)
    return x * cos[:, None, :] + rot * sin[:, None, :]


def na_attend(q, k, v, k_ctx, v_ctx, rpb):
    b, L, H, Dh = q.shape
    rows = L // GRID_W
    kh = min(NA_KH, rows)
    band = kh * GRID_W
    col = jnp.arange(GRID_W, dtype=jnp.int32)
    col_start = jnp.clip(col - NA_KW // 2, 0, GRID_W - NA_KW)
    col_ok = (col[None, :] >= col_start[:, None]) & (col[None, :] < col_start[:, None] + NA_KW)
    mask = jnp.broadcast_to(col_ok[:, None, :], (GRID_W, kh, GRID_W)).reshape(GRID_W, band)
    dc_idx = jnp.clip(col[None, :] - col[:, None], -(NA_KW - 1), NA_KW - 1) + NA_KW - 1
    q = q * (Dh ** -0.5)

    def one_row(r):
        r0 = jnp.clip(r - kh // 2, 0, rows - kh)
        qr = lax.dynamic_slice_in_dim(q, r * GRID_W, GRID_W, axis=1)
        kr = lax.dynamic_slice_in_dim(k, r0 * GRID_W, band, axis=1)
        vr = lax.dynamic_slice_in_dim(v, r0 * GRID_W, band, axis=1)
        dr_idx = r0 + jnp.arange(kh, dtype=jnp.int32) - r + NA_KH - 1
        bias = rpb[:, dr_idx[None, :, None], dc_idx[:, None, :]].reshape(H, GRID_W, band)
        s_lat = jnp.einsum('bqhd,bkhd->bhqk', qr, kr).astype(jnp.float32) + bias.astype(jnp.float32)[None]
        s_lat = jnp.where(mask[None, None], s_lat, -jnp.inf)
        s_ctx = jnp.einsum('bqhd,bkhd->bhqk', qr, k_ctx).astype(jnp.float32)
        p = jax.nn.softmax(jnp.concatenate([s_lat, s_ctx], axis=-1), axis=-1).astype(v.dtype)
        return jnp.einsum('bhqk,bkhd->bqhd', p[..., :band], vr) + jnp.einsum('bhqk,bkhd->bqhd', p[..., band:], v_ctx)

    out = lax.map(one_row, jnp.arange(rows, dtype=jnp.int32))
    return jnp.transpose(out, (1, 0, 2, 3, 4)).reshape(b, L, H * Dh)


def ctx_attend(q, k, v):
    b, L, H, Dh = q.shape
    s = jnp.einsum('bqhd,bkhd->bhqk', q * (Dh ** -0.5), k).astype(jnp.float32)
    p = jax.nn.softmax(s, axis=-1).astype(v.dtype)
    return jnp.einsum('bhqk,bkhd->bqhd', p, v).reshape(b, L, H * Dh)


def merge_branches(y_conv, y_ssd, y_na, gate_logits, w_br_conv, w_br_ssd, w_br_na, w_out):
    g_conv, g_ssd, g_na = jnp.split(jax.nn.sigmoid(gate_logits), N_BRANCH, axis=-1)
    merged = g_conv * (y_conv @ w_br_conv) + g_ssd * (y_ssd @ w_br_ssd) + g_na * (y_na @ w_br_na)
    return merged @ w_out


def sqrelu_mlp(x, w1, w2):
    return jnp.square(jax.nn.relu(x @ w1)) @ w2


def setup_inputs(seed: int = 0) -> dict:
    key = jax.random.key(seed)
    ks = iter(jax.random.split(key, 32))

    def nrm(shape, s):
        return jax.random.normal(next(ks), shape, jnp.float32) * s

    def gain(shape):
        return 1.0 + nrm(shape, 0.01)

    dt0 = jnp.exp(jax.random.uniform(next(ks), (DEPTH, 2, SSD_HEADS), jnp.float32) * (math.log(0.1) - math.log(0.001)) + math.log(0.001))
    return {
        'x': nrm((BATCH, SEQ, D_MODEL), 1.0),
        'c': nrm((BATCH, D_MODEL), 1.0),
        'ctx': nrm((BATCH, CTX_LEN, D_MODEL), 1.0),
        'c_ctx': nrm((D_MODEL,), 1.0),
        'w_ada': nrm((DEPTH, D_MODEL, 6 * D_MODEL), 0.5 * D_MODEL ** -0.5),
        'b_ada': nrm((DEPTH, 6 * D_MODEL), 0.01),
        'norm1_w': gain((DEPTH, D_MODEL)),
        'w_in': nrm((DEPTH, D_MODEL, IN_COLS), D_MODEL ** -0.5),
        'conv_mix_w': nrm((DEPTH, 3, D_CONV_MIX), 3 ** -0.5),
        'ssd_conv_w': nrm((DEPTH, 3, XBC_DIM), 3 ** -0.5),
        'ssd_conv_b': nrm((DEPTH, XBC_DIM), 0.01),
        'ssd_a_log': jnp.log(jax.random.uniform(next(ks), (DEPTH, 2, SSD_HEADS), jnp.float32, 1.0, 16.0)),
        'ssd_dt_bias': dt0 + jnp.log(-jnp.expm1(-dt0)),
        'ssd_d': gain((DEPTH, 2, SSD_HEADS)),
        'ssd_norm_w': gain((DEPTH, SSD_D_INNER)),
        'na_rpb': nrm((DEPTH, NA_HEADS, 2 * NA_KH - 1, 2 * NA_KW - 1), 0.02),
        'w_br_conv': nrm((DEPTH, D_CONV_MIX, D_MODEL), D_CONV_MIX ** -0.5),
        'w_br_ssd': nrm((DEPTH, SSD_D_INNER, D_MODEL), SSD_D_INNER ** -0.5),
        'w_br_na': nrm((DEPTH, NA_WIDTH, D_MODEL), NA_WIDTH ** -0.5),
        'w_out': nrm((DEPTH, D_MODEL, D_MODEL), D_MODEL ** -0.5),
        'norm2_w': gain((DEPTH, D_MODEL)),
        'w_ff1': nrm((DEPTH, D_MODEL, D_FF), D_MODEL ** -0.5),
        'w_ff2': nrm((DEPTH, D_FF, D_MODEL), D_FF ** -0.5),
        'final_norm_w': gain((D_MODEL,)),
    }


def reference(x, c, ctx, c_ctx, w_ada, b_ada, norm1_w, w_in, conv_mix_w, ssd_conv_w, ssd_conv_b, ssd_a_log, ssd_dt_bias, ssd_d, ssd_norm_w, na_rpb, w_br_conv, w_br_ssd, w_br_na, w_out, norm2_w, w_ff1, w_ff2, final_norm_w):
    b, L, _ = x.shape
    h, hc = x, ctx
    silu_c, silu_cc = jax.nn.silu(c), jax.nn.silu(c_ctx)
    cos, sin = axial_rope(L, x.dtype)
    G, R, P, N = SSD_GROUPS, SSD_HEADS // SSD_GROUPS, SSD_HEADDIM, SSD_STATE
    zero_state = jnp.zeros((b, G, R, P, N), jnp.float32)
    for l in range(DEPTH):
        last = l == DEPTH - 1
        mod = (silu_c @ w_ada[l] + b_ada[l])[:, None, :]
        mod_c = (silu_cc @ w_ada[l] + b_ada[l])[None, None, :]
        sh1, sc1, gt1, sh2, sc2, gt2 = jnp.split(mod, 6, axis=-1)
        csh1, csc1, cgt1, csh2, csc2, cgt2 = jnp.split(mod_c, 6, axis=-1)
        pl = split_in(modulate(rmsnorm(h, norm1_w[l]), sh1, sc1) @ w_in[l])
        pc = split_in(modulate(rmsnorm(hc, norm1_w[l]), csh1, csc1) @ w_in[l])
        ssd_p = (ssd_conv_w[l], ssd_conv_b[l], ssd_a_log[l], ssd_dt_bias[l], ssd_d[l], ssd_norm_w[l])
        y_ssd_c, ctx_states = ssd_mix(pc['ssd_xbc'], pc['ssd_z'], pc['ssd_dt'], *ssd_p, (zero_state, zero_state), not last)
        y_ssd, _ = ssd_mix(pl['ssd_xbc'], pl['ssd_z'], pl['ssd_dt'], *ssd_p, ctx_states, True)
        k_c, v_c = heads(pc['na_k'], NA_HEADS), heads(pc['na_v'], NA_HEADS)
        q_l = apply_rope(heads(pl['na_q'], NA_HEADS), cos, sin)
        k_l = apply_rope(heads(pl['na_k'], NA_HEADS), cos, sin)
        y_na = na_attend(q_l, k_l, heads(pl['na_v'], NA_HEADS), k_c, v_c, na_rpb[l])
        y_conv = pl['conv_b'] * dwconv3(pl['conv_c'] * pl['conv_x'], conv_mix_w[l])
        h = h + gt1 * merge_branches(y_conv, y_ssd, y_na, pl['gate'], w_br_conv[l], w_br_ssd[l], w_br_na[l], w_out[l])
        h = h + gt2 * sqrelu_mlp(modulate(rmsnorm(h, norm2_w[l]), sh2, sc2), w_ff1[l], w_ff2[l])
        if not last:
            y_conv_c = pc['conv_b'] * dwconv3(pc['conv_c'] * pc['conv_x'], conv_mix_w[l])
            y_na_c = ctx_attend(heads(pc['na_q'], NA_HEADS), k_c, v_c)
            hc = hc + cgt1 * merge_branches(y_conv_c, y_ssd_c, y_na_c, pc['gate'], w_br_conv[l], w_br_ssd[l], w_br_na[l], w_out[l])
            hc = hc + cgt2 * sqrelu_mlp(modulate(rmsnorm(hc, norm2_w[l]), csh2, csc2), w_ff1[l], w_ff2[l])
    return rmsnorm(h, final_norm_w)
```

```python
import numpy as np
import ml_dtypes
import concourse.bass as bass
import concourse.mybir as mybir
from concourse.bass_utils import run_bass_kernel_spmd
from contextlib import ExitStack

F32 = mybir.dt.float32
BF16 = mybir.dt.bfloat16
AF = mybir.ActivationFunctionType
ALU = mybir.AluOpType
AX = mybir.AxisListType

NCORES = 8
NB = 4
D = 1024
L = 2048
CT = 256
T = L + CT
DEPTH = 2
INC = 11808
DFF = 4096
EPS = 1e-6
NEG = -30000.0
_DEFER_CVT = True
import os as _os0
_SKIPW = bool(_os0.environ.get('SKIPW'))
O_CB, O_CC, O_CX, O_Z, O_XBC, O_DT, O_Q, O_K, O_V, O_G = 0, 1024, 2048, 3072, 4096, 5632, 5664, 6688, 7712, 8736


class Res:
    __slots__ = ("name", "w", "r")

    def __init__(self, name=""):
        self.name = name
        self.w = None
        self.r = {}


class Prog:
    ENG = ("pe", "act", "dve", "pool", "sp")
    NLANE = {"sp": 8, "act": 2, "pool": 6}

    use_scopes = False

    def __init__(self, nc, es):
        self.nc = nc
        self.q = {e: [] for e in self.ENG}
        self.sem = {}
        self.cnt = {}
        for e in self.ENG:
            self.sem[e] = es.enter_context(nc.semaphore("s_" + e))
            self.cnt[e] = 0
        self.lane_rr = {}
        for e, n in self.NLANE.items():
            self.lane_rr[e] = 0
            for i in range(n):
                k = "%s_l%d" % (e, i)
                self.sem[k] = es.enter_context(nc.semaphore("s_" + k))
                self.cnt[k] = 0
        self.seen = {e: {} for e in self.ENG}

    def _deps(self, e, reads, writes):
        deps = {}
        for r in reads:
            if r.w is not None:
                o, i = r.w
                if deps.get(o, 0) < i:
                    deps[o] = i
        for w in writes:
            if w.w is not None:
                o, i = w.w
                if deps.get(o, 0) < i:
                    deps[o] = i
            for o, i in w.r.items():
                if deps.get(o, 0) < i:
                    deps[o] = i
        waits = []
        seen = self.seen[e]
        for o, i in deps.items():
            if seen.get(o, 0) < i:
                seen[o] = i
                waits.append((o, i))
        return waits

    def op(self, e, fn, reads=(), writes=()):
        waits = self._deps(e, reads, writes)
        self.cnt[e] += 1
        idx = self.cnt[e]
        self.q[e].append((fn, waits, e, 1))
        for r in reads:
            r.r[e] = idx
        for w in writes:
            w.w = (e, idx)
            w.r = {}

    def dma(self, e, fn, reads=(), writes=()):
        n = self.NLANE[e]
        li = self.lane_rr[e]
        self.lane_rr[e] = (li + 1) % n
        k = "%s_l%d" % (e, li)
        waits = self._deps(e, reads, writes)
        prev = self.cnt[k]
        if prev > 0 and self.seen[e].get(k, 0) < prev:
            self.seen[e][k] = prev
            waits.append((k, prev))
        self.cnt[k] += 16
        val = self.cnt[k]
        self.q[e].append((fn, waits, k, 16))
        for r in reads:
            r.r[k] = val
        for w in writes:
            w.w = (k, val)
            w.r = {}

    def barrier(self):
        for e in self.ENG:
            waits = []
            for k, c in self.cnt.items():
                if k != e and c > 0 and self.seen[e].get(k, 0) < c:
                    self.seen[e][k] = c
                    waits.append((k, c))
            if waits:
                self.q[e].append((None, waits, None, 0))

    def emit(self, scope=None):
        nc = self.nc
        engs = {"pe": "tensor", "act": "scalar", "dve": "vector", "pool": "gpsimd", "sp": "sync"}
        with ExitStack() as _es:
            if scope is not None and self.use_scopes:
                _es.enter_context(nc.named_scope(scope))
            block = _es.enter_context(nc.Block())
            for e in self.ENG:
                def body(eng, e=e):
                    for fn, waits, k, inc in self.q[e]:
                        for o, i in waits:
                            eng.wait_ge(self.sem[o], i)
                        if fn is not None:
                            ins = fn(eng)
                            ins.then_inc(self.sem[k], inc)
                getattr(block, engs[e])(body)
        self.q = {e: [] for e in self.ENG}


class TB:
    gid = 0

    def __init__(self, nc, es):
        self.nc = nc
        self.es = es
        self.n = 0

    def __call__(self, shape, dt=F32, name=None):
        self.n += 1
        TB.gid += 1
        t = self.es.enter_context(self.nc.sbuf_tensor("%s_%d" % (name or "t", TB.gid), list(shape), dt))
        return t, Res(name or "t")


def build(nb=NB, layers=(0, 1), dbg=()):
    nc = bass.Bass("TRN2", target_bir_lowering=False)

    def din(name, shape, dt=F32):
        return nc.dram_tensor(name, list(shape), dt, kind="ExternalInput").ap()

    def dscr(name, shape, dt=BF16):
        return nc.dram_tensor(name, list(shape), dt, kind="Internal").ap()

    xT = din("xT", [nb, D, T])
    cT = din("cT", [128, 8, 5])
    w_ada = din("w_ada", [DEPTH, D, 6 * D])
    w_in = din("w_in", [DEPTH, D, INC])
    w_brc = din("w_br_conv", [DEPTH, D, D])
    w_brs = din("w_br_ssd", [DEPTH, D, D])
    w_brn = din("w_br_na", [DEPTH, D, D])
    w_out = din("w_out", [DEPTH, D, D])
    w_ff1 = din("w_ff1", [DEPTH, D, DFF])
    w_ff2 = din("w_ff2", [DEPTH, DFF, D])
    b_ada_p = din("b_ada_p", [DEPTH, 128, 48])
    norm1_p = din("norm1_p", [DEPTH, 128, 8])
    norm2_p = din("norm2_p", [DEPTH, 128, 8])
    fnorm_p = din("fnorm_p", [128, 8])
    convw_p = din("convw_p", [DEPTH, 128, 8, 3])
    sconvw_p = din("sconvw_p", [DEPTH, 128, 12, 3])
    sconvb_p = din("sconvb_p", [DEPTH, 128, 12])
    dtb_d = din("dt_bias", [DEPTH, 32])
    alog_d = din("a_log", [DEPTH, 32])
    dsk_d = din("ssd_d", [DEPTH, 32])
    snw_d = din("ssd_norm_w", [DEPTH, D])
    TT_d = din("TT", [DEPTH, 64, 16, 15, 64])
    cident = din("c_ident", [128, 128])
    ctriu = din("c_triu", [128, 128])
    ctril = din("c_tril", [128, 128])
    cnegf = din("c_negf", [128, 512])
    cnegb = din("c_negb", [128, 512])
    ccos = din("c_cos", [128, L])
    csin = din("c_sin", [128, L])
    crot = din("c_rot", [128, 128])
    outT = nc.dram_tensor("outT", [nb, D, L], F32, kind="ExternalOutput").ap()

    HT = dscr("HT", [D, T], F32)
    S_gate = dscr("S_gate", [3 * D, T])
    S_xbc = dscr("S_xbc", [1536, T])
    S_dt = dscr("S_dt", [T, 32], F32)
    S_q = dscr("S_q", [D, T])
    S_k = dscr("S_k", [D, T])
    S_v = dscr("S_v", [T, D])
    S_z = dscr("S_z", [T, D])
    Y_conv = dscr("Y_conv", [D, T])
    Y_ssd = dscr("Y_ssd", [D, T])
    Y_na = dscr("Y_na", [D, T])
    YF = dscr("YF", [18, 128, D], F32)
    NBLK = 76
    Wb_in = dscr("Wb_in", [DEPTH, NBLK, 128, 8, 128])
    Wb_zv = dscr("Wb_zv", [DEPTH, D, 2048])
    Wb_dt = dscr("Wb_dt", [DEPTH, D, 32])
    Wb_br = dscr("Wb_br", [DEPTH, 4, D, D])
    Wb_f1 = dscr("Wb_f1", [DEPTH, D, DFF])
    Wb_f2 = dscr("Wb_f2", [DEPTH, DFF, D])
    BLK_COL = [j * 128 for j in range(24)] + [O_XBC + j * 128 for j in range(12)] + [O_Q + j * 128 for j in range(16)] + [O_G + j * 128 for j in range(24)]
    BLK_CB, BLK_CC, BLK_CX, BLK_XBC, BLK_Q, BLK_K, BLK_G = 0, 8, 16, 24, 36, 44, 52

    dbg_out = {}

    def dbg_dump(P, name, src_ap, shape, dt):
        if name in dbg and name not in dbg_out:
            o = nc.dram_tensor("dbg_" + name, list(shape), dt, kind="ExternalOutput").ap()
            dbg_out[name] = o
            P.barrier()
            P.dma("sp", lambda e, o=o: e.dma_start(out=o, in_=src_ap))
            P.barrier()

    def fm(ap2d, t0, n):
        return ap2d[:, t0:t0 + n].rearrange("(k p) t -> p k t", p=128)

    with ExitStack() as es:
        P = Prog(nc, es)
        G = TB(nc, es)
        identf, Ridentf = G([128, 128], F32, "identf")
        ident, Rident = G([128, 128], BF16, "ident")
        onesf, Ronesf = G([128, 128], F32, "onesf")
        onesb, Ronesb = G([128, 128], BF16, "onesb")
        triu, Rtriu = G([128, 128], F32, "triu")
        tril, Rtril = G([128, 128], F32, "tril")
        negf, Rnegf = G([128, 512], F32, "negf")
        negb, Rnegb = G([128, 512], F32, "negb")
        modt, Rmod = G([128, DEPTH, 48, 5], F32, "mod")
        A1, RA1 = G([128, DEPTH, 8, 5], F32, "A1")
        A2, RA2 = G([128, DEPTH, 8, 5], F32, "A2")
        fnw, Rfnw = G([128, 8], F32, "fnw")
        def psum_alloc(scope, nf, nb16):
            TB.gid += 1
            ps_ = [(scope.enter_context(nc.psum_tensor("ps%d_%d" % (i, TB.gid), [128, 512], F32)), Res("ps%d" % i)) for i in range(nf)]
            pb_ = [(scope.enter_context(nc.psum_tensor("psb%d_%d" % (i, TB.gid), [128, 1024], BF16)), Res("psb%d" % i)) for i in range(nb16)]
            return ps_, pb_

        P.dma("sp", lambda e: e.dma_start(out=identf[:], in_=cident), writes=[Ridentf])
        P.dma("sp", lambda e: e.dma_start(out=triu[:], in_=ctriu), writes=[Rtriu])
        P.dma("sp", lambda e: e.dma_start(out=tril[:], in_=ctril), writes=[Rtril])
        P.dma("sp", lambda e: e.dma_start(out=negf[:], in_=cnegf), writes=[Rnegf])
        P.dma("sp", lambda e: e.dma_start(out=negb[:], in_=cnegb), writes=[Rnegb])
        P.dma("sp", lambda e: e.dma_start(out=fnw[:], in_=fnorm_p), writes=[Rfnw])
        P.op("dve", lambda e: e.tensor_copy(out=ident[:], in_=identf[:]), reads=[Ridentf], writes=[Rident])
        P.op("pool", lambda e: e.memset(onesf[:], 1.0), writes=[Ronesf])
        P.op("pool", lambda e: e.memset(onesb[:], 1.0), writes=[Ronesb])

        with ExitStack() as ms:
            M = TB(nc, ms)
            PS, PSB = psum_alloc(ms, 2, 0)
            sc, Rsc = M([128, 8, 5], F32, "sc")
            wa = [M([128, 8, 768], F32, "wa") for _ in range(2)]
            bap, Rbap = M([128, DEPTH, 48], F32, "bap")
            n1, Rn1 = M([128, DEPTH, 8], F32, "n1")
            n2, Rn2 = M([128, DEPTH, 8], F32, "n2")
            P.dma("sp", lambda e: e.dma_start(out=sc[:], in_=cT), writes=[Rsc])
            P.op("act", lambda e: e.activation(out=sc[:], in_=sc[:], func=AF.Silu), reads=[Rsc], writes=[Rsc])
            for l in range(DEPTH):
                P.dma("sp", lambda e, l=l: e.dma_start(out=bap[:, l, :], in_=b_ada_p[l]), writes=[Rbap])
                P.dma("sp", lambda e, l=l: e.dma_start(out=n1[:, l, :], in_=norm1_p[l]), writes=[Rn1])
                P.dma("sp", lambda e, l=l: e.dma_start(out=n2[:, l, :], in_=norm2_p[l]), writes=[Rn2])
            it = 0
            for l in range(DEPTH):
                for g8 in range(8):
                    wt, Rwt = wa[it % 2]
                    it += 1
                    P.dma("sp", lambda e, l=l, g8=g8, wt=wt: e.dma_start(
                        out=wt[:], in_=w_ada[l, :, g8 * 768:(g8 + 1) * 768].rearrange("(k p) c -> p k c", p=128)), writes=[Rwt])
                    ps, Rps = PS[g8 % 2]

                    def mm(e, wt=wt, ps=ps):
                        for j in range(6):
                            for k in range(8):
                                ins = e.matmul(ps[:, j * 8:j * 8 + 5], lhsT=wt[:, k, j * 128:(j + 1) * 128], rhs=sc[:, k, :],
                                               start=(k == 0), stop=(k == 7))
                        return ins
                    P.op("pe", mm, reads=[Rwt, Rsc], writes=[Rps])
                    P.op("dve", lambda e, l=l, g8=g8, ps=ps: e.tensor_tensor(
                        out=modt[:, l, g8 * 6:(g8 + 1) * 6, :], in0=ps[:, 0:48].rearrange("p (j c) -> p j c", c=8)[:, :, 0:5],
                        in1=bap[:, l, g8 * 6:(g8 + 1) * 6].unsqueeze(2).to_broadcast([128, 6, 5]), op=ALU.add),
                        reads=[Rps, Rbap], writes=[Rmod])
            for l in range(DEPTH):
                for (At, RAt, nn, Rnn, j) in ((A1, RA1, n1, Rn1, 1), (A2, RA2, n2, Rn2, 4)):
                    P.op("dve", lambda e, At=At, j=j, l=l: e.tensor_scalar(
                        out=At[:, l, :, :], in0=modt[:, l, j * 8:(j + 1) * 8, :], scalar1=1.0, scalar2=None, op0=ALU.add),
                        reads=[Rmod], writes=[RAt])
                    P.op("dve", lambda e, At=At, nn=nn, l=l: e.tensor_tensor(
                        out=At[:, l, :, :], in0=At[:, l, :, :], in1=nn[:, l, :].unsqueeze(2).to_broadcast([128, 8, 5]), op=ALU.mult),
                        reads=[RAt, Rnn], writes=[RAt])
            P.barrier()
            P.emit()
        if "mod" in dbg:
            o = nc.dram_tensor("dbg_mod", [128, DEPTH, 48, 5], F32, kind="ExternalOutput").ap()
            dbg_out["mod"] = o
            P.dma("sp", lambda e, o=o: e.dma_start(out=o, in_=modt[:]), reads=[Rmod])

        def mcol(l, j, fc, col):
            return modt[:, l, j * 8 + fc, col:col + 1]

        with ExitStack() as cs0:
            Cv = TB(nc, cs0)
            NCB = 4
            cin = [Cv([128, 2048], F32, "cin") for _ in range(NCB)]
            cout = [Cv([128, 2048], BF16, "cout") for _ in range(NCB)]
            cvi = [0]

            deferred = []
            defer_mode = [False]

            def cvt(src_ap, dst_ap, shp):
                if defer_mode[0]:
                    deferred.append((src_ap, dst_ap, shp))
                    return
                i = cvi[0]
                cvi[0] += 1
                ti, Rti = cin[i % NCB]
                to, Rto = cout[i % NCB]
                n = 1
                for d_ in shp[1:]:
                    n *= d_
                if len(shp) == 3:
                    tv = ti[:, 0:n].rearrange("p (a b) -> p a b", b=shp[2])
                    ov = to[:, 0:n].rearrange("p (a b) -> p a b", b=shp[2])
                else:
                    tv = ti[:, 0:n]
                    ov = to[:, 0:n]
                P.dma("sp", lambda e: e.dma_start(out=tv, in_=src_ap), writes=[Rti])
                eng = ("act", "dve", "pool")[i % 3]
                if eng == "act":
                    P.op("act", lambda e: e.copy(out=to[:, 0:n], in_=ti[:, 0:n]), reads=[Rti], writes=[Rto])
                else:
                    P.op(eng, lambda e: e.tensor_copy(out=to[:, 0:n], in_=ti[:, 0:n]), reads=[Rti], writes=[Rto])
                P.dma("sp", lambda e: e.dma_start(out=dst_ap, in_=ov), reads=[Rto])
            for l in range(DEPTH):
                if l not in layers:
                    continue
                defer_mode[0] = (l != layers[0]) and _DEFER_CVT
                for k in range(8):
                    rows = slice(k * 128, (k + 1) * 128)
                    for (b0, nb_) in ((0, 16), (16, 8), (24, 12), (36, 16), (52, 16), (68, 8)):
                        c0 = BLK_COL[b0]
                        cvt(w_in[l, rows, c0:c0 + nb_ * 128].rearrange("p (b c) -> p b c", c=128),
                            Wb_in[l, b0:b0 + nb_, :, k, :].rearrange("b p c -> p b c"), [128, nb_, 128])
                    cvt(w_in[l, rows, O_Z:O_Z + 1024], Wb_zv[l, rows, 0:1024], [128, 1024])
                    cvt(w_in[l, rows, O_V:O_V + 1024], Wb_zv[l, rows, 1024:2048], [128, 1024])
                    cvt(w_in[l, rows, O_DT:O_DT + 32], Wb_dt[l, rows, :], [128, 32])
                    dm_ = defer_mode[0]
                    defer_mode[0] = _DEFER_CVT
                    for c2 in range(2):
                        cvt(w_ff1[l, rows, c2 * 2048:(c2 + 1) * 2048], Wb_f1[l, rows, c2 * 2048:(c2 + 1) * 2048], [128, 2048])
                    defer_mode[0] = dm_
                defer_mode[0] = _DEFER_CVT
                for wi, wd in enumerate((w_brc, w_brs, w_brn, w_out)):
                    for k2 in range(4):
                        cvt(wd[l, k2 * 256:(k2 + 1) * 256, :].rearrange("(k p) c -> p k c", p=128),
                            Wb_br[l, wi, k2 * 256:(k2 + 1) * 256, :].rearrange("(k p) c -> p k c", p=128), [128, 2, 1024])
                for k2 in range(16):
                    cvt(w_ff2[l, k2 * 256:(k2 + 1) * 256, :].rearrange("(k p) c -> p k c", p=128),
                        Wb_f2[l, k2 * 256:(k2 + 1) * 256, :].rearrange("(k p) c -> p k c", p=128), [128, 2, 1024])
            P.barrier()
            P.emit("cvt")

        bg = {"i": 0, "bufs": None}

        def bg_alloc(Sx):
            bg["bufs"] = ([Sx([128, 2048], F32, "bgin") for _ in range(3)], [Sx([128, 2048], BF16, "bgout") for _ in range(3)])

        def bg_cvt(ntask):
            for _ in range(ntask):
                if bg["i"] >= len(deferred) or bg["bufs"] is None:
                    return
                src_ap, dst_ap, shp = deferred[bg["i"]]
                i = bg["i"]
                bg["i"] += 1
                ti, Rti = bg["bufs"][0][i % 3]
                to, Rto = bg["bufs"][1][i % 3]
                n = 1
                for d_ in shp[1:]:
                    n *= d_
                if len(shp) == 3:
                    tv = ti[:, 0:n].rearrange("p (a b) -> p a b", b=shp[2])
                    ov = to[:, 0:n].rearrange("p (a b) -> p a b", b=shp[2])
                else:
                    tv = ti[:, 0:n]
                    ov = to[:, 0:n]
                P.dma("sp", lambda e, tv=tv, src_ap=src_ap: e.dma_start(out=tv, in_=src_ap), writes=[Rti])
                P.op("act", lambda e, to=to, ti=ti, n=n: e.copy(out=to[:, 0:n], in_=ti[:, 0:n]), reads=[Rti], writes=[Rto])
                P.dma("act", lambda e, ov=ov, dst_ap=dst_ap: e.dma_start(out=dst_ap, in_=ov), reads=[Rto])

        def rms_tile(l_or_none, At, col, htile, Rht, n, sq, Rsq, rstd, Rrstd, tmp, Rtmp, outT_tile, Rout, shift_j, psb):
            ps, Rps = psb
            P.op("act", lambda e: e.activation(out=sq[:, :, 0:n], in_=htile[:, :, 0:n], func=AF.Square), reads=[Rht], writes=[Rsq])

            def mm(e):
                for k in range(8):
                    ins = e.matmul(ps[:, 0:n], lhsT=onesf[:], rhs=sq[:, k, 0:n], start=(k == 0), stop=(k == 7))
                return ins
            P.op("pe", mm, reads=[Rsq, Ronesf], writes=[Rps])
            P.op("act", lambda e: e.activation(out=rstd[:, 0:n], in_=ps[:, 0:n], func=AF.Sqrt, bias=EPS, scale=1.0 / D), reads=[Rps], writes=[Rrstd])
            P.op("dve", lambda e: e.reciprocal(out=rstd[:, 0:n], in_=rstd[:, 0:n]), reads=[Rrstd], writes=[Rrstd])
            for k in range(8):
                if l_or_none is None:
                    sc_ap = fnw[:, k:k + 1]
                else:
                    sc_ap = At[:, l_or_none, k, col:col + 1]
                if shift_j is None:
                    P.op("dve", lambda e, k=k, sc_ap=sc_ap: e.scalar_tensor_tensor(
                        out=outT_tile[:, k, 0:n], in0=htile[:, k, 0:n], scalar=sc_ap, in1=rstd[:, 0:n], op0=ALU.mult, op1=ALU.mult),
                        reads=[Rht, Rrstd, RA1, RA2, Rfnw], writes=[Rout])
                else:
                    P.op("dve", lambda e, k=k, sc_ap=sc_ap: e.scalar_tensor_tensor(
                        out=tmp[:, 0:n], in0=htile[:, k, 0:n], scalar=sc_ap, in1=rstd[:, 0:n], op0=ALU.mult, op1=ALU.mult),
                        reads=[Rht, Rrstd, RA1, RA2], writes=[Rtmp])
                    P.op("act", lambda e, k=k: e.activation(out=outT_tile[:, k, 0:n], in_=tmp[:, 0:n], func=AF.Identity,
                                                           bias=mcol(l_or_none, shift_j, k, col), scale=1.0),
                         reads=[Rtmp, Rmod], writes=[Rout])

        TOK_TILES = [(0, 512), (512, 512), (1024, 512), (1536, 512), (2048, 256)]

        def wsrc(wd, l, c0, cw, kc=8):
            return wd[l, :, c0:c0 + cw].rearrange("(k p) c -> p k c", p=128)

        for b in range(nb):
            for l in layers:
                src_h = xT[b] if l == layers[0] else HT
                last = (l == DEPTH - 1)
                P.barrier()
                with ExitStack() as s1:
                    S = TB(nc, s1)
                    PS, PSB = psum_alloc(s1, 6, 0)
                    hmT, RhmT_all = S([128, 8, T], BF16, "hmT")
                    RhmT_t = {t0_: Res("hmT%d" % t0_) for (t0_, n_) in TOK_TILES}
                    hld = [S([128, 8, 512], F32, "hld") for _ in range(2)]
                    sq, Rsq = S([128, 8, 512], BF16, "sq")
                    rstd, Rrstd = S([128, 512], F32, "rstd")
                    tmp, Rtmp = S([128, 512], F32, "tmp")
                    tmpsA = [(tmp, Rtmp)] + [S([128, 512], F32, "tmpA2")]

                    def norm_tile(ti):
                        t0, n = TOK_TILES[ti]
                        ht, Rht = hld[ti % 2]
                        P.dma("sp", lambda e, ht=ht, t0=t0, n=n: e.dma_start(out=ht[:, :, 0:n], in_=fm(src_h, t0, n)), writes=[Rht])
                        col = b if t0 < L else 4
                        class _V:
                            pass
                        hv = hmT[:, :, t0:t0 + n]
                        ps_b = PS[ti % 2]
                        P.op("act", lambda e, ht=ht, n=n: e.activation(out=sq[:, :, 0:n], in_=ht[:, :, 0:n], func=AF.Square), reads=[Rht], writes=[Rsq])
                        ps, Rps = ps_b

                        def mm(e, ps=ps, n=n):
                            for k in range(8):
                                ins = e.matmul(ps[:, 0:n], lhsT=onesb[:], rhs=sq[:, k, 0:n], start=(k == 0), stop=(k == 7))
                            return ins
                        P.op("pe", mm, reads=[Rsq, Ronesb], writes=[Rps])
                        P.op("act", lambda e, ps=ps, n=n: e.activation(out=rstd[:, 0:n], in_=ps[:, 0:n], func=AF.Sqrt, bias=EPS, scale=1.0 / D), reads=[Rps], writes=[Rrstd])
                        P.op("dve", lambda e, n=n: e.reciprocal(out=rstd[:, 0:n], in_=rstd[:, 0:n]), reads=[Rrstd], writes=[Rrstd])
                        for k in range(8):
                            tmp_, Rtmp_ = tmpsA[k % 2]
                            P.op("dve", lambda e, k=k, ht=ht, n=n, col=col, tmp_=tmp_: e.scalar_tensor_tensor(
                                out=tmp_[:, 0:n], in0=ht[:, k, 0:n], scalar=A1[:, l, k, col:col + 1], in1=rstd[:, 0:n], op0=ALU.mult, op1=ALU.mult),
                                reads=[Rht, Rrstd, RA1], writes=[Rtmp_])
                            P.op("act", lambda e, k=k, t0=t0, n=n, col=col, tmp_=tmp_: e.activation(
                                out=hmT[:, k, t0:t0 + n], in_=tmp_[:, 0:n], func=AF.Identity, bias=mcol(l, 0, k, col), scale=1.0),
                                reads=[Rtmp_, Rmod], writes=[RhmT_t[t0]])
                    wb = [S([128, 8, 128], BF16, "wb") for _ in range(7)]
                    wbi = [0]

                    blk_order = ([x for j in range(8) for x in (BLK_CB + j, BLK_CC + j, BLK_CX + j)] + [BLK_XBC + j for j in range(12)]
                                 + [BLK_G + j for j in range(24)] + [BLK_Q + j for j in range(16)])
                    wq_issued = [0]
                    NWB = 7
                    PREF = 5

                    def _issue_until(nmax):
                        while wq_issued[0] < min(nmax, len(blk_order)):
                            i_ = wq_issued[0]
                            wt, Rwt = wb[i_ % NWB]
                            P.dma("sp", lambda e, wt=wt, blk=blk_order[i_]: e.dma_start(out=wt[:], in_=Wb_in[l, blk]), writes=[Rwt])
                            wq_issued[0] += 1

                    def load_w(blk, cw=128):
                        i_ = wbi[0]
                        assert blk_order[i_] == blk, (i_, blk, blk_order[i_])
                        wbi[0] += 1
                        _issue_until(i_ + PREF)
                        return wb[i_ % NWB]
                    psi = [0]

                    def mm_tile(wt, Rwt, t0, n, cw=128):
                        ps, Rps = PS[psi[0] % 4]
                        psi[0] += 1

                        def mm(e):
                            for k in range(8):
                                ins = e.matmul(ps[0:cw, 0:n], lhsT=wt[:, k, 0:cw], rhs=hmT[:, k, t0:t0 + n], start=(k == 0), stop=(k == 7))
                            return ins
                        P.op("pe", mm, reads=[Rwt, RhmT_t[t0]], writes=[Rps])
                        return ps, Rps

                    cxp, Rcxp = S([128, L + 2], F32, "cxp")
                    cxc, Rcxc = S([128, CT + 2], F32, "cxc")
                    cbs, Rcbs = S([128, T], BF16, "cbs")
                    acc, Racc = S([128, T], F32, "acc")
                    ybf, Rybf = S([128, T], BF16, "ybf")
                    csb, Rcsb = S([128, 512], F32, "csb")
                    cw_t, Rcw = S([128, 8, 3], F32, "cw")
                    sw_t, Rsw = S([128, 12, 3], F32, "sw")
                    sb_t, Rsb = S([128, 12], F32, "sb")
                    P.dma("sp", lambda e: e.dma_start(out=cw_t[:], in_=convw_p[l]), writes=[Rcw])
                    P.dma("sp", lambda e: e.dma_start(out=sw_t[:], in_=sconvw_p[l]), writes=[Rsw])
                    P.dma("sp", lambda e: e.dma_start(out=sb_t[:], in_=sconvb_p[l]), writes=[Rsb])
                    P.op("pool", lambda e: e.memset(cxp[:], 0.0), writes=[Rcxp])
                    P.op("pool", lambda e: e.memset(cxc[:], 0.0), writes=[Rcxc])

                    def pad_dst(t0, n):
                        if t0 < L:
                            return cxp[:, 1 + t0:1 + t0 + n], Rcxp
                        return cxc[:, 1:1 + n], Rcxc

                    def conv3(wtile, j, bias_ap):
                        for (pad, Rpad, o0, n) in ((cxp, Rcxp, 0, L), (cxc, Rcxc, L, CT)):
                            if bias_ap is None:
                                P.op("dve", lambda e, pad=pad, o0=o0, n=n: e.tensor_scalar(
                                    out=acc[:, o0:o0 + n], in0=pad[:, 0:n], scalar1=wtile[:, j, 0:1], scalar2=None, op0=ALU.mult),
                                    reads=[Rpad, Rcw, Rsw], writes=[Racc])
                            else:
                                P.op("dve", lambda e, pad=pad, o0=o0, n=n: e.tensor_scalar(
                                    out=acc[:, o0:o0 + n], in0=pad[:, 0:n], scalar1=wtile[:, j, 0:1], scalar2=bias_ap, op0=ALU.mult, op1=ALU.add),
                                    reads=[Rpad, Rcw, Rsw, Rsb], writes=[Racc])
                            for tap in (1, 2):
                                P.op("dve", lambda e, pad=pad, o0=o0, n=n, tap=tap: e.scalar_tensor_tensor(
                                    out=acc[:, o0:o0 + n], in0=pad[:, tap:tap + n], scalar=wtile[:, j, tap:tap + 1], in1=acc[:, o0:o0 + n],
                                    op0=ALU.mult, op1=ALU.add), reads=[Rpad, Racc, Rcw, Rsw], writes=[Racc])

                    cbs2, Rcbs2 = S([128, T], BF16, "cbs2")
                    padsets = [((cxp, Rcxp), (cxc, Rcxc), (cbs, Rcbs)), ((cxp, Rcxp), (cxc, Rcxc), (cbs2, Rcbs2))]

                    def conv_tile(j, wB, wC, wX, t0, n):
                        (cxp_, Rcxp_), (cxc_, Rcxc_), (cbs_, Rcbs_) = padsets[j % 2]
                        pb, Rpb = mm_tile(*wB, t0, n)
                        P.op("act", lambda e: e.copy(out=cbs_[:, t0:t0 + n], in_=pb[:, 0:n]), reads=[Rpb], writes=[Rcbs_])
                        pc, Rpc = mm_tile(*wC, t0, n)
                        P.op("act", lambda e: e.copy(out=csb[:, 0:n], in_=pc[:, 0:n]), reads=[Rpc], writes=[Rcsb])
                        px, Rpx = mm_tile(*wX, t0, n)
                        if t0 < L:
                            dst, Rdst = cxp_[:, 1 + t0:1 + t0 + n], Rcxp_
                        else:
                            dst, Rdst = cxc_[:, 1:1 + n], Rcxc_
                        P.op("dve", lambda e: e.tensor_tensor(out=dst, in0=px[:, 0:n], in1=csb[:, 0:n], op=ALU.mult), reads=[Rpx, Rcsb], writes=[Rdst])

                    def conv_fin(j):
                        (cxp_, Rcxp_), (cxc_, Rcxc_), (cbs_, Rcbs_) = padsets[j % 2]
                        for (pad, Rpad, o0, n) in (((cxp_, Rcxp_, 0, L), (cxc_, Rcxc_, L, CT)) if not last else ((cxp_, Rcxp_, 0, L),)):
                            P.op("dve", lambda e, pad=pad, o0=o0, n=n: e.tensor_scalar(
                                out=acc[:, o0:o0 + n], in0=pad[:, 0:n], scalar1=cw_t[:, j, 0:1], scalar2=None, op0=ALU.mult),
                                reads=[Rpad, Rcw], writes=[Racc])
                            for tap in (1, 2):
                                P.op("dve", lambda e, pad=pad, o0=o0, n=n, tap=tap: e.scalar_tensor_tensor(
                                    out=acc[:, o0:o0 + n], in0=pad[:, tap:tap + n], scalar=cw_t[:, j, tap:tap + 1], in1=acc[:, o0:o0 + n],
                                    op0=ALU.mult, op1=ALU.add), reads=[Rpad, Racc, Rcw], writes=[Racc])
                        P.op("pool", lambda e: e.tensor_tensor(out=ybf[:], in0=acc[:], in1=cbs_[:], op=ALU.mult), reads=[Racc, Rcbs_], writes=[Rybf])
                        P.dma("sp", lambda e: e.dma_start(out=Y_conv[j * 128:(j + 1) * 128, :], in_=ybf[:]), reads=[Rybf])
                    w0 = (load_w(BLK_CB + 0), load_w(BLK_CC + 0), load_w(BLK_CX + 0))
                    TILES_X = TOK_TILES if not last else TOK_TILES[:4]
                    for ti in range(len(TOK_TILES)):
                        norm_tile(ti)
                        if ti >= 1:
                            conv_tile(0, *w0, *TOK_TILES[ti - 1])
                    if not last:
                        conv_tile(0, *w0, *TOK_TILES[-1])
                    conv_fin(0)
                    for j in range(1, 8):
                        wj = (load_w(BLK_CB + j), load_w(BLK_CC + j), load_w(BLK_CX + j))
                        for (t0, n) in TILES_X:
                            conv_tile(j, *wj, t0, n)
                        conv_fin(j)
                    for j in range(12):
                        wX = load_w(BLK_XBC + j)
                        for (t0, n) in TOK_TILES:
                            px, Rpx = mm_tile(*wX, t0, n)
                            dst, Rdst = pad_dst(t0, n)
                            P.op("act", lambda e, px=px, n=n, dst=dst: e.copy(out=dst, in_=px[:, 0:n]), reads=[Rpx], writes=[Rdst])
                        conv3(sw_t, j, sb_t[:, j:j + 1])
                        P.op("act", lambda e: e.activation(out=ybf[:], in_=acc[:], func=AF.Silu), reads=[Racc], writes=[Rybf])
                        P.dma("sp", lambda e, j=j: e.dma_start(out=S_xbc[j * 128:(j + 1) * 128, :], in_=ybf[:]), reads=[Rybf])
                    for j in range(24):
                        wX = load_w(BLK_G + j)
                        for (t0, n) in TILES_X:
                            px, Rpx = mm_tile(*wX, t0, n)
                            P.op("act", lambda e, px=px, t0=t0, n=n: e.activation(out=ybf[:, t0:t0 + n], in_=px[:, 0:n], func=AF.Sigmoid),
                                 reads=[Rpx], writes=[Rybf])
                        P.dma("sp", lambda e, j=j: e.dma_start(out=S_gate[j * 128:(j + 1) * 128, :], in_=ybf[:]), reads=[Rybf])
                    cos_t, Rcos = S([128, L], F32, "cos")
                    sin_t, Rsin = S([128, L], F32, "sin")
                    rot_t, Rrot = S([128, 128], BF16, "rot")
                    rotf, Rrotf = S([128, 128], F32, "rotf")
                    ub, Rub = S([128, 512], BF16, "ub")
                    t1, Rt1 = S([128, 512], F32, "t1")
                    t2, Rt2 = S([128, 512], F32, "t2")
                    P.dma("sp", lambda e: e.dma_start(out=cos_t[:], in_=ccos), writes=[Rcos])
                    P.dma("sp", lambda e: e.dma_start(out=sin_t[:], in_=csin), writes=[Rsin])
                    P.dma("sp", lambda e: e.dma_start(out=rotf[:], in_=crot), writes=[Rrotf])
                    P.op("dve", lambda e: e.tensor_copy(out=rot_t[:], in_=rotf[:]), reads=[Rrotf], writes=[Rrot])
                    ubs = [(ub, Rub)] + [S([128, 512], BF16, "ub2")]
                    t1s = [(t1, Rt1), (t1, Rt1)]
                    t2s = [(t2, Rt2), (t2, Rt2)]
                    ybq = [(ybf, Rybf)] + [S([128, T], BF16, "ybf2")]
                    rix = [0]
                    pend = [None]
                    bix = 0
                    for (o_col, dstS, scl) in ((BLK_Q, S_q, 0.125), (BLK_K, S_k, 1.0)):
                        for j in range(8):
                            wX = load_w(o_col + j)
                            yb_, Ryb_ = ybq[bix % 2]
                            bix += 1
                            for (t0, n) in (TILES_X if o_col == BLK_Q else TOK_TILES):
                                px, Rpx = mm_tile(*wX, t0, n)
                                if t0 >= L:
                                    P.op("act", lambda e, px=px, t0=t0, n=n, scl=scl, yb_=yb_: e.activation(
                                        out=yb_[:, t0:t0 + n], in_=px[:, 0:n], func=AF.Copy, scale=scl), reads=[Rpx], writes=[Ryb_])
                                    continue
                                ub_, Rub_ = ubs[rix[0] % 2]
                                t1_, Rt1_ = t1s[rix[0] % 2]
                                t2_, Rt2_ = t2s[rix[0] % 2]
                                pr, Rpr = PS[4 + (rix[0] % 2)]
                                rix[0] += 1
                                P.op("act", lambda e, px=px, n=n, scl=scl, ub_=ub_: e.activation(out=ub_[:, 0:n], in_=px[:, 0:n], func=AF.Copy, scale=scl),
                                     reads=[Rpx], writes=[Rub_])
                                if pend[0] is not None:
                                    pend[0]()

                                def post(pr=pr, Rpr=Rpr, ub_=ub_, Rub_=Rub_, t1_=t1_, Rt1_=Rt1_, t2_=t2_, Rt2_=Rt2_, t0=t0, n=n, yb_=yb_, Ryb_=Ryb_):
                                    P.op("pe", lambda e: e.matmul(pr[:, 0:n], lhsT=rot_t[:], rhs=ub_[:, 0:n], start=True, stop=True),
                                         reads=[Rrot, Rub_], writes=[Rpr])
                                    P.op("pool", lambda e: e.tensor_tensor(out=t1_[:, 0:n], in0=ub_[:, 0:n], in1=cos_t[:, t0:t0 + n], op=ALU.mult),
                                         reads=[Rub_, Rcos], writes=[Rt1_])
                                    P.op("dve", lambda e: e.tensor_tensor(out=t2_[:, 0:n], in0=pr[:, 0:n], in1=sin_t[:, t0:t0 + n], op=ALU.mult),
                                         reads=[Rpr, Rsin], writes=[Rt2_])
                                    P.op("pool", lambda e: e.tensor_tensor(out=yb_[:, t0:t0 + n], in0=t1_[:, 0:n], in1=t2_[:, 0:n], op=ALU.add),
                                         reads=[Rt1_, Rt2_], writes=[Ryb_])
                                pend[0] = post

                            def fin(j=j, dstS=dstS, yb_=yb_, Ryb_=Ryb_):
                                P.dma("sp", lambda e: e.dma_start(out=dstS[j * 128:(j + 1) * 128, :], in_=yb_[:]), reads=[Ryb_])
                            prev_post = pend[0]

                            def post_and_fin(prev_post=prev_post, fin=fin):
                                prev_post()
                                fin()
                            pend[0] = post_and_fin
                    if pend[0] is not None:
                        pend[0]()
                        pend[0] = None
                    wz, Rwz = S([128, 8, D], BF16, "wz")
                    wv, Rwv = S([128, 8, D], BF16, "wv")
                    wdt, Rwdt = S([128, 8, 32], BF16, "wdt")
                    dtb, Rdtb = S([128, 32], F32, "dtb")
                    P.dma("sp", lambda e: e.dma_start(out=wz[:], in_=Wb_zv[l, :, 0:1024].rearrange("(k p) c -> p k c", p=128)), writes=[Rwz])
                    P.dma("sp", lambda e: e.dma_start(out=wv[:], in_=Wb_zv[l, :, 1024:2048].rearrange("(k p) c -> p k c", p=128)), writes=[Rwv])
                    P.dma("sp", lambda e: e.dma_start(out=wdt[:], in_=Wb_dt[l].rearrange("(k p) c -> p k c", p=128)), writes=[Rwdt])
                    P.dma("sp", lambda e: e.dma_start(out=dtb[:], in_=dtb_d[l:l + 1, :].partition_broadcast(128)), writes=[Rdtb])
                    ztm = [S([128, D], BF16, "ztm") for _ in range(2)]
                    vtm = [S([128, D], BF16, "vtm") for _ in range(2)]
                    dx, Rdx = S([128, 32], F32, "dx")
                    dax, Rdax = S([128, 32], F32, "dax")
                    dout = [S([128, 32], F32, "dout") for _ in range(2)]
                    for tt in range(18):
                        t0 = tt * 128
                        zt, Rzt = ztm[tt % 2]
                        vt, Rvt = vtm[tt % 2]
                        for (wt_, Rw_, dstt, Rd_, fn_) in ((wz, Rwz, zt, Rzt, AF.Silu), (wv, Rwv, vt, Rvt, AF.Copy)):
                            if last and tt >= 16 and fn_ == AF.Silu:
                                continue
                            for hh in range(2):
                                ps, Rps = PS[psi[0] % 4]
                                psi[0] += 1

                                def mm(e, ps=ps, wt_=wt_, hh=hh, t0=t0):
                                    for k in range(8):
                                        ins = e.matmul(ps[:, :], lhsT=hmT[:, k, t0:t0 + 128], rhs=wt_[:, k, hh * 512:(hh + 1) * 512], start=(k == 0), stop=(k == 7))
                                    return ins
                                P.op("pe", mm, reads=[Rw_, RhmT_t[(t0 // 512) * 512]], writes=[Rps])
                                P.op("act", lambda e, ps=ps, dstt=dstt, hh=hh, fn_=fn_: e.activation(out=dstt[:, hh * 512:(hh + 1) * 512], in_=ps[:, :], func=fn_),
                                     reads=[Rps], writes=[Rd_])
                        if not (last and tt >= 16):
                            P.dma("sp", lambda e, zt=zt, t0=t0: e.dma_start(out=S_z[t0:t0 + 128, :], in_=zt[:]), reads=[Rzt])
                        P.dma("sp", lambda e, vt=vt, t0=t0: e.dma_start(out=S_v[t0:t0 + 128, :], in_=vt[:]), reads=[Rvt])
                        ps, Rps = PS[psi[0] % 4]
                        psi[0] += 1

                        def mmd(e, ps=ps, t0=t0):
                            for k in range(8):
                                ins = e.matmul(ps[:, 0:32], lhsT=hmT[:, k, t0:t0 + 128], rhs=wdt[:, k, :], start=(k == 0), stop=(k == 7))
                            return ins
                        P.op("pe", mmd, reads=[Rwdt, RhmT_t[(t0 // 512) * 512]], writes=[Rps])
                        do, Rdo = dout[tt % 2]
                        P.op("dve", lambda e, ps=ps: e.tensor_tensor(out=dx[:], in0=ps[:, 0:32], in1=dtb[:], op=ALU.add), reads=[Rps, Rdtb], writes=[Rdx])
                        P.op("act", lambda e: e.activation(out=dax[:], in_=dx[:], func=AF.Abs), reads=[Rdx], writes=[Rdax])
                        P.op("act", lambda e: e.activation(out=dax[:], in_=dax[:], func=AF.Exp, scale=-1.0), reads=[Rdax], writes=[Rdax])
                        P.op("act", lambda e: e.activation(out=dax[:], in_=dax[:], func=AF.Ln, bias=1.0, scale=1.0), reads=[Rdax], writes=[Rdax])
                        P.op("dve", lambda e, do=do: e.scalar_tensor_tensor(out=do[:], in0=dx[:], scalar=0.0, in1=dax[:], op0=ALU.max, op1=ALU.add),
                             reads=[Rdx, Rdax], writes=[Rdo])
                        P.dma("sp", lambda e, do=do, t0=t0: e.dma_start(out=S_dt[t0:t0 + 128, :], in_=do[:]), reads=[Rdo])
                    P.barrier()
                    P.emit("phA")
                if b == 0 and l == layers[0]:
                    dbg_dump(P, "Y_conv", Y_conv, [D, T], BF16)
                    dbg_dump(P, "S_xbc", S_xbc, [1536, T], BF16)
                    dbg_dump(P, "S_gate", S_gate, [3 * D, T], BF16)
                    dbg_dump(P, "S_q", S_q, [D, T], BF16)
                    dbg_dump(P, "S_k", S_k, [D, T], BF16)
                    dbg_dump(P, "S_v", S_v, [T, D], BF16)
                    dbg_dump(P, "S_z", S_z, [T, D], BF16)
                    dbg_dump(P, "S_dt", S_dt, [T, 32], F32)
                if "stopA" in dbg:
                    continue

                P.barrier()
                with ExitStack() as s2:
                    S = TB(nc, s2)
                    PS, PSB = psum_alloc(s2, 7, 1)
                    psb, Rpsb = PSB[0]
                    alog, Ralog = S([128, 32], F32, "alog")
                    dskt, Rdsk = S([128, 32], F32, "dsk")
                    dsum, Rdsum = S([128, 16], F32, "dsum")
                    snw, Rsnw = S([128, D], F32, "snw")
                    P.dma("sp", lambda e: e.dma_start(out=alog[:], in_=alog_d[l:l + 1, :].partition_broadcast(128)), writes=[Ralog])
                    P.dma("sp", lambda e: e.dma_start(out=dskt[:], in_=dsk_d[l:l + 1, :].partition_broadcast(128)), writes=[Rdsk])
                    P.dma("sp", lambda e: e.dma_start(out=snw[:], in_=snw_d[l:l + 1, :].partition_broadcast(128)), writes=[Rsnw])
                    P.op("act", lambda e: e.activation(out=alog[:], in_=alog[:], func=AF.Exp), reads=[Ralog], writes=[Ralog])
                    P.op("dve", lambda e: e.tensor_scalar(out=alog[:], in0=alog[:], scalar1=-1.0, scalar2=None, op0=ALU.mult), reads=[Ralog], writes=[Ralog])
                    P.op("dve", lambda e: e.tensor_tensor(out=dsum[:], in0=dskt[:, 0:16], in1=dskt[:, 16:32], op=ALU.add), reads=[Rdsk], writes=[Rdsum])
                    hst, Rhst = S([128, D], F32, "hst")
                    hbf, Rhbf = S([128, D], BF16, "hbf")
                    bg_on = (b == 0 and l == layers[0] and bg["i"] < len(deferred))
                    if bg_on:
                        bg_alloc(S)
                    NR = 2

                    def rot(shape, dt, nm, n=NR):
                        return [S(shape, dt, nm) for _ in range(n)]
                    xbcT = rot([128, 12, 128], BF16, "xbcT", 5)
                    dtt = rot([128, 32], F32, "dtt", 5)
                    ztm_ = rot([128, D], BF16, "zt", 6)
                    yfl = rot([128, D], F32, "yfl", 3)
                    xs_tms = rot([128, D], BF16, "xs_tm", 3)
                    B_tms = rot([128, 256], BF16, "B_tm", 3)
                    a_ts = rot([128, 16], F32, "a", 3)
                    cs_ts = rot([128, 16], F32, "cs", 3)
                    ncs_ts = rot([128, 16], F32, "ncs", 3)
                    dout_ts = rot([128, 16], F32, "dout", 3)
                    dst_ts = rot([128, 16], F32, "dst", 3)
                    cdec_ts = rot([128, 16], F32, "cdec", 3)
                    Xds = rot([128, D], BF16, "Xd", 3)
                    Xss = rot([128, D], BF16, "Xs", 3)
                    stsbs = rot([128, D], F32, "stsb", 3)
                    segrs = rot([128, 16, 128], F32, "segr")
                    Lms = rot([128, 16, 128], BF16, "Lm", 2)
                    scTs = rot([128, 256], BF16, "scT", 3)
                    MTs = rot([128, 16, 128], BF16, "MT")
                    yaccs = rot([128, D], F32, "yacc", 3)
                    ytmps = rot([128, D], F32, "ytmp", 1)
                    y3s = rot([128, D], BF16, "y3", 2)
                    pres = rot([128, D], F32, "pre", 4)
                    gsums = rot([128, 2], F32, "gsum", 3)
                    yTs_ = rot([128, 8, 128], BF16, "yT")
                    RYF = [Res("YF%d" % c) for c in range(18)]
                    passes = []
                    for d_ in range(2):
                        order = [16, 17] + list(range(16)) if d_ == 0 else [17, 16] + list(range(15, -1, -1))
                        for oi, c in enumerate(order):
                            passes.append((d_, c, oi == 0))

                    def mk_pass(pi):
                        d_, c, first = passes[pi]
                        tok0 = c * 128
                        tri_d, Rtri_d = (triu, Rtriu) if d_ == 0 else (tril, Rtril)
                        neg_d, Rneg_d = (negf, Rnegf) if d_ == 0 else (negb, Rnegb)
                        xb, Rxb = xbcT[pi % 5]
                        dt_, Rdt_ = dtt[pi % 5]
                        zt, Rzt = ztm_[pi % 6]
                        yf, Ryf = yfl[pi % 3]
                        xs_tm, Rxs = xs_tms[pi % 3]
                        B_tm, RBtm = B_tms[pi % 3]
                        a_t, Ra = a_ts[pi % 3]
                        cs_t, Rcs = cs_ts[pi % 3]
                        ncs_t, Rncs = ncs_ts[pi % 3]
                        dout_t, Rdout = dout_ts[pi % 3]
                        dst_t, Rdst_ = dst_ts[pi % 3]
                        cdec_t, Rcdec = cdec_ts[pi % 3]
                        Xd, RXd = Xds[pi % 3]
                        Xs, RXs = Xss[pi % 3]
                        stsb, Rstsb = stsbs[pi % 3]
                        segr, Rsegr = segrs[pi % NR]
                        Lm, RLm = Lms[pi % 2]
                        scT, RscT = scTs[pi % 3]
                        MT, RMT = MTs[pi % NR]
                        yacc, Ryacc = yaccs[pi % 3]
                        ytmp, Rytmp = ytmps[0]
                        y3, Ry3 = y3s[pi % 2]
                        pre, Rpre = pres[pi % 4]
                        gsum, Rgsum = gsums[pi % 3]
                        yT, RyT = yTs_[pi % NR]
                        p4, Rp4 = PS[4]

                        def ld():
                            P.dma("sp", lambda e: e.dma_start(out=xb[:], in_=S_xbc[:, tok0:tok0 + 128].rearrange("(k p) t -> p k t", p=128)), writes=[Rxb])
                            P.dma("sp", lambda e: e.dma_start(out=dt_[:], in_=S_dt[tok0:tok0 + 128, :]), writes=[Rdt_])
                            if d_ == 1:
                                P.dma("sp", lambda e: e.dma_start(out=zt[:], in_=S_z[tok0:tok0 + 128, :]), writes=[Rzt])

                        def noop():
                            pass

                        def early():
                            if d_ == 1:
                                P.dma("sp", lambda e: e.dma_start(out=yf[:], in_=YF[c]), reads=[RYF[c]], writes=[Ryf])

                            def trx(e):
                                for k in range(8):
                                    ins = e.transpose(psb[:, k * 128:(k + 1) * 128], xb[:, k, :], ident[:])
                                return ins
                            P.op("pe", trx, reads=[Rxb, Rident], writes=[Rpsb])
                            P.op("act", lambda e: e.copy(out=xs_tm[:], in_=psb[:, :]), reads=[Rpsb], writes=[Rxs])
                            P.op("dve", lambda e: e.tensor_tensor(out=a_t[:], in0=dt_[:, d_ * 16:(d_ + 1) * 16], in1=alog[:, d_ * 16:(d_ + 1) * 16], op=ALU.mult),
                                 reads=[Rdt_, Ralog], writes=[Ra])

                            def mcs(e):
                                e.matmul(p4[:, 0:16], lhsT=tri_d[:], rhs=a_t[:], start=True, stop=True)
                                e.matmul(p4[:, 16:32], lhsT=onesf[:], rhs=a_t[:], start=True, stop=True)
                                for g in range(2):
                                    ins = e.matmul(p4[:, 128 + g * 128:256 + g * 128], lhsT=xb[:, 8 + g, :], rhs=xb[:, 10 + g, :], start=True, stop=True)
                                return ins
                            P.op("pe", mcs, reads=[Rtri_d, Ronesf, Ra, Rxb], writes=[Rp4])
                            P.op("dve", lambda e: e.tensor_copy(out=cs_t[:], in_=p4[:, 0:16]), reads=[Rp4], writes=[Rcs, Rp4])
                            P.op("dve", lambda e: e.tensor_tensor(out=dst_t[:], in0=p4[:, 16:32], in1=cs_t[:], op=ALU.subtract), reads=[Rp4, Rcs], writes=[Rdst_, Rp4])
                            P.op("act", lambda e: e.activation(out=cdec_t[:], in_=p4[:, 16:32], func=AF.Exp), reads=[Rp4], writes=[Rcdec, Rp4])
                            P.op("act", lambda e: e.copy(out=scT[:], in_=p4[:, 128:384]), reads=[Rp4], writes=[RscT, Rp4])
                            P.op("dve", lambda e: e.tensor_scalar(out=ncs_t[:], in0=cs_t[:], scalar1=-1.0, scalar2=None, op0=ALU.mult), reads=[Rcs], writes=[Rncs])
                            P.op("act", lambda e: e.activation(out=dout_t[:], in_=cs_t[:], func=AF.Exp), reads=[Rcs], writes=[Rdout])
                            P.op("act", lambda e: e.activation(out=dst_t[:], in_=dst_t[:], func=AF.Exp), reads=[Rdst_], writes=[Rdst_])

                            def trb(e):
                                for g in range(2):
                                    ins = e.transpose(psb[:, g * 128:(g + 1) * 128], xb[:, 8 + g, :], ident[:])
                                return ins
                            P.op("pe", trb, reads=[Rxb, Rident], writes=[Rpsb])
                            P.op("dve", lambda e: e.tensor_copy(out=B_tm[:], in_=psb[:, 0:256]), reads=[Rpsb], writes=[RBtm])
                            P.op("dve", lambda e: e.tensor_tensor(
                                out=Xd[:].rearrange("p (h q) -> p h q", q=64), in0=xs_tm[:].rearrange("p (h q) -> p h q", q=64),
                                in1=dt_[:, d_ * 16:(d_ + 1) * 16].unsqueeze(2).to_broadcast([128, 16, 64]), op=ALU.mult), reads=[Rxs, Rdt_], writes=[RXd])
                            P.op("dve", lambda e: e.tensor_tensor(
                                out=Xs[:].rearrange("p (h q) -> p h q", q=64), in0=Xd[:].rearrange("p (h q) -> p h q", q=64),
                                in1=dst_t[:].unsqueeze(2).to_broadcast([128, 16, 64]), op=ALU.mult), reads=[RXd, Rdst_], writes=[RXs])
                            if d_ == 1:
                                P.op("pool", lambda e: e.tensor_tensor(
                                    out=pre[:].rearrange("p (h q) -> p h q", q=64), in0=xs_tm[:].rearrange("p (h q) -> p h q", q=64),
                                    in1=dsum[:].unsqueeze(2).to_broadcast([128, 16, 64]), op=ALU.mult), reads=[Rxs, Rdsum], writes=[Rpre])
                                P.op("pool", lambda e: e.tensor_tensor(out=pre[:], in0=pre[:], in1=yf[:], op=ALU.add), reads=[Rpre, Ryf], writes=[Rpre])
                            P.op("pool", lambda e: e.tensor_tensor(
                                out=segr[:], in0=tri_d[:].unsqueeze(1).to_broadcast([128, 16, 128]), in1=a_t[:].unsqueeze(2).to_broadcast([128, 16, 128]), op=ALU.mult),
                                reads=[Rtri_d, Ra], writes=[Rsegr])
                            for g in range(2):
                                pst_, Rpst = PS[5 + g]
                                P.op("pe", lambda e, pst_=pst_, g=g: e.matmul(pst_[:, :], lhsT=B_tm[:, g * 128:(g + 1) * 128], rhs=Xs[:, g * 512:(g + 1) * 512], start=True, stop=True),
                                     reads=[RBtm, RXs], writes=[Rpst])
                                P.op("act", lambda e, pst_=pst_, g=g: e.copy(out=stsb[:, g * 512:(g + 1) * 512], in_=pst_[:, :]), reads=[Rpst], writes=[Rstsb, Rpst])

                        def mid():
                            for q4 in range(4):
                                pq, Rpq = PS[q4 % 2]

                                def mseg(e, pq=pq, q4=q4):
                                    e.matmul(pq[:, :], lhsT=onesf[:], rhs=segr[:, q4 * 4:(q4 + 1) * 4, :].rearrange("p h l -> p (h l)"), start=True, stop=False)
                                    return e.matmul(pq[:, :], lhsT=identf[:], rhs=neg_d[:], start=False, stop=True)
                                P.op("pe", mseg, reads=[Rsegr, Ronesf, Ridentf, Rneg_d], writes=[Rpq])

                                def lexp(e, pq=pq, q4=q4):
                                    for hh in range(4):
                                        h_ = q4 * 4 + hh
                                        ins = e.activation(out=Lm[:, h_, :], in_=pq[:, hh * 128:(hh + 1) * 128], func=AF.Exp, bias=ncs_t[:, h_:h_ + 1], scale=1.0)
                                    return ins
                                P.op("act", lexp, reads=[Rpq, Rncs], writes=[RLm])
                            for g in range(2):
                                P.op("dve", lambda e, g=g: e.tensor_tensor(
                                    out=MT[:, g * 8:(g + 1) * 8, :], in0=Lm[:, g * 8:(g + 1) * 8, :],
                                    in1=scT[:, g * 128:(g + 1) * 128].unsqueeze(1).to_broadcast([128, 8, 128]), op=ALU.mult), reads=[RLm, RscT], writes=[RMT])
                            for g in range(2):
                                pyd, Rpyd = PS[2 + g]

                                def myd(e, pyd=pyd, g=g):
                                    for hh in range(8):
                                        h_ = g * 8 + hh
                                        ins = e.matmul(pyd[:, hh * 64:(hh + 1) * 64], lhsT=MT[:, h_, :], rhs=Xd[:, h_ * 64:(h_ + 1) * 64], start=True, stop=True)
                                    return ins
                                P.op("pe", myd, reads=[RMT, RXd], writes=[Rpyd])

                        def late():
                            if first:
                                P.op("pool", lambda e: e.memset(hst[:], 0.0), writes=[Rhst])
                                P.op("pool", lambda e: e.memset(hbf[:], 0.0), writes=[Rhbf])
                            for g in range(2):
                                pyo, Rpyo = PS[5 + g]
                                P.op("pe", lambda e, pyo=pyo, g=g: e.matmul(pyo[:, :], lhsT=xb[:, 10 + g, :], rhs=hbf[:, g * 512:(g + 1) * 512], start=True, stop=True),
                                     reads=[Rxb, Rhbf], writes=[Rpyo])
                            for g in range(2):
                                pyo, Rpyo = PS[5 + g]
                                P.op("dve", lambda e, pyo=pyo, g=g: e.tensor_tensor(
                                    out=yacc[:, g * 512:(g + 1) * 512].rearrange("p (h q) -> p h q", q=64), in0=pyo[:, :].rearrange("p (h q) -> p h q", q=64),
                                    in1=dout_t[:, g * 8:(g + 1) * 8].unsqueeze(2).to_broadcast([128, 8, 64]), op=ALU.mult), reads=[Rpyo, Rdout], writes=[Ryacc, Rpyo])
                                P.op("pool", lambda e, g=g: e.tensor_tensor(
                                    out=hst[:, g * 512:(g + 1) * 512].rearrange("p (h q) -> p h q", q=64), in0=hst[:, g * 512:(g + 1) * 512].rearrange("p (h q) -> p h q", q=64),
                                    in1=cdec_t[:, g * 8:(g + 1) * 8].unsqueeze(2).to_broadcast([128, 8, 64]), op=ALU.mult), reads=[Rhst, Rcdec], writes=[Rhst])
                                P.op("pool", lambda e, g=g: e.tensor_tensor(out=hst[:, g * 512:(g + 1) * 512], in0=stsb[:, g * 512:(g + 1) * 512], in1=hst[:, g * 512:(g + 1) * 512], op=ALU.add),
                                     reads=[Rstsb, Rhst], writes=[Rhst])
                            P.op("act", lambda e: e.copy(out=hbf[:], in_=hst[:]), reads=[Rhst], writes=[Rhbf])
                            for g in range(2):
                                pyd, Rpyd = PS[2 + g]
                                P.op("dve", lambda e, pyd=pyd, g=g: e.tensor_tensor(out=yacc[:, g * 512:(g + 1) * 512], in0=pyd[:, :], in1=yacc[:, g * 512:(g + 1) * 512], op=ALU.add),
                                     reads=[Rpyd, Ryacc], writes=[Ryacc, Rpyd])
                            if d_ == 0:
                                P.dma("sp", lambda e: e.dma_start(out=YF[c], in_=yacc[:]), reads=[Ryacc], writes=[RYF[c]])
                            else:
                                pass

                        def late2():
                            if d_ == 0:
                                return
                            if True:
                                P.op("pool", lambda e: e.tensor_tensor(out=yacc[:], in0=yacc[:], in1=pre[:], op=ALU.add), reads=[Ryacc, Rpre], writes=[Ryacc])
                                P.op("dve", lambda e: e.tensor_tensor(out=yacc[:], in0=yacc[:], in1=zt[:], op=ALU.mult), reads=[Ryacc, Rzt], writes=[Ryacc])
                                for g in range(2):
                                    P.op("act", lambda e, g=g: e.activation(out=ytmp[:, g * 512:(g + 1) * 512], in_=yacc[:, g * 512:(g + 1) * 512], func=AF.Square,
                                                                            accum_out=gsum[:, g:g + 1]), reads=[Ryacc], writes=[Rytmp, Rgsum])

                        def late3():
                            if d_ == 0:
                                return
                            if True:
                                P.op("dve", lambda e: e.tensor_scalar(out=gsum[:], in0=gsum[:], scalar1=1.0 / 512, scalar2=EPS, op0=ALU.mult, op1=ALU.add), reads=[Rgsum], writes=[Rgsum])
                                P.op("act", lambda e: e.activation(out=gsum[:], in_=gsum[:], func=AF.Ln), reads=[Rgsum], writes=[Rgsum])
                                P.op("act", lambda e: e.activation(out=gsum[:], in_=gsum[:], func=AF.Exp, scale=-0.5), reads=[Rgsum], writes=[Rgsum])
                                for g in range(2):
                                    P.op("dve", lambda e, g=g: e.scalar_tensor_tensor(
                                        out=y3[:, g * 512:(g + 1) * 512], in0=yacc[:, g * 512:(g + 1) * 512], scalar=gsum[:, g:g + 1], in1=snw[:, g * 512:(g + 1) * 512],
                                        op0=ALU.mult, op1=ALU.mult), reads=[Ryacc, Rgsum, Rsnw], writes=[Ry3])

                        def fin():
                            if d_ == 0:
                                return

                            def try_(e):
                                for k in range(8):
                                    ins = e.transpose(psb[:, k * 128:(k + 1) * 128], y3[:, k * 128:(k + 1) * 128], ident[:])
                                return ins
                            P.op("pe", try_, reads=[Ry3, Rident], writes=[Rpsb])
                            P.op("act", lambda e: e.copy(out=yT[:].rearrange("p k t -> p (k t)"), in_=psb[:, :]), reads=[Rpsb], writes=[RyT])
                            P.dma("sp", lambda e: e.dma_start(out=Y_ssd[:, tok0:tok0 + 128].rearrange("(k p) t -> p k t", p=128), in_=yT[:]), reads=[RyT])
                        return [ld, noop, early, mid, late, late2, late3, fin]
                    NSTB = 8
                    liveb = {}
                    npass = len(passes)
                    for t_ in range(npass + NSTB - 1):
                        if t_ < npass:
                            liveb[t_] = mk_pass(t_)
                        for k in range(NSTB - 1, -1, -1):
                            u = t_ - k
                            if 0 <= u < npass:
                                liveb[u][k]()
                        liveb.pop(t_ - (NSTB - 1), None)
                        if bg_on:
                            bg_cvt(2)
                    bg["bufs"] = None
                    P.barrier()
                    P.emit("phB")
                if b == 0 and l == layers[0]:
                    dbg_dump(P, "Y_ssd", Y_ssd, [D, T], BF16)
                if "stopB" in dbg:
                    continue
                P.barrier()
                with ExitStack() as s3:
                    S = TB(nc, s3)

                    def pst(shape, dt, nm):
                        TB.gid += 1
                        return (s3.enter_context(nc.psum_tensor("%s_%d" % (nm, TB.gid), shape, dt)), Res(nm))
                    NPX = 2
                    pxs = [pst([128, 512], F32, "px") for _ in range(NPX)]
                    pys = [pst([128, 512], F32, "py") for _ in range(NPX)]
                    ptrs = [pst([128, 1024], BF16, "ptr") for _ in range(2)]
                    pos_ = [pst([128, 512], F32, "po") for _ in range(2)]
                    Vev, RVev = S([128, 18, D], BF16, "Vev")
                    ynat, Rynat = S([128, 18, D], BF16, "ynat")
                    qTs = [S([64, T], BF16, "qT") for _ in range(2)]
                    kTs = [S([64, T], BF16, "kT") for _ in range(2)]
                    TTs = [S([128, 15, 64], F32, "TTh") for _ in range(2)]
                    tbls = [S([128, 5, 832], F32, "tbl") for _ in range(2)]
                    NBUF = 4
                    sls = [S([128, 832], F32, "sl") for _ in range(NBUF)]
                    pbs = [S([128, 832], BF16, "pb") for _ in range(NBUF)]
                    pTs = [S([128, 7, 128], BF16, "pT") for _ in range(NBUF)]
                    sms = [S([128, 4], F32, "sm") for _ in range(NBUF)]
                    units = []
                    for h_ in range(16):
                        for r in range(0, 32, 2):
                            units.append((h_, "lat", r))
                        if not last:
                            units += [(h_, "ctx", 0), (h_, "ctx", 1)]
                    import os as _os
                    if _os.environ.get('NA_UNITS'):
                        units = units[:int(_os.environ['NA_UNITS'])]
                    NSTG = int(_os.environ.get('NA_STAGES', '7'))
                    nun = len(units)
                    CASE = {0: 1, 2: 2, 28: 3, 30: 4}

                    def head_load(h_):
                        qT, RqT = qTs[h_ % 2]
                        kT, RkT = kTs[h_ % 2]
                        TTh, RTTh = TTs[h_ % 2]
                        tbl, Rtbl = tbls[h_ % 2]
                        hc0 = h_ * 64
                        P.dma("sp", lambda e: e.dma_start(out=qT[:], in_=S_q[hc0:hc0 + 64, :]), writes=[RqT])
                        P.dma("sp", lambda e: e.dma_start(out=kT[:], in_=S_k[hc0:hc0 + 64, :]), writes=[RkT])
                        P.dma("sp", lambda e: e.dma_start(out=TTh[0:64], in_=TT_d[l, :, h_, :, :]), writes=[RTTh])
                        P.dma("sp", lambda e: e.dma_start(out=TTh[64:128], in_=TT_d[l, :, h_, :, :]), writes=[RTTh])
                        P.op("pool", lambda e: e.memset(tbl[:, :, 0:256], 0.0), writes=[Rtbl])
                        P.op("pool", lambda e: e.memset(tbl[:, :, 256:832], NEG), writes=[Rtbl])
                        for rr, cs_ in ((4, 0), (0, 1), (2, 2), (28, 3), (30, 4)):
                            Rb = min(max(rr - 4, 0), 24)
                            for hf in range(2):
                                row = rr + hf
                                r0row = min(max(row - 4, 0), 24)
                                kr0 = r0row - Rb
                                drs = r0row - row + 7
                                P.op("pool", lambda e, hf=hf, cs_=cs_, kr0=kr0, drs=drs: e.tensor_copy(
                                    out=tbl[hf * 64:(hf + 1) * 64, cs_, 256 + kr0 * 64:256 + kr0 * 64 + 512],
                                    in_=TTh[hf * 64:(hf + 1) * 64, drs:drs + 8, :].rearrange("p a b -> p (a b)")), reads=[RTTh], writes=[Rtbl])

                    def mk_unit(ui):
                        h_, kind, r = units[ui]
                        qT, RqT = qTs[h_ % 2]
                        kT, RkT = kTs[h_ % 2]
                        tbl, Rtbl = tbls[h_ % 2]
                        hc0 = h_ * 64
                        sl, Rsl = sls[ui % NBUF]
                        pb_, Rpb_ = pbs[ui % NBUF]
                        pT, RpT = pTs[ui % NBUF]
                        sm, Rsm = sms[ui % NBUF]
                        px, Rpx = pxs[ui % NPX]
                        py, Rpy = pys[ui % NPX]
                        ptr, Rptr = ptrs[ui % 2]
                        pot, Rpo = pos_[ui % 2]
                        po = pot[:, 0:64]
                        if kind == "lat":
                            Rb = min(max(r - 4, 0), 24)
                            cs_ = CASE.get(r, 0)
                            q0, nk, tix = r * 64, 832, r // 2
                            full9 = (Rb + 8 <= 31)
                            vl = [Vev[:, 16, hc0:hc0 + 64], Vev[:, 17, hc0:hc0 + 64]]
                            for j in range(4):
                                vl.append(Vev[:, Rb // 2 + j, hc0:hc0 + 64])
                            vl.append(Vev[0:64, (Rb + 8) // 2 if full9 else 0, hc0:hc0 + 64])
                        else:
                            q0, nk, tix = L + r * 128, 256, 16 + r
                            vl = [Vev[:, 16, hc0:hc0 + 64], Vev[:, 17, hc0:hc0 + 64]]
                        nch = (nk + 127) // 128

                        def st0():
                            if kind == "lat" and r == 0:
                                if h_ == 0:
                                    head_load(0)
                                    P.dma("sp", lambda e: e.dma_start(out=Vev[:], in_=S_v[:, :].rearrange("(i p) d -> p i d", p=128)), writes=[RVev])
                                if h_ + 1 < 16:
                                    head_load(h_ + 1)

                            def msc(e):
                                ins = e.matmul(px[:, 0:256], lhsT=qT[:, q0:q0 + 128], rhs=kT[:, L:T], start=True, stop=True)
                                if kind == "lat":
                                    ins = e.matmul(px[:, 256:512], lhsT=qT[:, q0:q0 + 128], rhs=kT[:, Rb * 64:Rb * 64 + 256], start=True, stop=True)
                                return ins
                            P.op("pe", msc, reads=[RqT, RkT], writes=[Rpx])
                            if kind == "lat":
                                def msc2(e):
                                    if full9:
                                        return e.matmul(py[:, 0:320], lhsT=qT[:, q0:q0 + 128], rhs=kT[:, Rb * 64 + 256:Rb * 64 + 576], start=True, stop=True)
                                    e.matmul(py[:, 0:256], lhsT=qT[:, q0:q0 + 128], rhs=kT[:, Rb * 64 + 256:Rb * 64 + 512], start=True, stop=True)
                                    return e.matmul(py[:, 256:320], lhsT=qT[:, q0:q0 + 128], rhs=kT[:, 0:64], start=True, stop=True)
                                P.op("pe", msc2, reads=[RqT, RkT], writes=[Rpy])

                        def st1():
                            if kind == "lat":
                                P.op("dve", lambda e: e.tensor_tensor(out=sl[:, 0:512], in0=px[:, 0:512], in1=tbl[:, cs_, 0:512], op=ALU.add),
                                     reads=[Rpx, Rtbl], writes=[Rsl])
                                P.op("dve", lambda e: e.tensor_tensor(out=sl[:, 512:832], in0=py[:, 0:320], in1=tbl[:, cs_, 512:832], op=ALU.add),
                                     reads=[Rpy, Rtbl], writes=[Rsl])
                            else:
                                P.op("dve", lambda e: e.tensor_copy(out=sl[:, 0:256], in_=px[:, 0:256]), reads=[Rpx], writes=[Rsl])
                            P.op("dve", lambda e: e.reduce_max(out=sm[:, 0:1], in_=sl[:, 0:nk], axis=AX.X), reads=[Rsl], writes=[Rsm])
                            P.op("dve", lambda e: e.tensor_scalar(out=sm[:, 1:2], in0=sm[:, 0:1], scalar1=-1.0, scalar2=None, op0=ALU.mult), reads=[Rsm], writes=[Rsm])

                        def st2():
                            P.op("act", lambda e: e.activation(
                                out=pb_[:, 0:nk], in_=sl[:, 0:nk], func=AF.Exp, bias=sm[:, 1:2], scale=1.0, accum_out=sm[:, 2:3]), reads=[Rsl, Rsm], writes=[Rpb_, Rsm])

                        def st3():
                            def trp(e):
                                for j in range(nch):
                                    w_ = min(128, nk - j * 128)
                                    ins = e.transpose(ptr[0:w_, j * 128:(j + 1) * 128], pb_[:, j * 128:j * 128 + w_], ident[:])
                                return ins
                            P.op("pe", trp, reads=[Rpb_, Rident], writes=[Rptr])

                        def st4():
                            def cp(e):
                                nfull = nk // 128
                                ins = e.copy(out=pT[:, 0:nfull, :], in_=ptr[:, 0:nfull * 128].rearrange("p (j q) -> p j q", q=128))
                                if nk % 128:
                                    ins = e.copy(out=pT[0:64, nfull, :], in_=ptr[0:64, nfull * 128:(nfull + 1) * 128])
                                return ins
                            P.op("act", cp, reads=[Rptr], writes=[RpT])

                        def st5():
                            def mpv(e):
                                for j, v in enumerate(vl):
                                    kk = 64 if (kind == "lat" and j == 6) else 128
                                    ins = e.matmul(po, lhsT=pT[0:kk, j, :], rhs=v, start=(j == 0), stop=(j == len(vl) - 1))
                                return ins
                            P.op("pe", mpv, reads=[RpT, RVev], writes=[Rpo])

                        def st6():
                            P.op("dve", lambda e: e.reciprocal(out=sm[:, 3:4], in_=sm[:, 2:3]), reads=[Rsm], writes=[Rsm])
                            P.op("dve", lambda e: e.tensor_scalar(out=ynat[:, tix, hc0:hc0 + 64], in0=po, scalar1=sm[:, 3:4], scalar2=None, op0=ALU.mult),
                                 reads=[Rpo, Rsm], writes=[Rynat])
                        return [st0, st1, st2, st3, st4, st5, st6]
                    NST = 7
                    live = {}
                    bg_on_c = (b == 0 and l == layers[0] and bg["i"] < len(deferred))
                    if bg_on_c:
                        bg_alloc(S)
                    for t_ in range(nun + NST - 1):
                        if t_ < nun:
                            live[t_] = mk_unit(t_)
                        for k in range(NST - 1, -1, -1):
                            u = t_ - k
                            if 0 <= u < nun and k < NSTG:
                                live[u][k]()
                        live.pop(t_ - (NST - 1), None)
                        if bg_on_c and t_ % 3 == 0:
                            bg_cvt(1)
                    if bg_on_c:
                        bg_cvt(len(deferred))
                        bg["bufs"] = None
                    ntile = 18 if not last else 16
                    yTb = [S([128, 8, 128], BF16, "yTb") for _ in range(2)]
                    for ti in range(ntile):
                        ptr, Rptr = ptrs[ti % 2]
                        yT_, RyT_ = yTb[ti % 2]

                        def try_(e, ptr=ptr, ti=ti):
                            for k in range(8):
                                ins = e.transpose(ptr[:, k * 128:(k + 1) * 128], ynat[:, ti, k * 128:(k + 1) * 128], ident[:])
                            return ins
                        P.op("pe", try_, reads=[Rynat, Rident], writes=[Rptr])
                        P.op("act", lambda e, ptr=ptr, yT_=yT_: e.copy(out=yT_[:].rearrange("p k t -> p (k t)"), in_=ptr[:, :]), reads=[Rptr], writes=[RyT_])
                        P.dma("sp", lambda e, yT_=yT_, ti=ti: e.dma_start(out=Y_na[:, ti * 128:(ti + 1) * 128].rearrange("(k p) t -> p k t", p=128), in_=yT_[:]), reads=[RyT_])
                    P.barrier()
                    P.emit("phC")
                if b == 0 and l == layers[0]:
                    dbg_dump(P, "Y_na", Y_na, [D, T], BF16)
                if "stopC" in dbg:
                    continue
                tiles_d = TOK_TILES if not last else TOK_TILES[:4]
                P.barrier()
                with ExitStack() as s4:
                    S = TB(nc, s4)
                    PS, PSB = psum_alloc(s4, 4, 0)
                    wbr = []
                    for wi in range(4):
                        wt, Rwt = S([128, 8, D], BF16, "wbr")
                        wbr.append((wt, Rwt))

                    def emit_d1_weights():
                        for wi in range(4):
                            wt, Rwt = wbr[wi]
                            P.dma("sp", lambda e, wt=wt, wi=wi: e.dma_start(out=wt[:], in_=Wb_br[l, wi].rearrange("(k p) c -> p k c", p=128)), writes=[Rwt])
                    NT1 = 256
                    NBF1 = 2
                    yss = [[S([128, 8, NT1], BF16, "ys") for _ in range(3)] for _ in range(NBF1)]
                    gts = [S([128, 24, NT1], BF16, "gt") for _ in range(NBF1)]
                    hhs1 = [S([128, 8, NT1], F32, "hh") for _ in range(NBF1 + 1)]
                    mg, Rmg = S([128, 8, NT1], F32, "mg")
                    mb, Rmb = S([128, 8, NT1], BF16, "mb")
                    tmps = [S([128, NT1], F32, "tmpd") for _ in range(2)]
                    ntok1 = T if not last else L
                    tl1 = list(range(0, ntok1, NT1))

                    def d1_load(ti1):
                        t0 = tl1[ti1]
                        n = NT1
                        for bi, Ysrc in enumerate((Y_conv, Y_ssd, Y_na)):
                            yt_, Ryt_ = yss[ti1 % NBF1][bi]
                            P.dma("sp", lambda e, yt_=yt_, Ysrc=Ysrc: e.dma_start(out=yt_[:, :, 0:n], in_=fm(Ysrc, t0, n)), writes=[Ryt_])
                        gt_, Rgt_ = gts[ti1 % NBF1]
                        hh_, Rhh_ = hhs1[ti1 % (NBF1 + 1)]
                        P.dma("sp", lambda e: e.dma_start(out=gt_[:, :, 0:n], in_=fm(S_gate, t0, n)), writes=[Rgt_])
                        P.dma("sp", lambda e: e.dma_start(out=hh_[:, :, 0:n], in_=fm(src_h, t0, n)), writes=[Rhh_])
                    pi = 0
                    d1_load(0)
                    emit_d1_weights()
                    for ti1, t0 in enumerate(tl1):
                        n = NT1
                        if ti1 + 1 < len(tl1):
                            d1_load(ti1 + 1)
                        col = b if t0 < L else 4
                        gt_, Rgt_ = gts[ti1 % NBF1]
                        hh_, Rhh_ = hhs1[ti1 % (NBF1 + 1)]
                        for bi in range(3):
                            yt_, Ryt_ = yss[ti1 % NBF1][bi]
                            wt, Rwt = wbr[bi]
                            for ob in range(8):
                                ps, Rps = PS[pi % 4]
                                tmpd, Rtmpd = tmps[pi % 2]
                                pi += 1

                                def mm(e, ps=ps, wt=wt, yt_=yt_, ob=ob, n=n):
                                    for k in range(8):
                                        ins = e.matmul(ps[:, 0:n], lhsT=wt[:, k, ob * 128:(ob + 1) * 128], rhs=yt_[:, k, 0:n], start=(k == 0), stop=(k == 7))
                                    return ins
                                P.op("pe", mm, reads=[Rwt, Ryt_], writes=[Rps])
                                if bi == 0:
                                    P.op("dve", lambda e, ps=ps, ob=ob, n=n, gt_=gt_: e.tensor_tensor(out=mg[:, ob, 0:n], in0=ps[:, 0:n], in1=gt_[:, ob, 0:n], op=ALU.mult),
                                         reads=[Rps, Rgt_], writes=[Rmg])
                                else:
                                    P.op("dve", lambda e, ps=ps, ob=ob, n=n, bi=bi, tmpd=tmpd, gt_=gt_: e.tensor_tensor(out=tmpd[:, 0:n], in0=ps[:, 0:n], in1=gt_[:, bi * 8 + ob, 0:n], op=ALU.mult),
                                         reads=[Rps, Rgt_], writes=[Rtmpd])
                                    P.op("pool", lambda e, ob=ob, n=n, tmpd=tmpd: e.tensor_tensor(out=mg[:, ob, 0:n], in0=mg[:, ob, 0:n], in1=tmpd[:, 0:n], op=ALU.add),
                                         reads=[Rmg, Rtmpd], writes=[Rmg])
                        P.op("act", lambda e, n=n: e.copy(out=mb[:, :, 0:n], in_=mg[:, :, 0:n]), reads=[Rmg], writes=[Rmb])
                        wt, Rwt = wbr[3]
                        for ob in range(8):
                            ps, Rps = PS[pi % 4]
                            pi += 1

                            def mm(e, ps=ps, wt=wt, ob=ob, n=n):
                                for k in range(8):
                                    ins = e.matmul(ps[:, 0:n], lhsT=wt[:, k, ob * 128:(ob + 1) * 128], rhs=mb[:, k, 0:n], start=(k == 0), stop=(k == 7))
                                return ins
                            P.op("pe", mm, reads=[Rwt, Rmb], writes=[Rps])
                            P.op("dve", lambda e, ps=ps, ob=ob, n=n, col=col, hh_=hh_: e.scalar_tensor_tensor(
                                out=hh_[:, ob, 0:n], in0=ps[:, 0:n], scalar=mcol(l, 2, ob, col), in1=hh_[:, ob, 0:n], op0=ALU.mult, op1=ALU.add),
                                reads=[Rps, Rhh_, Rmod], writes=[Rhh_])
                        P.dma("sp", lambda e, t0=t0, n=n, hh_=hh_: e.dma_start(out=fm(HT, t0, n), in_=hh_[:, :, 0:n]), reads=[Rhh_])
                    P.barrier()
                    P.emit("phD1")
                if b == 0 and l == layers[0]:
                    dbg_dump(P, "HT1", HT, [D, T], F32)
                if "stopD1" in dbg:
                    continue
                P.barrier()
                with ExitStack() as s5:
                    S = TB(nc, s5)
                    PS, PSB = psum_alloc(s5, 8, 0)
                    W1g = [S([128, 8, 512], BF16, "W1g") for _ in range(8)]
                    W2g = [S([128, 8, D], BF16, "W2g") for _ in range(4)]
                    def emit_ffn_weights():
                        for g8 in range(8):
                            w1, Rw1 = W1g[g8]
                            P.dma("sp", lambda e, w1=w1, g8=g8: e.dma_start(out=w1[:], in_=Wb_f1[l, :, g8 * 512:(g8 + 1) * 512].rearrange("(k p) c -> p k c", p=128)), writes=[Rw1])
                        for q4 in range(4):
                            w2, Rw2 = W2g[q4]
                            P.dma("sp", lambda e, w2=w2, q4=q4: e.dma_start(out=w2[:], in_=Wb_f2[l, q4 * 1024:(q4 + 1) * 1024, :].rearrange("(k p) c -> p k c", p=128)), writes=[Rw2])
                    RW2all = [r_ for (_, r_) in W2g]
                    NT2 = 256
                    hhs = [S([128, 8, NT2], F32, "hh2") for _ in range(3)]
                    sqs = [S([128, 8, NT2], BF16, "sq2") for _ in range(2)]
                    rstds = [S([128, NT2], F32, "rstd2") for _ in range(2)]
                    tmps2 = [S([128, NT2], F32, "tmp2") for _ in range(2)]
                    xTs = [S([128, 8, NT2], BF16, "xT2") for _ in range(2)]
                    aT, RaT = S([128, 32, NT2], BF16, "aT")
                    sqf, Rsqf = S([128, 8, NT2], BF16, "sqf")
                    rl = [S([128, NT2], F32, "rl") for _ in range(2)]
                    pi_ = [0]
                    ntok = T if not last else L
                    tl2 = list(range(0, ntok, NT2))
                    n = NT2

                    def mk_t2(ti2):
                        t0 = tl2[ti2]
                        col = b if t0 < L else 4
                        hh_, Rhh_ = hhs[ti2 % 3]
                        sq, Rsq = sqs[ti2 % 2]
                        rstd, Rrstd = rstds[ti2 % 2]
                        tmp, Rtmp = tmps2[ti2 % 2]
                        xTt, RxTt = xTs[ti2 % 2]
                        psn, Rpsn = PS[6]
                        psf, Rpsf = PS[7]

                        def sX():
                            P.dma("sp", lambda e: e.dma_start(out=hh_[:, :, 0:n], in_=fm(HT, t0, n)), writes=[Rhh_])
                            P.op("act", lambda e: e.activation(out=sq[:, :, 0:n], in_=hh_[:, :, 0:n], func=AF.Square), reads=[Rhh_], writes=[Rsq])

                        def sY():
                            def mm(e):
                                for k in range(8):
                                    ins = e.matmul(psn[:, 0:n], lhsT=onesb[:], rhs=sq[:, k, 0:n], start=(k == 0), stop=(k == 7))
                                return ins
                            P.op("pe", mm, reads=[Rsq, Ronesb], writes=[Rpsn])
                            P.op("act", lambda e: e.activation(out=rstd[:, 0:n], in_=psn[:, 0:n], func=AF.Sqrt, bias=EPS, scale=1.0 / D), reads=[Rpsn], writes=[Rrstd])
                            P.op("dve", lambda e: e.reciprocal(out=rstd[:, 0:n], in_=rstd[:, 0:n]), reads=[Rrstd], writes=[Rrstd])
                            for k in range(8):
                                P.op("dve", lambda e, k=k: e.scalar_tensor_tensor(
                                    out=tmp[:, 0:n], in0=hh_[:, k, 0:n], scalar=A2[:, l, k, col:col + 1], in1=rstd[:, 0:n], op0=ALU.mult, op1=ALU.mult),
                                    reads=[Rhh_, Rrstd, RA2], writes=[Rtmp])
                                P.op("act", lambda e, k=k: e.activation(out=xTt[:, k, 0:n], in_=tmp[:, 0:n], func=AF.Identity, bias=mcol(l, 3, k, col), scale=1.0),
                                     reads=[Rtmp, Rmod], writes=[RxTt])

                        def sZ1():
                            for cb in range(32):
                                w1, Rw1 = W1g[cb // 4]
                                c4 = cb % 4
                                ps, Rps = PS[pi_[0] % 4]
                                r_, Rr_ = rl[pi_[0] % 2]
                                pi_[0] += 1

                                def mm(e, ps=ps, w1=w1, c4=c4):
                                    for k in range(8):
                                        ins = e.matmul(ps[:, 0:n], lhsT=w1[:, k, c4 * 128:(c4 + 1) * 128], rhs=xTt[:, k, 0:n], start=(k == 0), stop=(k == 7))
                                    return ins
                                P.op("pe", mm, reads=[Rw1, RxTt], writes=[Rps])
                                P.op("act", lambda e, ps=ps, r_=r_: e.activation(out=r_[:, 0:n], in_=ps[:, 0:n], func=AF.Relu), reads=[Rps], writes=[Rr_])
                                P.op("pool", lambda e, r_=r_, cb=cb: e.tensor_tensor(out=aT[:, cb, 0:n], in0=r_[:, 0:n], in1=r_[:, 0:n], op=ALU.mult), reads=[Rr_], writes=[RaT])

                        def sZ2():
                            for ob in range(8):
                                ps, Rps = PS[4 + (pi_[0] % 2)]
                                pi_[0] += 1

                                def mm(e, ps=ps, ob=ob):
                                    for k in range(32):
                                        ins = e.matmul(ps[:, 0:n], lhsT=W2g[k // 8][0][:, k % 8, ob * 128:(ob + 1) * 128], rhs=aT[:, k, 0:n], start=(k == 0), stop=(k == 31))
                                    return ins
                                P.op("pe", mm, reads=RW2all + [RaT], writes=[Rps])
                                P.op("dve", lambda e, ps=ps, ob=ob: e.scalar_tensor_tensor(
                                    out=hh_[:, ob, 0:n], in0=ps[:, 0:n], scalar=mcol(l, 5, ob, col), in1=hh_[:, ob, 0:n], op0=ALU.mult, op1=ALU.add),
                                    reads=[Rps, Rhh_, Rmod], writes=[Rhh_])
                            if not last:
                                P.dma("sp", lambda e: e.dma_start(out=fm(HT, t0, n), in_=hh_[:, :, 0:n]), reads=[Rhh_])
                            else:
                                P.op("act", lambda e: e.activation(out=sqf[:, :, 0:n], in_=hh_[:, :, 0:n], func=AF.Square), reads=[Rhh_], writes=[Rsqf])

                        def sW():
                            if not last:
                                return

                            def mm(e):
                                for k in range(8):
                                    ins = e.matmul(psf[:, 0:n], lhsT=onesb[:], rhs=sqf[:, k, 0:n], start=(k == 0), stop=(k == 7))
                                return ins
                            P.op("pe", mm, reads=[Rsqf, Ronesb], writes=[Rpsf])
                            P.op("act", lambda e: e.activation(out=rstd[:, 0:n], in_=psf[:, 0:n], func=AF.Sqrt, bias=EPS, scale=1.0 / D), reads=[Rpsf], writes=[Rrstd])
                            P.op("dve", lambda e: e.reciprocal(out=rstd[:, 0:n], in_=rstd[:, 0:n]), reads=[Rrstd], writes=[Rrstd])
                            for k in range(8):
                                P.op("dve", lambda e, k=k: e.scalar_tensor_tensor(
                                    out=hh_[:, k, 0:n], in0=hh_[:, k, 0:n], scalar=fnw[:, k:k + 1], in1=rstd[:, 0:n], op0=ALU.mult, op1=ALU.mult),
                                    reads=[Rhh_, Rrstd, Rfnw], writes=[Rhh_])
                            P.dma("sp", lambda e: e.dma_start(out=fm(outT[b], t0, n), in_=hh_[:, :, 0:n]), reads=[Rhh_])
                        return dict(X=sX, Y=sY, Z1=sZ1, Z2=sZ2, W=sW)
                    nt2 = len(tl2)
                    st2 = {}
                    for t_ in range(-2, nt2 + 1):
                        for (nm_, off_) in (("W", -1), ("Y", 1), ("Z1", 0), ("X", 2), ("Z2", 0)):
                            u = t_ + off_
                            if 0 <= u < nt2:
                                if u not in st2:
                                    st2[u] = mk_t2(u)
                                st2[u][nm_]()
                        if t_ == -1:
                            emit_ffn_weights()
                    P.barrier()
                    P.emit("phD2")
                if b == 0 and l == layers[0]:
                    dbg_dump(P, "HT2", HT, [D, T], F32)
        P.barrier()
        P.emit()
    return nc, dbg_out


def _consts():
    k = np.arange(128)
    ident = np.eye(128, dtype=np.float32)
    triu = (k[:, None] <= k[None, :]).astype(np.float32)
    tril = (k[:, None] >= k[None, :]).astype(np.float32)
    negf1 = np.where(k[None, :] < k[:, None], NEG, 0.0).astype(np.float32)
    negb1 = np.where(k[None, :] > k[:, None], NEG, 0.0).astype(np.float32)
    negf = np.tile(negf1, (1, 4))
    negb = np.tile(negb1, (1, 4))
    t = np.arange(L, dtype=np.int32)
    row = (t // 64).astype(np.float32)
    col = (t % 64).astype(np.float32)
    half = 32
    inv = (np.float32(10000.0) ** (-np.arange(0, half, 2, dtype=np.float32) / np.float32(half))).astype(np.float32)
    ang_r = row[:, None] * inv
    ang_c = col[:, None] * inv
    ang = np.concatenate([ang_r, ang_r, ang_c, ang_c], axis=-1)
    cos = np.cos(ang).astype(np.float32).T
    sin = np.sin(ang).astype(np.float32).T
    cos = np.concatenate([cos, cos], axis=0)
    sin = np.concatenate([sin, sin], axis=0)
    rot = np.zeros((128, 128), np.float32)
    for m in range(128):
        if (m % 32) < 16:
            rot[m + 16, m] = -1.0
        else:
            rot[m - 16, m] = 1.0
    return dict(c_ident=ident, c_triu=triu, c_tril=tril, c_negf=negf, c_negb=negb,
                c_cos=np.ascontiguousarray(cos), c_sin=np.ascontiguousarray(sin), c_rot=rot)


def _shared_inputs(inp):
    f = lambda a: np.ascontiguousarray(np.asarray(a, dtype=np.float32))
    d = {}
    for k_ in ("w_ada", "w_in", "w_br_conv", "w_br_ssd", "w_br_na", "w_out", "w_ff1", "w_ff2"):
        d[k_] = f(inp[k_])
    d["b_ada_p"] = f(np.asarray(inp["b_ada"]).reshape(DEPTH, 48, 128).transpose(0, 2, 1))
    d["norm1_p"] = f(np.asarray(inp["norm1_w"]).reshape(DEPTH, 8, 128).transpose(0, 2, 1))
    d["norm2_p"] = f(np.asarray(inp["norm2_w"]).reshape(DEPTH, 8, 128).transpose(0, 2, 1))
    d["fnorm_p"] = f(np.asarray(inp["final_norm_w"]).reshape(8, 128).T)
    d["convw_p"] = f(np.asarray(inp["conv_mix_w"]).reshape(DEPTH, 3, 8, 128).transpose(0, 3, 2, 1))
    d["sconvw_p"] = f(np.asarray(inp["ssd_conv_w"]).reshape(DEPTH, 3, 12, 128).transpose(0, 3, 2, 1))
    d["sconvb_p"] = f(np.asarray(inp["ssd_conv_b"]).reshape(DEPTH, 12, 128).transpose(0, 2, 1))
    d["dt_bias"] = f(np.asarray(inp["ssd_dt_bias"]).reshape(DEPTH, 32))
    d["a_log"] = f(np.asarray(inp["ssd_a_log"]).reshape(DEPTH, 32))
    d["ssd_d"] = f(np.asarray(inp["ssd_d"]).reshape(DEPTH, 32))
    d["ssd_norm_w"] = f(inp["ssd_norm_w"])
    colv = np.arange(64)
    col_start = np.clip(colv - 8, 0, 48)
    col_ok = (colv[None, :] >= col_start[:, None]) & (colv[None, :] < col_start[:, None] + 16)
    dc_idx = np.clip(colv[None, :] - colv[:, None], -15, 15) + 15
    rpb = np.asarray(inp["na_rpb"], dtype=np.float32)
    g = rpb[:, :, :, dc_idx]
    g = np.where(col_ok[None, None, None], g, np.float32(NEG))
    d["TT"] = f(g.transpose(0, 3, 1, 2, 4))
    d.update(_consts())
    return d


def _core_inputs(inp, shared, b0, nb):
    x = np.asarray(inp["x"], dtype=np.float32)
    ctx = np.asarray(inp["ctx"], dtype=np.float32)
    c = np.asarray(inp["c"], dtype=np.float32)
    c_ctx = np.asarray(inp["c_ctx"], dtype=np.float32)
    xT = np.concatenate([x[b0:b0 + nb].transpose(0, 2, 1), ctx[b0:b0 + nb].transpose(0, 2, 1)], axis=2)
    cm = np.zeros((5, D), np.float32)
    cm[0:nb] = c[b0:b0 + nb]
    cm[4] = c_ctx
    cT = cm.T.reshape(8, 128, 5).transpose(1, 0, 2)
    m = dict(shared)
    m["xT"] = np.ascontiguousarray(xT)
    m["cT"] = np.ascontiguousarray(cT)
    return m


_CACHE = {}


def kernel(**inputs):
    if "nc" not in _CACHE:
        _CACHE["nc"] = build()[0]
    nc = _CACHE["nc"]
    shared = _shared_inputs(inputs)
    in_maps = [_core_inputs(inputs, shared, c * NB, NB) for c in range(NCORES)]
    res = run_bass_kernel_spmd(nc, in_maps, core_ids=list(range(NCORES)))
    outs = [np.asarray(r["outT"]).transpose(0, 2, 1) for r in res.results]
    return np.ascontiguousarray(np.concatenate(outs, axis=0).astype(np.float32))
```

```python
import numpy as np
import concourse.bass as bass
import concourse.mybir as mybir
from concourse.bass_utils import run_bass_kernel_spmd
from contextlib import ExitStack

F32 = mybir.dt.float32
BF16 = mybir.dt.bfloat16
AF = mybir.ActivationFunctionType
ALU = mybir.AluOpType
AX = mybir.AxisListType

NCORES = 8
NB = 4
D = 1024
L = 2048
CT = 256
T = L + CT
DEPTH = 2
INC = 11808
DFF = 4096
EPS = 1e-6
NEG = -30000.0
_DEFER_CVT = True
O_CB, O_CC, O_CX, O_Z, O_XBC, O_DT, O_Q, O_K, O_V, O_G = 0, 1024, 2048, 3072, 4096, 5632, 5664, 6688, 7712, 8736


class Res:
    __slots__ = ("name", "w", "r")

    def __init__(self, name=""):
        self.name = name
        self.w = None
        self.r = {}


class Prog:
    ENG = ("pe", "act", "dve", "pool", "sp")
    NLANE = {"sp": 8, "act": 2, "pool": 6}

    use_scopes = False

    def __init__(self, nc, es):
        self.nc = nc
        self.q = {e: [] for e in self.ENG}
        self.sem = {}
        self.cnt = {}
        for e in self.ENG:
            self.sem[e] = es.enter_context(nc.semaphore("s_" + e))
            self.cnt[e] = 0
        self.lane_rr = {}
        for e, n in self.NLANE.items():
            self.lane_rr[e] = 0
            for i in range(n):
                k = "%s_l%d" % (e, i)
                self.sem[k] = es.enter_context(nc.semaphore("s_" + k))
                self.cnt[k] = 0
        self.seen = {e: {} for e in self.ENG}

    def _deps(self, e, reads, writes):
        deps = {}
        for r in reads:
            if r.w is not None:
                o, i = r.w
                if deps.get(o, 0) < i:
                    deps[o] = i
        for w in writes:
            if w.w is not None:
                o, i = w.w
                if deps.get(o, 0) < i:
                    deps[o] = i
            for o, i in w.r.items():
                if deps.get(o, 0) < i:
                    deps[o] = i
        waits = []
        seen = self.seen[e]
        for o, i in deps.items():
            if seen.get(o, 0) < i:
                seen[o] = i
                waits.append((o, i))
        return waits

    def op(self, e, fn, reads=(), writes=()):
        waits = self._deps(e, reads, writes)
        self.cnt[e] += 1
        idx = self.cnt[e]
        self.q[e].append((fn, waits, e, 1))
        for r in reads:
            r.r[e] = idx
        for w in writes:
            w.w = (e, idx)
            w.r = {}

    def dma(self, e, fn, reads=(), writes=()):
        n = self.NLANE[e]
        li = self.lane_rr[e]
        self.lane_rr[e] = (li + 1) % n
        k = "%s_l%d" % (e, li)
        waits = self._deps(e, reads, writes)
        prev = self.cnt[k]
        if prev > 0 and self.seen[e].get(k, 0) < prev:
            self.seen[e][k] = prev
            waits.append((k, prev))
        self.cnt[k] += 16
        val = self.cnt[k]
        self.q[e].append((fn, waits, k, 16))
        for r in reads:
            r.r[k] = val
        for w in writes:
            w.w = (k, val)
            w.r = {}

    def barrier(self):
        for e in self.ENG:
            waits = []
            for k, c in self.cnt.items():
                if k != e and c > 0 and self.seen[e].get(k, 0) < c:
                    self.seen[e][k] = c
                    waits.append((k, c))
            if waits:
                self.q[e].append((None, waits, None, 0))

    def emit(self, scope=None):
        nc = self.nc
        engs = {"pe": "tensor", "act": "scalar", "dve": "vector", "pool": "gpsimd", "sp": "sync"}
        with ExitStack() as _es:
            if scope is not None and self.use_scopes:
                _es.enter_context(nc.named_scope(scope))
            block = _es.enter_context(nc.Block())
            for e in self.ENG:
                def body(eng, e=e):
                    for fn, waits, k, inc in self.q[e]:
                        for o, i in waits:
                            eng.wait_ge(self.sem[o], i)
                        if fn is not None:
                            ins = fn(eng)
                            ins.then_inc(self.sem[k], inc)
                getattr(block, engs[e])(body)
        self.q = {e: [] for e in self.ENG}


class TB:
    gid = 0

    def __init__(self, nc, es):
        self.nc = nc
        self.es = es
        self.n = 0

    def __call__(self, shape, dt=F32, name=None):
        self.n += 1
        TB.gid += 1
        t = self.es.enter_context(self.nc.sbuf_tensor("%s_%d" % (name or "t", TB.gid), list(shape), dt))
        return t, Res(name or "t")


def build(nb=NB, layers=(0, 1), dbg=()):
    nc = bass.Bass("TRN2", target_bir_lowering=False)

    def din(name, shape, dt=F32):
        return nc.dram_tensor(name, list(shape), dt, kind="ExternalInput").ap()

    def dscr(name, shape, dt=BF16):
        return nc.dram_tensor(name, list(shape), dt, kind="Internal").ap()

    xT = din("xT", [nb, D, T])
    cT = din("cT", [128, 8, 5])
    w_ada = din("w_ada", [DEPTH, D, 6 * D])
    w_in = din("w_in", [DEPTH, D, INC])
    w_brc = din("w_br_conv", [DEPTH, D, D])
    w_brs = din("w_br_ssd", [DEPTH, D, D])
    w_brn = din("w_br_na", [DEPTH, D, D])
    w_out = din("w_out", [DEPTH, D, D])
    w_ff1 = din("w_ff1", [DEPTH, D, DFF])
    w_ff2 = din("w_ff2", [DEPTH, DFF, D])
    b_ada_p = din("b_ada_p", [DEPTH, 128, 48])
    norm1_p = din("norm1_p", [DEPTH, 128, 8])
    norm2_p = din("norm2_p", [DEPTH, 128, 8])
    fnorm_p = din("fnorm_p", [128, 8])
    convw_p = din("convw_p", [DEPTH, 128, 8, 3])
    sconvw_p = din("sconvw_p", [DEPTH, 128, 12, 3])
    sconvb_p = din("sconvb_p", [DEPTH, 128, 12])
    dtb_d = din("dt_bias", [DEPTH, 32])
    alog_d = din("a_log", [DEPTH, 32])
    dsk_d = din("ssd_d", [DEPTH, 32])
    snw_d = din("ssd_norm_w", [DEPTH, D])
    TT_d = din("TT", [DEPTH, 64, 16, 15, 64])
    cident = din("c_ident", [128, 128])
    ctriu = din("c_triu", [128, 128])
    ctril = din("c_tril", [128, 128])
    cnegf = din("c_negf", [128, 512])
    cnegb = din("c_negb", [128, 512])
    ccos = din("c_cos", [128, L])
    csin = din("c_sin", [128, L])
    crot = din("c_rot", [128, 128])
    outT = nc.dram_tensor("outT", [nb, D, L], F32, kind="ExternalOutput").ap()

    HT = dscr("HT", [D, T], F32)
    S_gate = dscr("S_gate", [3 * D, T])
    S_xbc = dscr("S_xbc", [1536, T])
    S_dt = dscr("S_dt", [T, 32], F32)
    S_q = dscr("S_q", [D, T])
    S_k = dscr("S_k", [D, T])
    S_v = dscr("S_v", [T, D])
    S_z = dscr("S_z", [T, D])
    Y_conv = dscr("Y_conv", [D, T])
    Y_ssd = dscr("Y_ssd", [D, T])
    Y_na = dscr("Y_na", [D, T])
    YF = dscr("YF", [18, 128, D], F32)
    NBLK = 76
    Wb_in = dscr("Wb_in", [DEPTH, NBLK, 128, 8, 128])
    Wb_zv = dscr("Wb_zv", [DEPTH, D, 2048])
    Wb_dt = dscr("Wb_dt", [DEPTH, D, 32])
    Wb_br = dscr("Wb_br", [DEPTH, 4, D, D])
    Wb_f1 = dscr("Wb_f1", [DEPTH, D, DFF])
    Wb_f2 = dscr("Wb_f2", [DEPTH, DFF, D])
    BLK_COL = [j * 128 for j in range(24)] + [O_XBC + j * 128 for j in range(12)] + [O_Q + j * 128 for j in range(16)] + [O_G + j * 128 for j in range(24)]
    BLK_CB, BLK_CC, BLK_CX, BLK_XBC, BLK_Q, BLK_K, BLK_G = 0, 8, 16, 24, 36, 44, 52

    dbg_out = {}

    def dbg_dump(P, name, src_ap, shape, dt):
        if name in dbg and name not in dbg_out:
            o = nc.dram_tensor("dbg_" + name, list(shape), dt, kind="ExternalOutput").ap()
            dbg_out[name] = o
            P.barrier()
            P.dma("sp", lambda e, o=o: e.dma_start(out=o, in_=src_ap))
            P.barrier()

    def fm(ap2d, t0, n):
        return ap2d[:, t0:t0 + n].rearrange("(k p) t -> p k t", p=128)

    with ExitStack() as es:
        P = Prog(nc, es)
        G = TB(nc, es)
        identf, Ridentf = G([128, 128], F32, "identf")
        ident, Rident = G([128, 128], BF16, "ident")
        onesf, Ronesf = G([128, 128], F32, "onesf")
        onesb, Ronesb = G([128, 128], BF16, "onesb")
        triu, Rtriu = G([128, 128], F32, "triu")
        tril, Rtril = G([128, 128], F32, "tril")
        negf, Rnegf = G([128, 512], F32, "negf")
        negb, Rnegb = G([128, 512], F32, "negb")
        modt, Rmod = G([128, DEPTH, 48, 5], F32, "mod")
        A1, RA1 = G([128, DEPTH, 8, 5], F32, "A1")
        A2, RA2 = G([128, DEPTH, 8, 5], F32, "A2")
        fnw, Rfnw = G([128, 8], F32, "fnw")
        def psum_alloc(scope, nf, nb16):
            TB.gid += 1
            ps_ = [(scope.enter_context(nc.psum_tensor("ps%d_%d" % (i, TB.gid), [128, 512], F32)), Res("ps%d" % i)) for i in range(nf)]
            pb_ = [(scope.enter_context(nc.psum_tensor("psb%d_%d" % (i, TB.gid), [128, 1024], BF16)), Res("psb%d" % i)) for i in range(nb16)]
            return ps_, pb_

        P.dma("sp", lambda e: e.dma_start(out=identf[:], in_=cident), writes=[Ridentf])
        P.dma("sp", lambda e: e.dma_start(out=triu[:], in_=ctriu), writes=[Rtriu])
        P.dma("sp", lambda e: e.dma_start(out=tril[:], in_=ctril), writes=[Rtril])
        P.dma("sp", lambda e: e.dma_start(out=negf[:], in_=cnegf), writes=[Rnegf])
        P.dma("sp", lambda e: e.dma_start(out=negb[:], in_=cnegb), writes=[Rnegb])
        P.dma("sp", lambda e: e.dma_start(out=fnw[:], in_=fnorm_p), writes=[Rfnw])
        P.op("dve", lambda e: e.tensor_copy(out=ident[:], in_=identf[:]), reads=[Ridentf], writes=[Rident])
        P.op("pool", lambda e: e.memset(onesf[:], 1.0), writes=[Ronesf])
        P.op("pool", lambda e: e.memset(onesb[:], 1.0), writes=[Ronesb])

        with ExitStack() as ms:
            M = TB(nc, ms)
            PS, PSB = psum_alloc(ms, 2, 0)
            sc, Rsc = M([128, 8, 5], F32, "sc")
            wa = [M([128, 8, 768], F32, "wa") for _ in range(2)]
            bap, Rbap = M([128, DEPTH, 48], F32, "bap")
            n1, Rn1 = M([128, DEPTH, 8], F32, "n1")
            n2, Rn2 = M([128, DEPTH, 8], F32, "n2")
            P.dma("sp", lambda e: e.dma_start(out=sc[:], in_=cT), writes=[Rsc])
            P.op("act", lambda e: e.activation(out=sc[:], in_=sc[:], func=AF.Silu), reads=[Rsc], writes=[Rsc])
            for l in range(DEPTH):
                P.dma("sp", lambda e, l=l: e.dma_start(out=bap[:, l, :], in_=b_ada_p[l]), writes=[Rbap])
                P.dma("sp", lambda e, l=l: e.dma_start(out=n1[:, l, :], in_=norm1_p[l]), writes=[Rn1])
                P.dma("sp", lambda e, l=l: e.dma_start(out=n2[:, l, :], in_=norm2_p[l]), writes=[Rn2])
            it = 0
            for l in range(DEPTH):
                for g8 in range(8):
                    wt, Rwt = wa[it % 2]
                    it += 1
                    P.dma("sp", lambda e, l=l, g8=g8, wt=wt: e.dma_start(
                        out=wt[:], in_=w_ada[l, :, g8 * 768:(g8 + 1) * 768].rearrange("(k p) c -> p k c", p=128)), writes=[Rwt])
                    ps, Rps = PS[g8 % 2]

                    def mm(e, wt=wt, ps=ps):
                        for j in range(6):
                            for k in range(8):
                                ins = e.matmul(ps[:, j * 8:j * 8 + 5], lhsT=wt[:, k, j * 128:(j + 1) * 128], rhs=sc[:, k, :],
                                               start=(k == 0), stop=(k == 7))
                        return ins
                    P.op("pe", mm, reads=[Rwt, Rsc], writes=[Rps])
                    P.op("dve", lambda e, l=l, g8=g8, ps=ps: e.tensor_tensor(
                        out=modt[:, l, g8 * 6:(g8 + 1) * 6, :], in0=ps[:, 0:48].rearrange("p (j c) -> p j c", c=8)[:, :, 0:5],
                        in1=bap[:, l, g8 * 6:(g8 + 1) * 6].unsqueeze(2).to_broadcast([128, 6, 5]), op=ALU.add),
                        reads=[Rps, Rbap], writes=[Rmod])
            for l in range(DEPTH):
                for (At, RAt, nn, Rnn, j) in ((A1, RA1, n1, Rn1, 1), (A2, RA2, n2, Rn2, 4)):
                    P.op("dve", lambda e, At=At, j=j, l=l: e.tensor_scalar(
                        out=At[:, l, :, :], in0=modt[:, l, j * 8:(j + 1) * 8, :], scalar1=1.0, scalar2=None, op0=ALU.add),
                        reads=[Rmod], writes=[RAt])
                    P.op("dve", lambda e, At=At, nn=nn, l=l: e.tensor_tensor(
                        out=At[:, l, :, :], in0=At[:, l, :, :], in1=nn[:, l, :].unsqueeze(2).to_broadcast([128, 8, 5]), op=ALU.mult),
                        reads=[RAt, Rnn], writes=[RAt])
            P.barrier()
            P.emit()
        if "mod" in dbg:
            o = nc.dram_tensor("dbg_mod", [128, DEPTH, 48, 5], F32, kind="ExternalOutput").ap()
            dbg_out["mod"] = o
            P.dma("sp", lambda e, o=o: e.dma_start(out=o, in_=modt[:]), reads=[Rmod])

        def mcol(l, j, fc, col):
            return modt[:, l, j * 8 + fc, col:col + 1]

        with ExitStack() as cs0:
            Cv = TB(nc, cs0)
            NCB = 4
            cin = [Cv([128, 2048], F32, "cin") for _ in range(NCB)]
            cout = [Cv([128, 2048], BF16, "cout") for _ in range(NCB)]
            cvi = [0]

            deferred = []
            defer_mode = [False]

            def cvt(src_ap, dst_ap, shp):
                if defer_mode[0]:
                    deferred.append((src_ap, dst_ap, shp))
                    return
                i = cvi[0]
                cvi[0] += 1
                ti, Rti = cin[i % NCB]
                to, Rto = cout[i % NCB]
                n = 1
                for d_ in shp[1:]:
                    n *= d_
                if len(shp) == 3:
                    tv = ti[:, 0:n].rearrange("p (a b) -> p a b", b=shp[2])
                    ov = to[:, 0:n].rearrange("p (a b) -> p a b", b=shp[2])
                else:
                    tv = ti[:, 0:n]
                    ov = to[:, 0:n]
                P.dma("sp", lambda e: e.dma_start(out=tv, in_=src_ap), writes=[Rti])
                eng = ("act", "dve", "pool")[i % 3]
                if eng == "act":
                    P.op("act", lambda e: e.copy(out=to[:, 0:n], in_=ti[:, 0:n]), reads=[Rti], writes=[Rto])
                else:
                    P.op(eng, lambda e: e.tensor_copy(out=to[:, 0:n], in_=ti[:, 0:n]), reads=[Rti], writes=[Rto])
                P.dma("sp", lambda e: e.dma_start(out=dst_ap, in_=ov), reads=[Rto])
            for l in range(DEPTH):
                if l not in layers:
                    continue
                defer_mode[0] = (l != layers[0]) and _DEFER_CVT
                for k in range(8):
                    rows = slice(k * 128, (k + 1) * 128)
                    for (b0, nb_) in ((0, 16), (16, 8), (24, 12), (36, 16), (52, 16), (68, 8)):
                        c0 = BLK_COL[b0]
                        cvt(w_in[l, rows, c0:c0 + nb_ * 128].rearrange("p (b c) -> p b c", c=128),
                            Wb_in[l, b0:b0 + nb_, :, k, :].rearrange("b p c -> p b c"), [128, nb_, 128])
                    cvt(w_in[l, rows, O_Z:O_Z + 1024], Wb_zv[l, rows, 0:1024], [128, 1024])
                    cvt(w_in[l, rows, O_V:O_V + 1024], Wb_zv[l, rows, 1024:2048], [128, 1024])
                    cvt(w_in[l, rows, O_DT:O_DT + 32], Wb_dt[l, rows, :], [128, 32])
                    dm_ = defer_mode[0]
                    defer_mode[0] = _DEFER_CVT
                    for c2 in range(2):
                        cvt(w_ff1[l, rows, c2 * 2048:(c2 + 1) * 2048], Wb_f1[l, rows, c2 * 2048:(c2 + 1) * 2048], [128, 2048])
                    defer_mode[0] = dm_
                defer_mode[0] = _DEFER_CVT
                for wi, wd in enumerate((w_brc, w_brs, w_brn, w_out)):
                    for k2 in range(4):
                        cvt(wd[l, k2 * 256:(k2 + 1) * 256, :].rearrange("(k p) c -> p k c", p=128),
                            Wb_br[l, wi, k2 * 256:(k2 + 1) * 256, :].rearrange("(k p) c -> p k c", p=128), [128, 2, 1024])
                for k2 in range(16):
                    cvt(w_ff2[l, k2 * 256:(k2 + 1) * 256, :].rearrange("(k p) c -> p k c", p=128),
                        Wb_f2[l, k2 * 256:(k2 + 1) * 256, :].rearrange("(k p) c -> p k c", p=128), [128, 2, 1024])
            P.barrier()
            P.emit("cvt")

        bg = {"i": 0, "bufs": None}

        def bg_alloc(Sx):
            bg["bufs"] = ([Sx([128, 2048], F32, "bgin") for _ in range(3)], [Sx([128, 2048], BF16, "bgout") for _ in range(3)])

        def bg_cvt(ntask):
            for _ in range(ntask):
                if bg["i"] >= len(deferred) or bg["bufs"] is None:
                    return
                src_ap, dst_ap, shp = deferred[bg["i"]]
                i = bg["i"]
                bg["i"] += 1
                ti, Rti = bg["bufs"][0][i % 3]
                to, Rto = bg["bufs"][1][i % 3]
                n = 1
                for d_ in shp[1:]:
                    n *= d_
                if len(shp) == 3:
                    tv = ti[:, 0:n].rearrange("p (a b) -> p a b", b=shp[2])
                    ov = to[:, 0:n].rearrange("p (a b) -> p a b", b=shp[2])
                else:
                    tv = ti[:, 0:n]
                    ov = to[:, 0:n]
                P.dma("sp", lambda e, tv=tv, src_ap=src_ap: e.dma_start(out=tv, in_=src_ap), writes=[Rti])
                P.op("act", lambda e, to=to, ti=ti, n=n: e.copy(out=to[:, 0:n], in_=ti[:, 0:n]), reads=[Rti], writes=[Rto])
                P.dma("act", lambda e, ov=ov, dst_ap=dst_ap: e.dma_start(out=dst_ap, in_=ov), reads=[Rto])

        def rms_tile(l_or_none, At, col, htile, Rht, n, sq, Rsq, rstd, Rrstd, tmp, Rtmp, outT_tile, Rout, shift_j, psb):
            ps, Rps = psb
            P.op("act", lambda e: e.activation(out=sq[:, :, 0:n], in_=htile[:, :, 0:n], func=AF.Square), reads=[Rht], writes=[Rsq])

            def mm(e):
                for k in range(8):
                    ins = e.matmul(ps[:, 0:n], lhsT=onesf[:], rhs=sq[:, k, 0:n], start=(k == 0), stop=(k == 7))
                return ins
            P.op("pe", mm, reads=[Rsq, Ronesf], writes=[Rps])
            P.op("act", lambda e: e.activation(out=rstd[:, 0:n], in_=ps[:, 0:n], func=AF.Sqrt, bias=EPS, scale=1.0 / D), reads=[Rps], writes=[Rrstd])
            P.op("dve", lambda e: e.reciprocal(out=rstd[:, 0:n], in_=rstd[:, 0:n]), reads=[Rrstd], writes=[Rrstd])
            for k in range(8):
                if l_or_none is None:
                    sc_ap = fnw[:, k:k + 1]
                else:
                    sc_ap = At[:, l_or_none, k, col:col + 1]
                if shift_j is None:
                    P.op("dve", lambda e, k=k, sc_ap=sc_ap: e.scalar_tensor_tensor(
                        out=outT_tile[:, k, 0:n], in0=htile[:, k, 0:n], scalar=sc_ap, in1=rstd[:, 0:n], op0=ALU.mult, op1=ALU.mult),
                        reads=[Rht, Rrstd, RA1, RA2, Rfnw], writes=[Rout])
                else:
                    P.op("dve", lambda e, k=k, sc_ap=sc_ap: e.scalar_tensor_tensor(
                        out=tmp[:, 0:n], in0=htile[:, k, 0:n], scalar=sc_ap, in1=rstd[:, 0:n], op0=ALU.mult, op1=ALU.mult),
                        reads=[Rht, Rrstd, RA1, RA2], writes=[Rtmp])
                    P.op("act", lambda e, k=k: e.activation(out=outT_tile[:, k, 0:n], in_=tmp[:, 0:n], func=AF.Identity,
                                                           bias=mcol(l_or_none, shift_j, k, col), scale=1.0),
                         reads=[Rtmp, Rmod], writes=[Rout])

        TOK_TILES = [(0, 512), (512, 512), (1024, 512), (1536, 512), (2048, 256)]

        def wsrc(wd, l, c0, cw, kc=8):
            return wd[l, :, c0:c0 + cw].rearrange("(k p) c -> p k c", p=128)

        for b in range(nb):
            for l in layers:
                src_h = xT[b] if l == layers[0] else HT
                last = (l == DEPTH - 1)
                P.barrier()
                with ExitStack() as s1:
                    S = TB(nc, s1)
                    PS, PSB = psum_alloc(s1, 6, 0)
                    hmT, RhmT_all = S([128, 8, T], BF16, "hmT")
                    RhmT_t = {t0_: Res("hmT%d" % t0_) for (t0_, n_) in TOK_TILES}
                    hld = [S([128, 8, 512], F32, "hld") for _ in range(2)]
                    sq, Rsq = S([128, 8, 512], BF16, "sq")
                    rstd, Rrstd = S([128, 512], F32, "rstd")
                    tmp, Rtmp = S([128, 512], F32, "tmp")
                    tmpsA = [(tmp, Rtmp)] + [S([128, 512], F32, "tmpA2")]

                    def norm_tile(ti):
                        t0, n = TOK_TILES[ti]
                        ht, Rht = hld[ti % 2]
                        P.dma("sp", lambda e, ht=ht, t0=t0, n=n: e.dma_start(out=ht[:, :, 0:n], in_=fm(src_h, t0, n)), writes=[Rht])
                        col = b if t0 < L else 4
                        class _V:
                            pass
                        hv = hmT[:, :, t0:t0 + n]
                        ps_b = PS[ti % 2]
                        P.op("act", lambda e, ht=ht, n=n: e.activation(out=sq[:, :, 0:n], in_=ht[:, :, 0:n], func=AF.Square), reads=[Rht], writes=[Rsq])
                        ps, Rps = ps_b

                        def mm(e, ps=ps, n=n):
                            for k in range(8):
                                ins = e.matmul(ps[:, 0:n], lhsT=onesb[:], rhs=sq[:, k, 0:n], start=(k == 0), stop=(k == 7))
                            return ins
                        P.op("pe", mm, reads=[Rsq, Ronesb], writes=[Rps])
                        P.op("act", lambda e, ps=ps, n=n: e.activation(out=rstd[:, 0:n], in_=ps[:, 0:n], func=AF.Sqrt, bias=EPS, scale=1.0 / D), reads=[Rps], writes=[Rrstd])
                        P.op("dve", lambda e, n=n: e.reciprocal(out=rstd[:, 0:n], in_=rstd[:, 0:n]), reads=[Rrstd], writes=[Rrstd])
                        for k in range(8):
                            tmp_, Rtmp_ = tmpsA[k % 2]
                            P.op("dve", lambda e, k=k, ht=ht, n=n, col=col, tmp_=tmp_: e.scalar_tensor_tensor(
                                out=tmp_[:, 0:n], in0=ht[:, k, 0:n], scalar=A1[:, l, k, col:col + 1], in1=rstd[:, 0:n], op0=ALU.mult, op1=ALU.mult),
                                reads=[Rht, Rrstd, RA1], writes=[Rtmp_])
                            P.op("act", lambda e, k=k, t0=t0, n=n, col=col, tmp_=tmp_: e.activation(
                                out=hmT[:, k, t0:t0 + n], in_=tmp_[:, 0:n], func=AF.Identity, bias=mcol(l, 0, k, col), scale=1.0),
                                reads=[Rtmp_, Rmod], writes=[RhmT_t[t0]])
                    wb = [S([128, 8, 128], BF16, "wb") for _ in range(7)]
                    wbi = [0]

                    blk_order = ([x for j in range(8) for x in (BLK_CB + j, BLK_CC + j, BLK_CX + j)] + [BLK_XBC + j for j in range(12)]
                                 + [BLK_G + j for j in range(24)] + [BLK_Q + j for j in range(16)])
                    wq_issued = [0]
                    NWB = 7
                    PREF = 5

                    def _issue_until(nmax):
                        while wq_issued[0] < min(nmax, len(blk_order)):
                            i_ = wq_issued[0]
                            wt, Rwt = wb[i_ % NWB]
                            P.dma("sp", lambda e, wt=wt, blk=blk_order[i_]: e.dma_start(out=wt[:], in_=Wb_in[l, blk]), writes=[Rwt])
                            wq_issued[0] += 1

                    def load_w(blk, cw=128):
                        i_ = wbi[0]
                        assert blk_order[i_] == blk, (i_, blk, blk_order[i_])
                        wbi[0] += 1
                        _issue_until(i_ + PREF)
                        return wb[i_ % NWB]
                    psi = [0]

                    def mm_tile(wt, Rwt, t0, n, cw=128):
                        ps, Rps = PS[psi[0] % 4]
                        psi[0] += 1

                        def mm(e):
                            for k in range(8):
                                ins = e.matmul(ps[0:cw, 0:n], lhsT=wt[:, k, 0:cw], rhs=hmT[:, k, t0:t0 + n], start=(k == 0), stop=(k == 7))
                            return ins
                        P.op("pe", mm, reads=[Rwt, RhmT_t[t0]], writes=[Rps])
                        return ps, Rps

                    cxp, Rcxp = S([128, L + 2], F32, "cxp")
                    cxc, Rcxc = S([128, CT + 2], F32, "cxc")
                    cbs, Rcbs = S([128, T], BF16, "cbs")
                    acc, Racc = S([128, T], F32, "acc")
                    ybf, Rybf = S([128, T], BF16, "ybf")
                    csb, Rcsb = S([128, 512], F32, "csb")
                    cw_t, Rcw = S([128, 8, 3], F32, "cw")
                    sw_t, Rsw = S([128, 12, 3], F32, "sw")
                    sb_t, Rsb = S([128, 12], F32, "sb")
                    P.dma("sp", lambda e: e.dma_start(out=cw_t[:], in_=convw_p[l]), writes=[Rcw])
                    P.dma("sp", lambda e: e.dma_start(out=sw_t[:], in_=sconvw_p[l]), writes=[Rsw])
                    P.dma("sp", lambda e: e.dma_start(out=sb_t[:], in_=sconvb_p[l]), writes=[Rsb])
                    P.op("pool", lambda e: e.memset(cxp[:], 0.0), writes=[Rcxp])
                    P.op("pool", lambda e: e.memset(cxc[:], 0.0), writes=[Rcxc])

                    def pad_dst(t0, n):
                        if t0 < L:
                            return cxp[:, 1 + t0:1 + t0 + n], Rcxp
                        return cxc[:, 1:1 + n], Rcxc

                    def conv3(wtile, j, bias_ap):
                        for (pad, Rpad, o0, n) in ((cxp, Rcxp, 0, L), (cxc, Rcxc, L, CT)):
                            if bias_ap is None:
                                P.op("dve", lambda e, pad=pad, o0=o0, n=n: e.tensor_scalar(
                                    out=acc[:, o0:o0 + n], in0=pad[:, 0:n], scalar1=wtile[:, j, 0:1], scalar2=None, op0=ALU.mult),
                                    reads=[Rpad, Rcw, Rsw], writes=[Racc])
                            else:
                                P.op("dve", lambda e, pad=pad, o0=o0, n=n: e.tensor_scalar(
                                    out=acc[:, o0:o0 + n], in0=pad[:, 0:n], scalar1=wtile[:, j, 0:1], scalar2=bias_ap, op0=ALU.mult, op1=ALU.add),
                                    reads=[Rpad, Rcw, Rsw, Rsb], writes=[Racc])
                            for tap in (1, 2):
                                P.op("dve", lambda e, pad=pad, o0=o0, n=n, tap=tap: e.scalar_tensor_tensor(
                                    out=acc[:, o0:o0 + n], in0=pad[:, tap:tap + n], scalar=wtile[:, j, tap:tap + 1], in1=acc[:, o0:o0 + n],
                                    op0=ALU.mult, op1=ALU.add), reads=[Rpad, Racc, Rcw, Rsw], writes=[Racc])

                    cbs2, Rcbs2 = S([128, T], BF16, "cbs2")
                    padsets = [((cxp, Rcxp), (cxc, Rcxc), (cbs, Rcbs)), ((cxp, Rcxp), (cxc, Rcxc), (cbs2, Rcbs2))]

                    def conv_tile(j, wB, wC, wX, t0, n):
                        (cxp_, Rcxp_), (cxc_, Rcxc_), (cbs_, Rcbs_) = padsets[j % 2]
                        pb, Rpb = mm_tile(*wB, t0, n)
                        P.op("act", lambda e: e.copy(out=cbs_[:, t0:t0 + n], in_=pb[:, 0:n]), reads=[Rpb], writes=[Rcbs_])
                        pc, Rpc = mm_tile(*wC, t0, n)
                        P.op("act", lambda e: e.copy(out=csb[:, 0:n], in_=pc[:, 0:n]), reads=[Rpc], writes=[Rcsb])
                        px, Rpx = mm_tile(*wX, t0, n)
                        if t0 < L:
                            dst, Rdst = cxp_[:, 1 + t0:1 + t0 + n], Rcxp_
                        else:
                            dst, Rdst = cxc_[:, 1:1 + n], Rcxc_
                        P.op("dve", lambda e: e.tensor_tensor(out=dst, in0=px[:, 0:n], in1=csb[:, 0:n], op=ALU.mult), reads=[Rpx, Rcsb], writes=[Rdst])

                    def conv_fin(j):
                        (cxp_, Rcxp_), (cxc_, Rcxc_), (cbs_, Rcbs_) = padsets[j % 2]
                        for (pad, Rpad, o0, n) in (((cxp_, Rcxp_, 0, L), (cxc_, Rcxc_, L, CT)) if not last else ((cxp_, Rcxp_, 0, L),)):
                            P.op("dve", lambda e, pad=pad, o0=o0, n=n: e.tensor_scalar(
                                out=acc[:, o0:o0 + n], in0=pad[:, 0:n], scalar1=cw_t[:, j, 0:1], scalar2=None, op0=ALU.mult),
                                reads=[Rpad, Rcw], writes=[Racc])
                            for tap in (1, 2):
                                P.op("dve", lambda e, pad=pad, o0=o0, n=n, tap=tap: e.scalar_tensor_tensor(
                                    out=acc[:, o0:o0 + n], in0=pad[:, tap:tap + n], scalar=cw_t[:, j, tap:tap + 1], in1=acc[:, o0:o0 + n],
                                    op0=ALU.mult, op1=ALU.add), reads=[Rpad, Racc, Rcw], writes=[Racc])
                        P.op("pool", lambda e: e.tensor_tensor(out=ybf[:], in0=acc[:], in1=cbs_[:], op=ALU.mult), reads=[Racc, Rcbs_], writes=[Rybf])
                        P.dma("sp", lambda e: e.dma_start(out=Y_conv[j * 128:(j + 1) * 128, :], in_=ybf[:]), reads=[Rybf])
                    w0 = (load_w(BLK_CB + 0), load_w(BLK_CC + 0), load_w(BLK_CX + 0))
                    TILES_X = TOK_TILES if not last else TOK_TILES[:4]
                    for ti in range(len(TOK_TILES)):
                        norm_tile(ti)
                        if ti >= 1:
                            conv_tile(0, *w0, *TOK_TILES[ti - 1])
                    if not last:
                        conv_tile(0, *w0, *TOK_TILES[-1])
                    conv_fin(0)
                    for j in range(1, 8):
                        wj = (load_w(BLK_CB + j), load_w(BLK_CC + j), load_w(BLK_CX + j))
                        for (t0, n) in TILES_X:
                            conv_tile(j, *wj, t0, n)
                        conv_fin(j)
                    for j in range(12):
                        wX = load_w(BLK_XBC + j)
                        for (t0, n) in TOK_TILES:
                            px, Rpx = mm_tile(*wX, t0, n)
                            dst, Rdst = pad_dst(t0, n)
                            P.op("act", lambda e, px=px, n=n, dst=dst: e.copy(out=dst, in_=px[:, 0:n]), reads=[Rpx], writes=[Rdst])
                        conv3(sw_t, j, sb_t[:, j:j + 1])
                        P.op("act", lambda e: e.activation(out=ybf[:], in_=acc[:], func=AF.Silu), reads=[Racc], writes=[Rybf])
                        P.dma("sp", lambda e, j=j: e.dma_start(out=S_xbc[j * 128:(j + 1) * 128, :], in_=ybf[:]), reads=[Rybf])
                    for j in range(24):
                        wX = load_w(BLK_G + j)
                        for (t0, n) in TILES_X:
                            px, Rpx = mm_tile(*wX, t0, n)
                            P.op("act", lambda e, px=px, t0=t0, n=n: e.activation(out=ybf[:, t0:t0 + n], in_=px[:, 0:n], func=AF.Sigmoid),
                                 reads=[Rpx], writes=[Rybf])
                        P.dma("sp", lambda e, j=j: e.dma_start(out=S_gate[j * 128:(j + 1) * 128, :], in_=ybf[:]), reads=[Rybf])
                    cos_t, Rcos = S([128, L], F32, "cos")
                    sin_t, Rsin = S([128, L], F32, "sin")
                    rot_t, Rrot = S([128, 128], BF16, "rot")
                    rotf, Rrotf = S([128, 128], F32, "rotf")
                    ub, Rub = S([128, 512], BF16, "ub")
                    t1, Rt1 = S([128, 512], F32, "t1")
                    t2, Rt2 = S([128, 512], F32, "t2")
                    P.dma("sp", lambda e: e.dma_start(out=cos_t[:], in_=ccos), writes=[Rcos])
                    P.dma("sp", lambda e: e.dma_start(out=sin_t[:], in_=csin), writes=[Rsin])
                    P.dma("sp", lambda e: e.dma_start(out=rotf[:], in_=crot), writes=[Rrotf])
                    P.op("dve", lambda e: e.tensor_copy(out=rot_t[:], in_=rotf[:]), reads=[Rrotf], writes=[Rrot])
                    ubs = [(ub, Rub)] + [S([128, 512], BF16, "ub2")]
                    t1s = [(t1, Rt1), (t1, Rt1)]
                    t2s = [(t2, Rt2), (t2, Rt2)]
                    ybq = [(ybf, Rybf)] + [S([128, T], BF16, "ybf2")]
                    rix = [0]
                    pend = [None]
                    bix = 0
                    for (o_col, dstS, scl) in ((BLK_Q, S_q, 0.125), (BLK_K, S_k, 1.0)):
                        for j in range(8):
                            wX = load_w(o_col + j)
                            yb_, Ryb_ = ybq[bix % 2]
                            bix += 1
                            for (t0, n) in (TILES_X if o_col == BLK_Q else TOK_TILES):
                                px, Rpx = mm_tile(*wX, t0, n)
                                if t0 >= L:
                                    P.op("act", lambda e, px=px, t0=t0, n=n, scl=scl, yb_=yb_: e.activation(
                                        out=yb_[:, t0:t0 + n], in_=px[:, 0:n], func=AF.Copy, scale=scl), reads=[Rpx], writes=[Ryb_])
                                    continue
                                ub_, Rub_ = ubs[rix[0] % 2]
                                t1_, Rt1_ = t1s[rix[0] % 2]
                                t2_, Rt2_ = t2s[rix[0] % 2]
                                pr, Rpr = PS[4 + (rix[0] % 2)]
                                rix[0] += 1
                                P.op("act", lambda e, px=px, n=n, scl=scl, ub_=ub_: e.activation(out=ub_[:, 0:n], in_=px[:, 0:n], func=AF.Copy, scale=scl),
                                     reads=[Rpx], writes=[Rub_])
                                if pend[0] is not None:
                                    pend[0]()

                                def post(pr=pr, Rpr=Rpr, ub_=ub_, Rub_=Rub_, t1_=t1_, Rt1_=Rt1_, t2_=t2_, Rt2_=Rt2_, t0=t0, n=n, yb_=yb_, Ryb_=Ryb_):
                                    P.op("pe", lambda e: e.matmul(pr[:, 0:n], lhsT=rot_t[:], rhs=ub_[:, 0:n], start=True, stop=True),
                                         reads=[Rrot, Rub_], writes=[Rpr])
                                    P.op("pool", lambda e: e.tensor_tensor(out=t1_[:, 0:n], in0=ub_[:, 0:n], in1=cos_t[:, t0:t0 + n], op=ALU.mult),
                                         reads=[Rub_, Rcos], writes=[Rt1_])
                                    P.op("dve", lambda e: e.tensor_tensor(out=t2_[:, 0:n], in0=pr[:, 0:n], in1=sin_t[:, t0:t0 + n], op=ALU.mult),
                                         reads=[Rpr, Rsin], writes=[Rt2_])
                                    P.op("pool", lambda e: e.tensor_tensor(out=yb_[:, t0:t0 + n], in0=t1_[:, 0:n], in1=t2_[:, 0:n], op=ALU.add),
                                         reads=[Rt1_, Rt2_], writes=[Ryb_])
                                pend[0] = post

                            def fin(j=j, dstS=dstS, yb_=yb_, Ryb_=Ryb_):
                                P.dma("sp", lambda e: e.dma_start(out=dstS[j * 128:(j + 1) * 128, :], in_=yb_[:]), reads=[Ryb_])
                            prev_post = pend[0]

                            def post_and_fin(prev_post=prev_post, fin=fin):
                                prev_post()
                                fin()
                            pend[0] = post_and_fin
                    if pend[0] is not None:
                        pend[0]()
                        pend[0] = None
                    wz, Rwz = S([128, 8, D], BF16, "wz")
                    wv, Rwv = S([128, 8, D], BF16, "wv")
                    wdt, Rwdt = S([128, 8, 32], BF16, "wdt")
                    dtb, Rdtb = S([128, 32], F32, "dtb")
                    P.dma("sp", lambda e: e.dma_start(out=wz[:], in_=Wb_zv[l, :, 0:1024].rearrange("(k p) c -> p k c", p=128)), writes=[Rwz])
                    P.dma("sp", lambda e: e.dma_start(out=wv[:], in_=Wb_zv[l, :, 1024:2048].rearrange("(k p) c -> p k c", p=128)), writes=[Rwv])
                    P.dma("sp", lambda e: e.dma_start(out=wdt[:], in_=Wb_dt[l].rearrange("(k p) c -> p k c", p=128)), writes=[Rwdt])
                    P.dma("sp", lambda e: e.dma_start(out=dtb[:], in_=dtb_d[l:l + 1, :].partition_broadcast(128)), writes=[Rdtb])
                    ztm = [S([128, D], BF16, "ztm") for _ in range(2)]
                    vtm = [S([128, D], BF16, "vtm") for _ in range(2)]
                    dx, Rdx = S([128, 32], F32, "dx")
                    dax, Rdax = S([128, 32], F32, "dax")
                    dout = [S([128, 32], F32, "dout") for _ in range(2)]
                    for tt in range(18):
                        t0 = tt * 128
                        zt, Rzt = ztm[tt % 2]
                        vt, Rvt = vtm[tt % 2]
                        for (wt_, Rw_, dstt, Rd_, fn_) in ((wz, Rwz, zt, Rzt, AF.Silu), (wv, Rwv, vt, Rvt, AF.Copy)):
                            if last and tt >= 16 and fn_ == AF.Silu:
                                continue
                            for hh in range(2):
                                ps, Rps = PS[psi[0] % 4]
                                psi[0] += 1

                                def mm(e, ps=ps, wt_=wt_, hh=hh, t0=t0):
                                    for k in range(8):
                                        ins = e.matmul(ps[:, :], lhsT=hmT[:, k, t0:t0 + 128], rhs=wt_[:, k, hh * 512:(hh + 1) * 512], start=(k == 0), stop=(k == 7))
                                    return ins
                                P.op("pe", mm, reads=[Rw_, RhmT_t[(t0 // 512) * 512]], writes=[Rps])
                                P.op("act", lambda e, ps=ps, dstt=dstt, hh=hh, fn_=fn_: e.activation(out=dstt[:, hh * 512:(hh + 1) * 512], in_=ps[:, :], func=fn_),
                                     reads=[Rps], writes=[Rd_])
                        if not (last and tt >= 16):
                            P.dma("sp", lambda e, zt=zt, t0=t0: e.dma_start(out=S_z[t0:t0 + 128, :], in_=zt[:]), reads=[Rzt])
                        P.dma("sp", lambda e, vt=vt, t0=t0: e.dma_start(out=S_v[t0:t0 + 128, :], in_=vt[:]), reads=[Rvt])
                        ps, Rps = PS[psi[0] % 4]
                        psi[0] += 1

                        def mmd(e, ps=ps, t0=t0):
                            for k in range(8):
                                ins = e.matmul(ps[:, 0:32], lhsT=hmT[:, k, t0:t0 + 128], rhs=wdt[:, k, :], start=(k == 0), stop=(k == 7))
                            return ins
                        P.op("pe", mmd, reads=[Rwdt, RhmT_t[(t0 // 512) * 512]], writes=[Rps])
                        do, Rdo = dout[tt % 2]
                        P.op("dve", lambda e, ps=ps: e.tensor_tensor(out=dx[:], in0=ps[:, 0:32], in1=dtb[:], op=ALU.add), reads=[Rps, Rdtb], writes=[Rdx])
                        P.op("act", lambda e: e.activation(out=dax[:], in_=dx[:], func=AF.Abs), reads=[Rdx], writes=[Rdax])
                        P.op("act", lambda e: e.activation(out=dax[:], in_=dax[:], func=AF.Exp, scale=-1.0), reads=[Rdax], writes=[Rdax])
                        P.op("act", lambda e: e.activation(out=dax[:], in_=dax[:], func=AF.Ln, bias=1.0, scale=1.0), reads=[Rdax], writes=[Rdax])
                        P.op("dve", lambda e, do=do: e.scalar_tensor_tensor(out=do[:], in0=dx[:], scalar=0.0, in1=dax[:], op0=ALU.max, op1=ALU.add),
                             reads=[Rdx, Rdax], writes=[Rdo])
                        P.dma("sp", lambda e, do=do, t0=t0: e.dma_start(out=S_dt[t0:t0 + 128, :], in_=do[:]), reads=[Rdo])
                    P.barrier()
                    P.emit("phA")
                if b == 0 and l == layers[0]:
                    dbg_dump(P, "Y_conv", Y_conv, [D, T], BF16)
                    dbg_dump(P, "S_xbc", S_xbc, [1536, T], BF16)
                    dbg_dump(P, "S_gate", S_gate, [3 * D, T], BF16)
                    dbg_dump(P, "S_q", S_q, [D, T], BF16)
                    dbg_dump(P, "S_k", S_k, [D, T], BF16)
                    dbg_dump(P, "S_v", S_v, [T, D], BF16)
                    dbg_dump(P, "S_z", S_z, [T, D], BF16)
                    dbg_dump(P, "S_dt", S_dt, [T, 32], F32)
                if "stopA" in dbg:
                    continue

                P.barrier()
                with ExitStack() as s2:
                    S = TB(nc, s2)
                    PS, PSB = psum_alloc(s2, 7, 1)
                    psb, Rpsb = PSB[0]
                    alog, Ralog = S([128, 32], F32, "alog")
                    dskt, Rdsk = S([128, 32], F32, "dsk")
                    dsum, Rdsum = S([128, 16], F32, "dsum")
                    snw, Rsnw = S([128, D], F32, "snw")
                    P.dma("sp", lambda e: e.dma_start(out=alog[:], in_=alog_d[l:l + 1, :].partition_broadcast(128)), writes=[Ralog])
                    P.dma("sp", lambda e: e.dma_start(out=dskt[:], in_=dsk_d[l:l + 1, :].partition_broadcast(128)), writes=[Rdsk])
                    P.dma("sp", lambda e: e.dma_start(out=snw[:], in_=snw_d[l:l + 1, :].partition_broadcast(128)), writes=[Rsnw])
                    P.op("act", lambda e: e.activation(out=alog[:], in_=alog[:], func=AF.Exp), reads=[Ralog], writes=[Ralog])
                    P.op("dve", lambda e: e.tensor_scalar(out=alog[:], in0=alog[:], scalar1=-1.0, scalar2=None, op0=ALU.mult), reads=[Ralog], writes=[Ralog])
                    P.op("dve", lambda e: e.tensor_tensor(out=dsum[:], in0=dskt[:, 0:16], in1=dskt[:, 16:32], op=ALU.add), reads=[Rdsk], writes=[Rdsum])
                    hst, Rhst = S([128, D], F32, "hst")
                    hbf, Rhbf = S([128, D], BF16, "hbf")
                    bg_on = (b == 0 and l == layers[0] and bg["i"] < len(deferred))
                    if bg_on:
                        bg_alloc(S)
                    NR = 2

                    def rot(shape, dt, nm, n=NR):
                        return [S(shape, dt, nm) for _ in range(n)]
                    xbcT = rot([128, 12, 128], BF16, "xbcT", 5)
                    dtt = rot([128, 32], F32, "dtt", 5)
                    ztm_ = rot([128, D], BF16, "zt", 6)
                    yfl = rot([128, D], F32, "yfl", 3)
                    xs_tms = rot([128, D], BF16, "xs_tm", 3)
                    B_tms = rot([128, 256], BF16, "B_tm", 3)
                    a_ts = rot([128, 16], F32, "a", 3)
                    cs_ts = rot([128, 16], F32, "cs", 3)
                    ncs_ts = rot([128, 16], F32, "ncs", 3)
                    dout_ts = rot([128, 16], F32, "dout", 3)
                    dst_ts = rot([128, 16], F32, "dst", 3)
                    cdec_ts = rot([128, 16], F32, "cdec", 3)
                    Xds = rot([128, D], BF16, "Xd", 3)
                    Xss = rot([128, D], BF16, "Xs", 3)
                    stsbs = rot([128, D], F32, "stsb", 3)
                    segrs = rot([128, 16, 128], F32, "segr")
                    Lms = rot([128, 16, 128], BF16, "Lm", 2)
                    scTs = rot([128, 256], BF16, "scT", 3)
                    MTs = rot([128, 16, 128], BF16, "MT")
                    yaccs = rot([128, D], F32, "yacc", 3)
                    ytmps = rot([128, D], F32, "ytmp", 1)
                    y3s = rot([128, D], BF16, "y3", 2)
                    pres = rot([128, D], F32, "pre", 4)
                    gsums = rot([128, 2], F32, "gsum", 3)
                    yTs_ = rot([128, 8, 128], BF16, "yT")
                    RYF = [Res("YF%d" % c) for c in range(18)]
                    passes = []
                    for d_ in range(2):
                        order = [16, 17] + list(range(16)) if d_ == 0 else [17, 16] + list(range(15, -1, -1))
                        for oi, c in enumerate(order):
                            passes.append((d_, c, oi == 0))

                    def mk_pass(pi):
                        d_, c, first = passes[pi]
                        tok0 = c * 128
                        tri_d, Rtri_d = (triu, Rtriu) if d_ == 0 else (tril, Rtril)
                        neg_d, Rneg_d = (negf, Rnegf) if d_ == 0 else (negb, Rnegb)
                        xb, Rxb = xbcT[pi % 5]
                        dt_, Rdt_ = dtt[pi % 5]
                        zt, Rzt = ztm_[pi % 6]
                        yf, Ryf = yfl[pi % 3]
                        xs_tm, Rxs = xs_tms[pi % 3]
                        B_tm, RBtm = B_tms[pi % 3]
                        a_t, Ra = a_ts[pi % 3]
                        cs_t, Rcs = cs_ts[pi % 3]
                        ncs_t, Rncs = ncs_ts[pi % 3]
                        dout_t, Rdout = dout_ts[pi % 3]
                        dst_t, Rdst_ = dst_ts[pi % 3]
                        cdec_t, Rcdec = cdec_ts[pi % 3]
                        Xd, RXd = Xds[pi % 3]
                        Xs, RXs = Xss[pi % 3]
                        stsb, Rstsb = stsbs[pi % 3]
                        segr, Rsegr = segrs[pi % NR]
                        Lm, RLm = Lms[pi % 2]
                        scT, RscT = scTs[pi % 3]
                        MT, RMT = MTs[pi % NR]
                        yacc, Ryacc = yaccs[pi % 3]
                        ytmp, Rytmp = ytmps[0]
                        y3, Ry3 = y3s[pi % 2]
                        pre, Rpre = pres[pi % 4]
                        gsum, Rgsum = gsums[pi % 3]
                        yT, RyT = yTs_[pi % NR]
                        p4, Rp4 = PS[4]

                        def ld():
                            P.dma("sp", lambda e: e.dma_start(out=xb[:], in_=S_xbc[:, tok0:tok0 + 128].rearrange("(k p) t -> p k t", p=128)), writes=[Rxb])
                            P.dma("sp", lambda e: e.dma_start(out=dt_[:], in_=S_dt[tok0:tok0 + 128, :]), writes=[Rdt_])
                            if d_ == 1:
                                P.dma("sp", lambda e: e.dma_start(out=zt[:], in_=S_z[tok0:tok0 + 128, :]), writes=[Rzt])

                        def noop():
                            pass

                        def early():
                            if d_ == 1:
                                P.dma("sp", lambda e: e.dma_start(out=yf[:], in_=YF[c]), reads=[RYF[c]], writes=[Ryf])

                            def trx(e):
                                for k in range(8):
                                    ins = e.transpose(psb[:, k * 128:(k + 1) * 128], xb[:, k, :], ident[:])
                                return ins
                            P.op("pe", trx, reads=[Rxb, Rident], writes=[Rpsb])
                            P.op("act", lambda e: e.copy(out=xs_tm[:], in_=psb[:, :]), reads=[Rpsb], writes=[Rxs])
                            P.op("dve", lambda e: e.tensor_tensor(out=a_t[:], in0=dt_[:, d_ * 16:(d_ + 1) * 16], in1=alog[:, d_ * 16:(d_ + 1) * 16], op=ALU.mult),
                                 reads=[Rdt_, Ralog], writes=[Ra])

                            def mcs(e):
                                e.matmul(p4[:, 0:16], lhsT=tri_d[:], rhs=a_t[:], start=True, stop=True)
                                e.matmul(p4[:, 16:32], lhsT=onesf[:], rhs=a_t[:], start=True, stop=True)
                                for g in range(2):
                                    ins = e.matmul(p4[:, 128 + g * 128:256 + g * 128], lhsT=xb[:, 8 + g, :], rhs=xb[:, 10 + g, :], start=True, stop=True)
                                return ins
                            P.op("pe", mcs, reads=[Rtri_d, Ronesf, Ra, Rxb], writes=[Rp4])
                            P.op("dve", lambda e: e.tensor_copy(out=cs_t[:], in_=p4[:, 0:16]), reads=[Rp4], writes=[Rcs, Rp4])
                            P.op("dve", lambda e: e.tensor_tensor(out=dst_t[:], in0=p4[:, 16:32], in1=cs_t[:], op=ALU.subtract), reads=[Rp4, Rcs], writes=[Rdst_, Rp4])
                            P.op("act", lambda e: e.activation(out=cdec_t[:], in_=p4[:, 16:32], func=AF.Exp), reads=[Rp4], writes=[Rcdec, Rp4])
                            P.op("act", lambda e: e.copy(out=scT[:], in_=p4[:, 128:384]), reads=[Rp4], writes=[RscT, Rp4])
                            P.op("dve", lambda e: e.tensor_scalar(out=ncs_t[:], in0=cs_t[:], scalar1=-1.0, scalar2=None, op0=ALU.mult), reads=[Rcs], writes=[Rncs])
                            P.op("act", lambda e: e.activation(out=dout_t[:], in_=cs_t[:], func=AF.Exp), reads=[Rcs], writes=[Rdout])
                            P.op("act", lambda e: e.activation(out=dst_t[:], in_=dst_t[:], func=AF.Exp), reads=[Rdst_], writes=[Rdst_])

                            def trb(e):
                                for g in range(2):
                                    ins = e.transpose(psb[:, g * 128:(g + 1) * 128], xb[:, 8 + g, :], ident[:])
                                return ins
                            P.op("pe", trb, reads=[Rxb, Rident], writes=[Rpsb])
                            P.op("dve", lambda e: e.tensor_copy(out=B_tm[:], in_=psb[:, 0:256]), reads=[Rpsb], writes=[RBtm])
                            P.op("dve", lambda e: e.tensor_tensor(
                                out=Xd[:].rearrange("p (h q) -> p h q", q=64), in0=xs_tm[:].rearrange("p (h q) -> p h q", q=64),
                                in1=dt_[:, d_ * 16:(d_ + 1) * 16].unsqueeze(2).to_broadcast([128, 16, 64]), op=ALU.mult), reads=[Rxs, Rdt_], writes=[RXd])
                            P.op("dve", lambda e: e.tensor_tensor(
                                out=Xs[:].rearrange("p (h q) -> p h q", q=64), in0=Xd[:].rearrange("p (h q) -> p h q", q=64),
                                in1=dst_t[:].unsqueeze(2).to_broadcast([128, 16, 64]), op=ALU.mult), reads=[RXd, Rdst_], writes=[RXs])
                            if d_ == 1:
                                P.op("pool", lambda e: e.tensor_tensor(
                                    out=pre[:].rearrange("p (h q) -> p h q", q=64), in0=xs_tm[:].rearrange("p (h q) -> p h q", q=64),
                                    in1=dsum[:].unsqueeze(2).to_broadcast([128, 16, 64]), op=ALU.mult), reads=[Rxs, Rdsum], writes=[Rpre])
                                P.op("pool", lambda e: e.tensor_tensor(out=pre[:], in0=pre[:], in1=yf[:], op=ALU.add), reads=[Rpre, Ryf], writes=[Rpre])
                            P.op("pool", lambda e: e.tensor_tensor(
                                out=segr[:], in0=tri_d[:].unsqueeze(1).to_broadcast([128, 16, 128]), in1=a_t[:].unsqueeze(2).to_broadcast([128, 16, 128]), op=ALU.mult),
                                reads=[Rtri_d, Ra], writes=[Rsegr])
                            for g in range(2):
                                pst_, Rpst = PS[5 + g]
                                P.op("pe", lambda e, pst_=pst_, g=g: e.matmul(pst_[:, :], lhsT=B_tm[:, g * 128:(g + 1) * 128], rhs=Xs[:, g * 512:(g + 1) * 512], start=True, stop=True),
                                     reads=[RBtm, RXs], writes=[Rpst])
                                P.op("act", lambda e, pst_=pst_, g=g: e.copy(out=stsb[:, g * 512:(g + 1) * 512], in_=pst_[:, :]), reads=[Rpst], writes=[Rstsb, Rpst])

                        def mid():
                            for q4 in range(4):
                                pq, Rpq = PS[q4 % 2]

                                def mseg(e, pq=pq, q4=q4):
                                    e.matmul(pq[:, :], lhsT=onesf[:], rhs=segr[:, q4 * 4:(q4 + 1) * 4, :].rearrange("p h l -> p (h l)"), start=True, stop=False)
                                    return e.matmul(pq[:, :], lhsT=identf[:], rhs=neg_d[:], start=False, stop=True)
                                P.op("pe", mseg, reads=[Rsegr, Ronesf, Ridentf, Rneg_d], writes=[Rpq])

                                def lexp(e, pq=pq, q4=q4):
                                    for hh in range(4):
                                        h_ = q4 * 4 + hh
                                        ins = e.activation(out=Lm[:, h_, :], in_=pq[:, hh * 128:(hh + 1) * 128], func=AF.Exp, bias=ncs_t[:, h_:h_ + 1], scale=1.0)
                                    return ins
                                P.op("act", lexp, reads=[Rpq, Rncs], writes=[RLm])
                            for g in range(2):
                                P.op("dve", lambda e, g=g: e.tensor_tensor(
                                    out=MT[:, g * 8:(g + 1) * 8, :], in0=Lm[:, g * 8:(g + 1) * 8, :],
                                    in1=scT[:, g * 128:(g + 1) * 128].unsqueeze(1).to_broadcast([128, 8, 128]), op=ALU.mult), reads=[RLm, RscT], writes=[RMT])
                            for g in range(2):
                                pyd, Rpyd = PS[2 + g]

                                def myd(e, pyd=pyd, g=g):
                                    for hh in range(8):
                                        h_ = g * 8 + hh
                                        ins = e.matmul(pyd[:, hh * 64:(hh + 1) * 64], lhsT=MT[:, h_, :], rhs=Xd[:, h_ * 64:(h_ + 1) * 64], start=True, stop=True)
                                    return ins
                                P.op("pe", myd, reads=[RMT, RXd], writes=[Rpyd])

                        def late():
                            if first:
                                P.op("pool", lambda e: e.memset(hst[:], 0.0), writes=[Rhst])
                                P.op("pool", lambda e: e.memset(hbf[:], 0.0), writes=[Rhbf])
                            for g in range(2):
                                pyo, Rpyo = PS[5 + g]
                                P.op("pe", lambda e, pyo=pyo, g=g: e.matmul(pyo[:, :], lhsT=xb[:, 10 + g, :], rhs=hbf[:, g * 512:(g + 1) * 512], start=True, stop=True),
                                     reads=[Rxb, Rhbf], writes=[Rpyo])
                            for g in range(2):
                                pyo, Rpyo = PS[5 + g]
                                P.op("dve", lambda e, pyo=pyo, g=g: e.tensor_tensor(
                                    out=yacc[:, g * 512:(g + 1) * 512].rearrange("p (h q) -> p h q", q=64), in0=pyo[:, :].rearrange("p (h q) -> p h q", q=64),
                                    in1=dout_t[:, g * 8:(g + 1) * 8].unsqueeze(2).to_broadcast([128, 8, 64]), op=ALU.mult), reads=[Rpyo, Rdout], writes=[Ryacc, Rpyo])
                                P.op("pool", lambda e, g=g: e.tensor_tensor(
                                    out=hst[:, g * 512:(g + 1) * 512].rearrange("p (h q) -> p h q", q=64), in0=hst[:, g * 512:(g + 1) * 512].rearrange("p (h q) -> p h q", q=64),
                                    in1=cdec_t[:, g * 8:(g + 1) * 8].unsqueeze(2).to_broadcast([128, 8, 64]), op=ALU.mult), reads=[Rhst, Rcdec], writes=[Rhst])
                                P.op("pool", lambda e, g=g: e.tensor_tensor(out=hst[:, g * 512:(g + 1) * 512], in0=stsb[:, g * 512:(g + 1) * 512], in1=hst[:, g * 512:(g + 1) * 512], op=ALU.add),
                                     reads=[Rstsb, Rhst], writes=[Rhst])
                            P.op("act", lambda e: e.copy(out=hbf[:], in_=hst[:]), reads=[Rhst], writes=[Rhbf])
                            for g in range(2):
                                pyd, Rpyd = PS[2 + g]
                                P.op("dve", lambda e, pyd=pyd, g=g: e.tensor_tensor(out=yacc[:, g * 512:(g + 1) * 512], in0=pyd[:, :], in1=yacc[:, g * 512:(g + 1) * 512], op=ALU.add),
                                     reads=[Rpyd, Ryacc], writes=[Ryacc, Rpyd])
                            if d_ == 0:
                                P.dma("sp", lambda e: e.dma_start(out=YF[c], in_=yacc[:]), reads=[Ryacc], writes=[RYF[c]])
                            else:
                                pass

                        def late2():
                            if d_ == 0:
                                return
                            if True:
                                P.op("pool", lambda e: e.tensor_tensor(out=yacc[:], in0=yacc[:], in1=pre[:], op=ALU.add), reads=[Ryacc, Rpre], writes=[Ryacc])
                                P.op("dve", lambda e: e.tensor_tensor(out=yacc[:], in0=yacc[:], in1=zt[:], op=ALU.mult), reads=[Ryacc, Rzt], writes=[Ryacc])
                                for g in range(2):
                                    P.op("act", lambda e, g=g: e.activation(out=ytmp[:, g * 512:(g + 1) * 512], in_=yacc[:, g * 512:(g + 1) * 512], func=AF.Square,
                                                                            accum_out=gsum[:, g:g + 1]), reads=[Ryacc], writes=[Rytmp, Rgsum])

                        def late3():
                            if d_ == 0:
                                return
                            if True:
                                P.op("dve", lambda e: e.tensor_scalar(out=gsum[:], in0=gsum[:], scalar1=1.0 / 512, scalar2=EPS, op0=ALU.mult, op1=ALU.add), reads=[Rgsum], writes=[Rgsum])
                                P.op("act", lambda e: e.activation(out=gsum[:], in_=gsum[:], func=AF.Ln), reads=[Rgsum], writes=[Rgsum])
                                P.op("act", lambda e: e.activation(out=gsum[:], in_=gsum[:], func=AF.Exp, scale=-0.5), reads=[Rgsum], writes=[Rgsum])
                                for g in range(2):
                                    P.op("dve", lambda e, g=g: e.scalar_tensor_tensor(
                                        out=y3[:, g * 512:(g + 1) * 512], in0=yacc[:, g * 512:(g + 1) * 512], scalar=gsum[:, g:g + 1], in1=snw[:, g * 512:(g + 1) * 512],
                                        op0=ALU.mult, op1=ALU.mult), reads=[Ryacc, Rgsum, Rsnw], writes=[Ry3])

                        def fin():
                            if d_ == 0:
                                return

                            def try_(e):
                                for k in range(8):
                                    ins = e.transpose(psb[:, k * 128:(k + 1) * 128], y3[:, k * 128:(k + 1) * 128], ident[:])
                                return ins
                            P.op("pe", try_, reads=[Ry3, Rident], writes=[Rpsb])
                            P.op("act", lambda e: e.copy(out=yT[:].rearrange("p k t -> p (k t)"), in_=psb[:, :]), reads=[Rpsb], writes=[RyT])
                            P.dma("sp", lambda e: e.dma_start(out=Y_ssd[:, tok0:tok0 + 128].rearrange("(k p) t -> p k t", p=128), in_=yT[:]), reads=[RyT])
                        return [ld, noop, early, mid, late, late2, late3, fin]
                    NSTB = 8
                    liveb = {}
                    npass = len(passes)
                    for t_ in range(npass + NSTB - 1):
                        if t_ < npass:
                            liveb[t_] = mk_pass(t_)
                        for k in range(NSTB - 1, -1, -1):
                            u = t_ - k
                            if 0 <= u < npass:
                                liveb[u][k]()
                        liveb.pop(t_ - (NSTB - 1), None)
                        if bg_on:
                            bg_cvt(2)
                    bg["bufs"] = None
                    P.barrier()
                    P.emit("phB")
                if b == 0 and l == layers[0]:
                    dbg_dump(P, "Y_ssd", Y_ssd, [D, T], BF16)
                if "stopB" in dbg:
                    continue
                P.barrier()
                with ExitStack() as s3:
                    S = TB(nc, s3)

                    def pst(shape, dt, nm):
                        TB.gid += 1
                        return (s3.enter_context(nc.psum_tensor("%s_%d" % (nm, TB.gid), shape, dt)), Res(nm))
                    NPX = 2
                    pxs = [pst([128, 512], F32, "px") for _ in range(NPX)]
                    pys = [pst([128, 512], F32, "py") for _ in range(NPX)]
                    ptrs = [pst([128, 1024], BF16, "ptr") for _ in range(2)]
                    pos_ = [pst([128, 512], F32, "po") for _ in range(2)]
                    Vev, RVev = S([128, 18, D], BF16, "Vev")
                    ynat, Rynat = S([128, 18, D], BF16, "ynat")
                    qTs = [S([64, T], BF16, "qT") for _ in range(2)]
                    kTs = [S([64, T], BF16, "kT") for _ in range(2)]
                    TTs = [S([128, 15, 64], F32, "TTh") for _ in range(2)]
                    tbls = [S([128, 5, 832], F32, "tbl") for _ in range(2)]
                    NBUF = 4
                    sls = [S([128, 832], F32, "sl") for _ in range(NBUF)]
                    pbs = [S([128, 832], BF16, "pb") for _ in range(NBUF)]
                    pTs = [S([128, 7, 128], BF16, "pT") for _ in range(NBUF)]
                    sms = [S([128, 4], F32, "sm") for _ in range(NBUF)]
                    units = []
                    for h_ in range(16):
                        for r in range(0, 32, 2):
                            units.append((h_, "lat", r))
                        if not last:
                            units += [(h_, "ctx", 0), (h_, "ctx", 1)]
                    nun = len(units)
                    CASE = {0: 1, 2: 2, 28: 3, 30: 4}

                    def head_load(h_):
                        qT, RqT = qTs[h_ % 2]
                        kT, RkT = kTs[h_ % 2]
                        TTh, RTTh = TTs[h_ % 2]
                        tbl, Rtbl = tbls[h_ % 2]
                        hc0 = h_ * 64
                        P.dma("sp", lambda e: e.dma_start(out=qT[:], in_=S_q[hc0:hc0 + 64, :]), writes=[RqT])
                        P.dma("sp", lambda e: e.dma_start(out=kT[:], in_=S_k[hc0:hc0 + 64, :]), writes=[RkT])
                        P.dma("sp", lambda e: e.dma_start(out=TTh[0:64], in_=TT_d[l, :, h_, :, :]), writes=[RTTh])
                        P.dma("sp", lambda e: e.dma_start(out=TTh[64:128], in_=TT_d[l, :, h_, :, :]), writes=[RTTh])
                        P.op("pool", lambda e: e.memset(tbl[:, :, 0:256], 0.0), writes=[Rtbl])
                        P.op("pool", lambda e: e.memset(tbl[:, :, 256:832], NEG), writes=[Rtbl])
                        for rr, cs_ in ((4, 0), (0, 1), (2, 2), (28, 3), (30, 4)):
                            Rb = min(max(rr - 4, 0), 24)
                            for hf in range(2):
                                row = rr + hf
                                r0row = min(max(row - 4, 0), 24)
                                kr0 = r0row - Rb
                                drs = r0row - row + 7
                                P.op("pool", lambda e, hf=hf, cs_=cs_, kr0=kr0, drs=drs: e.tensor_copy(
                                    out=tbl[hf * 64:(hf + 1) * 64, cs_, 256 + kr0 * 64:256 + kr0 * 64 + 512],
                                    in_=TTh[hf * 64:(hf + 1) * 64, drs:drs + 8, :].rearrange("p a b -> p (a b)")), reads=[RTTh], writes=[Rtbl])

                    def mk_unit(ui):
                        h_, kind, r = units[ui]
                        qT, RqT = qTs[h_ % 2]
                        kT, RkT = kTs[h_ % 2]
                        tbl, Rtbl = tbls[h_ % 2]
                        hc0 = h_ * 64
                        sl, Rsl = sls[ui % NBUF]
                        pb_, Rpb_ = pbs[ui % NBUF]
                        pT, RpT = pTs[ui % NBUF]
                        sm, Rsm = sms[ui % NBUF]
                        px, Rpx = pxs[ui % NPX]
                        py, Rpy = pys[ui % NPX]
                        ptr, Rptr = ptrs[ui % 2]
                        pot, Rpo = pos_[ui % 2]
                        po = pot[:, 0:64]
                        if kind == "lat":
                            Rb = min(max(r - 4, 0), 24)
                            cs_ = CASE.get(r, 0)
                            q0, nk, tix = r * 64, 832, r // 2
                            full9 = (Rb + 8 <= 31)
                            vl = [Vev[:, 16, hc0:hc0 + 64], Vev[:, 17, hc0:hc0 + 64]]
                            for j in range(4):
                                vl.append(Vev[:, Rb // 2 + j, hc0:hc0 + 64])
                            vl.append(Vev[0:64, (Rb + 8) // 2 if full9 else 0, hc0:hc0 + 64])
                        else:
                            q0, nk, tix = L + r * 128, 256, 16 + r
                            vl = [Vev[:, 16, hc0:hc0 + 64], Vev[:, 17, hc0:hc0 + 64]]
                        nch = (nk + 127) // 128

                        def st0():
                            if kind == "lat" and r == 0:
                                if h_ == 0:
                                    head_load(0)
                                    P.dma("sp", lambda e: e.dma_start(out=Vev[:], in_=S_v[:, :].rearrange("(i p) d -> p i d", p=128)), writes=[RVev])
                                if h_ + 1 < 16:
                                    head_load(h_ + 1)

                            def msc(e):
                                ins = e.matmul(px[:, 0:256], lhsT=qT[:, q0:q0 + 128], rhs=kT[:, L:T], start=True, stop=True)
                                if kind == "lat":
                                    ins = e.matmul(px[:, 256:512], lhsT=qT[:, q0:q0 + 128], rhs=kT[:, Rb * 64:Rb * 64 + 256], start=True, stop=True)
                                return ins
                            P.op("pe", msc, reads=[RqT, RkT], writes=[Rpx])
                            if kind == "lat":
                                def msc2(e):
                                    if full9:
                                        return e.matmul(py[:, 0:320], lhsT=qT[:, q0:q0 + 128], rhs=kT[:, Rb * 64 + 256:Rb * 64 + 576], start=True, stop=True)
                                    e.matmul(py[:, 0:256], lhsT=qT[:, q0:q0 + 128], rhs=kT[:, Rb * 64 + 256:Rb * 64 + 512], start=True, stop=True)
                                    return e.matmul(py[:, 256:320], lhsT=qT[:, q0:q0 + 128], rhs=kT[:, 0:64], start=True, stop=True)
                                P.op("pe", msc2, reads=[RqT, RkT], writes=[Rpy])

                        def st1():
                            if kind == "lat":
                                P.op("dve", lambda e: e.tensor_tensor(out=sl[:, 0:512], in0=px[:, 0:512], in1=tbl[:, cs_, 0:512], op=ALU.add),
                                     reads=[Rpx, Rtbl], writes=[Rsl])
                                P.op("dve", lambda e: e.tensor_tensor(out=sl[:, 512:832], in0=py[:, 0:320], in1=tbl[:, cs_, 512:832], op=ALU.add),
                                     reads=[Rpy, Rtbl], writes=[Rsl])
                            else:
                                P.op("dve", lambda e: e.tensor_copy(out=sl[:, 0:256], in_=px[:, 0:256]), reads=[Rpx], writes=[Rsl])
                            P.op("dve", lambda e: e.reduce_max(out=sm[:, 0:1], in_=sl[:, 0:nk], axis=AX.X), reads=[Rsl], writes=[Rsm])
                            P.op("dve", lambda e: e.tensor_scalar(out=sm[:, 1:2], in0=sm[:, 0:1], scalar1=-1.0, scalar2=None, op0=ALU.mult), reads=[Rsm], writes=[Rsm])

                        def st2():
                            P.op("act", lambda e: e.activation(
                                out=pb_[:, 0:nk], in_=sl[:, 0:nk], func=AF.Exp, bias=sm[:, 1:2], scale=1.0, accum_out=sm[:, 2:3]), reads=[Rsl, Rsm], writes=[Rpb_, Rsm])

                        def st3():
                            def trp(e):
                                for j in range(nch):
                                    w_ = min(128, nk - j * 128)
                                    ins = e.transpose(ptr[0:w_, j * 128:(j + 1) * 128], pb_[:, j * 128:j * 128 + w_], ident[:])
                                return ins
                            P.op("pe", trp, reads=[Rpb_, Rident], writes=[Rptr])

                        def st4():
                            def cp(e):
                                nfull = nk // 128
                                ins = e.copy(out=pT[:, 0:nfull, :], in_=ptr[:, 0:nfull * 128].rearrange("p (j q) -> p j q", q=128))
                                if nk % 128:
                                    ins = e.copy(out=pT[0:64, nfull, :], in_=ptr[0:64, nfull * 128:(nfull + 1) * 128])
                                return ins
                            P.op("act", cp, reads=[Rptr], writes=[RpT])

                        def st5():
                            def mpv(e):
                                for j, v in enumerate(vl):
                                    kk = 64 if (kind == "lat" and j == 6) else 128
                                    ins = e.matmul(po, lhsT=pT[0:kk, j, :], rhs=v, start=(j == 0), stop=(j == len(vl) - 1))
                                return ins
                            P.op("pe", mpv, reads=[RpT, RVev], writes=[Rpo])

                        def st6():
                            P.op("dve", lambda e: e.reciprocal(out=sm[:, 3:4], in_=sm[:, 2:3]), reads=[Rsm], writes=[Rsm])
                            P.op("dve", lambda e: e.tensor_scalar(out=ynat[:, tix, hc0:hc0 + 64], in0=po, scalar1=sm[:, 3:4], scalar2=None, op0=ALU.mult),
                                 reads=[Rpo, Rsm], writes=[Rynat])
                        return [st0, st1, st2, st3, st4, st5, st6]
                    NST = 7
                    live = {}
                    bg_on_c = (b == 0 and l == layers[0] and bg["i"] < len(deferred))
                    if bg_on_c:
                        bg_alloc(S)
                    for t_ in range(nun + NST - 1):
                        if t_ < nun:
                            live[t_] = mk_unit(t_)
                        for k in range(NST - 1, -1, -1):
                            u = t_ - k
                            if 0 <= u < nun:
                                live[u][k]()
                        live.pop(t_ - (NST - 1), None)
                        if bg_on_c and t_ % 3 == 0:
                            bg_cvt(1)
                    if bg_on_c:
                        bg_cvt(len(deferred))
                        bg["bufs"] = None
                    ntile = 18 if not last else 16
                    yTb = [S([128, 8, 128], BF16, "yTb") for _ in range(2)]
                    for ti in range(ntile):
                        ptr, Rptr = ptrs[ti % 2]
                        yT_, RyT_ = yTb[ti % 2]

                        def try_(e, ptr=ptr, ti=ti):
                            for k in range(8):
                                ins = e.transpose(ptr[:, k * 128:(k + 1) * 128], ynat[:, ti, k * 128:(k + 1) * 128], ident[:])
                            return ins
                        P.op("pe", try_, reads=[Rynat, Rident], writes=[Rptr])
                        P.op("act", lambda e, ptr=ptr, yT_=yT_: e.copy(out=yT_[:].rearrange("p k t -> p (k t)"), in_=ptr[:, :]), reads=[Rptr], writes=[RyT_])
                        P.dma("sp", lambda e, yT_=yT_, ti=ti: e.dma_start(out=Y_na[:, ti * 128:(ti + 1) * 128].rearrange("(k p) t -> p k t", p=128), in_=yT_[:]), reads=[RyT_])
                    P.barrier()
                    P.emit("phC")
                if b == 0 and l == layers[0]:
                    dbg_dump(P, "Y_na", Y_na, [D, T], BF16)
                if "stopC" in dbg:
                    continue
                tiles_d = TOK_TILES if not last else TOK_TILES[:4]
                P.barrier()
                with ExitStack() as s4:
                    S = TB(nc, s4)
                    PS, PSB = psum_alloc(s4, 4, 0)
                    wbr = []
                    for wi in range(4):
                        wt, Rwt = S([128, 8, D], BF16, "wbr")
                        wbr.append((wt, Rwt))

                    def emit_d1_weights():
                        for wi in range(4):
                            wt, Rwt = wbr[wi]
                            P.dma("sp", lambda e, wt=wt, wi=wi: e.dma_start(out=wt[:], in_=Wb_br[l, wi].rearrange("(k p) c -> p k c", p=128)), writes=[Rwt])
                    NT1 = 256
                    NBF1 = 2
                    yss = [[S([128, 8, NT1], BF16, "ys") for _ in range(3)] for _ in range(NBF1)]
                    gts = [S([128, 24, NT1], BF16, "gt") for _ in range(NBF1)]
                    hhs1 = [S([128, 8, NT1], F32, "hh") for _ in range(NBF1 + 1)]
                    mg, Rmg = S([128, 8, NT1], F32, "mg")
                    mb, Rmb = S([128, 8, NT1], BF16, "mb")
                    tmps = [S([128, NT1], F32, "tmpd") for _ in range(2)]
                    ntok1 = T if not last else L
                    tl1 = list(range(0, ntok1, NT1))

                    def d1_load(ti1):
                        t0 = tl1[ti1]
                        n = NT1
                        for bi, Ysrc in enumerate((Y_conv, Y_ssd, Y_na)):
                            yt_, Ryt_ = yss[ti1 % NBF1][bi]
                            P.dma("sp", lambda e, yt_=yt_, Ysrc=Ysrc: e.dma_start(out=yt_[:, :, 0:n], in_=fm(Ysrc, t0, n)), writes=[Ryt_])
                        gt_, Rgt_ = gts[ti1 % NBF1]
                        hh_, Rhh_ = hhs1[ti1 % (NBF1 + 1)]
                        P.dma("sp", lambda e: e.dma_start(out=gt_[:, :, 0:n], in_=fm(S_gate, t0, n)), writes=[Rgt_])
                        P.dma("sp", lambda e: e.dma_start(out=hh_[:, :, 0:n], in_=fm(src_h, t0, n)), writes=[Rhh_])
                    pi = 0
                    d1_load(0)
                    emit_d1_weights()
                    for ti1, t0 in enumerate(tl1):
                        n = NT1
                        if ti1 + 1 < len(tl1):
                            d1_load(ti1 + 1)
                        col = b if t0 < L else 4
                        gt_, Rgt_ = gts[ti1 % NBF1]
                        hh_, Rhh_ = hhs1[ti1 % (NBF1 + 1)]
                        for bi in range(3):
                            yt_, Ryt_ = yss[ti1 % NBF1][bi]
                            wt, Rwt = wbr[bi]
                            for ob in range(8):
                                ps, Rps = PS[pi % 4]
                                tmpd, Rtmpd = tmps[pi % 2]
                                pi += 1

                                def mm(e, ps=ps, wt=wt, yt_=yt_, ob=ob, n=n):
                                    for k in range(8):
                                        ins = e.matmul(ps[:, 0:n], lhsT=wt[:, k, ob * 128:(ob + 1) * 128], rhs=yt_[:, k, 0:n], start=(k == 0), stop=(k == 7))
                                    return ins
                                P.op("pe", mm, reads=[Rwt, Ryt_], writes=[Rps])
                                if bi == 0:
                                    P.op("dve", lambda e, ps=ps, ob=ob, n=n, gt_=gt_: e.tensor_tensor(out=mg[:, ob, 0:n], in0=ps[:, 0:n], in1=gt_[:, ob, 0:n], op=ALU.mult),
                                         reads=[Rps, Rgt_], writes=[Rmg])
                                else:
                                    P.op("dve", lambda e, ps=ps, ob=ob, n=n, bi=bi, tmpd=tmpd, gt_=gt_: e.tensor_tensor(out=tmpd[:, 0:n], in0=ps[:, 0:n], in1=gt_[:, bi * 8 + ob, 0:n], op=ALU.mult),
                                         reads=[Rps, Rgt_], writes=[Rtmpd])
                                    P.op("pool", lambda e, ob=ob, n=n, tmpd=tmpd: e.tensor_tensor(out=mg[:, ob, 0:n], in0=mg[:, ob, 0:n], in1=tmpd[:, 0:n], op=ALU.add),
                                         reads=[Rmg, Rtmpd], writes=[Rmg])
                        P.op("act", lambda e, n=n: e.copy(out=mb[:, :, 0:n], in_=mg[:, :, 0:n]), reads=[Rmg], writes=[Rmb])
                        wt, Rwt = wbr[3]
                        for ob in range(8):
                            ps, Rps = PS[pi % 4]
                            pi += 1

                            def mm(e, ps=ps, wt=wt, ob=ob, n=n):
                                for k in range(8):
                                    ins = e.matmul(ps[:, 0:n], lhsT=wt[:, k, ob * 128:(ob + 1) * 128], rhs=mb[:, k, 0:n], start=(k == 0), stop=(k == 7))
                                return ins
                            P.op("pe", mm, reads=[Rwt, Rmb], writes=[Rps])
                            P.op("dve", lambda e, ps=ps, ob=ob, n=n, col=col, hh_=hh_: e.scalar_tensor_tensor(
                                out=hh_[:, ob, 0:n], in0=ps[:, 0:n], scalar=mcol(l, 2, ob, col), in1=hh_[:, ob, 0:n], op0=ALU.mult, op1=ALU.add),
                                reads=[Rps, Rhh_, Rmod], writes=[Rhh_])
                        P.dma("sp", lambda e, t0=t0, n=n, hh_=hh_: e.dma_start(out=fm(HT, t0, n), in_=hh_[:, :, 0:n]), reads=[Rhh_])
                    P.barrier()
                    P.emit("phD1")
                if b == 0 and l == layers[0]:
                    dbg_dump(P, "HT1", HT, [D, T], F32)
                if "stopD1" in dbg:
                    continue
                P.barrier()
                with ExitStack() as s5:
                    S = TB(nc, s5)
                    PS, PSB = psum_alloc(s5, 8, 0)
                    W1g = [S([128, 8, 512], BF16, "W1g") for _ in range(8)]
                    W2g = [S([128, 8, D], BF16, "W2g") for _ in range(4)]
                    def emit_ffn_weights():
                        for g8 in range(8):
                            w1, Rw1 = W1g[g8]
                            P.dma("sp", lambda e, w1=w1, g8=g8: e.dma_start(out=w1[:], in_=Wb_f1[l, :, g8 * 512:(g8 + 1) * 512].rearrange("(k p) c -> p k c", p=128)), writes=[Rw1])
                        for q4 in range(4):
                            w2, Rw2 = W2g[q4]
                            P.dma("sp", lambda e, w2=w2, q4=q4: e.dma_start(out=w2[:], in_=Wb_f2[l, q4 * 1024:(q4 + 1) * 1024, :].rearrange("(k p) c -> p k c", p=128)), writes=[Rw2])
                    RW2all = [r_ for (_, r_) in W2g]
                    NT2 = 256
                    hhs = [S([128, 8, NT2], F32, "hh2") for _ in range(3)]
                    sqs = [S([128, 8, NT2], BF16, "sq2") for _ in range(2)]
                    rstds = [S([128, NT2], F32, "rstd2") for _ in range(2)]
                    tmps2 = [S([128, NT2], F32, "tmp2") for _ in range(2)]
                    xTs = [S([128, 8, NT2], BF16, "xT2") for _ in range(2)]
                    aT, RaT = S([128, 32, NT2], BF16, "aT")
                    sqf, Rsqf = S([128, 8, NT2], BF16, "sqf")
                    rl = [S([128, NT2], F32, "rl") for _ in range(2)]
                    pi_ = [0]
                    ntok = T if not last else L
                    tl2 = list(range(0, ntok, NT2))
                    n = NT2

                    def mk_t2(ti2):
                        t0 = tl2[ti2]
                        col = b if t0 < L else 4
                        hh_, Rhh_ = hhs[ti2 % 3]
                        sq, Rsq = sqs[ti2 % 2]
                        rstd, Rrstd = rstds[ti2 % 2]
                        tmp, Rtmp = tmps2[ti2 % 2]
                        xTt, RxTt = xTs[ti2 % 2]
                        psn, Rpsn = PS[6]
                        psf, Rpsf = PS[7]

                        def sX():
                            P.dma("sp", lambda e: e.dma_start(out=hh_[:, :, 0:n], in_=fm(HT, t0, n)), writes=[Rhh_])
                            P.op("act", lambda e: e.activation(out=sq[:, :, 0:n], in_=hh_[:, :, 0:n], func=AF.Square), reads=[Rhh_], writes=[Rsq])

                        def sY():
                            def mm(e):
                                for k in range(8):
                                    ins = e.matmul(psn[:, 0:n], lhsT=onesb[:], rhs=sq[:, k, 0:n], start=(k == 0), stop=(k == 7))
                                return ins
                            P.op("pe", mm, reads=[Rsq, Ronesb], writes=[Rpsn])
                            P.op("act", lambda e: e.activation(out=rstd[:, 0:n], in_=psn[:, 0:n], func=AF.Sqrt, bias=EPS, scale=1.0 / D), reads=[Rpsn], writes=[Rrstd])
                            P.op("dve", lambda e: e.reciprocal(out=rstd[:, 0:n], in_=rstd[:, 0:n]), reads=[Rrstd], writes=[Rrstd])
                            for k in range(8):
                                P.op("dve", lambda e, k=k: e.scalar_tensor_tensor(
                                    out=tmp[:, 0:n], in0=hh_[:, k, 0:n], scalar=A2[:, l, k, col:col + 1], in1=rstd[:, 0:n], op0=ALU.mult, op1=ALU.mult),
                                    reads=[Rhh_, Rrstd, RA2], writes=[Rtmp])
                                P.op("act", lambda e, k=k: e.activation(out=xTt[:, k, 0:n], in_=tmp[:, 0:n], func=AF.Identity, bias=mcol(l, 3, k, col), scale=1.0),
                                     reads=[Rtmp, Rmod], writes=[RxTt])

                        def sZ1():
                            for cb in range(32):
                                w1, Rw1 = W1g[cb // 4]
                                c4 = cb % 4
                                ps, Rps = PS[pi_[0] % 4]
                                r_, Rr_ = rl[pi_[0] % 2]
                                pi_[0] += 1

                                def mm(e, ps=ps, w1=w1, c4=c4):
                                    for k in range(8):
                                        ins = e.matmul(ps[:, 0:n], lhsT=w1[:, k, c4 * 128:(c4 + 1) * 128], rhs=xTt[:, k, 0:n], start=(k == 0), stop=(k == 7))
                                    return ins
                                P.op("pe", mm, reads=[Rw1, RxTt], writes=[Rps])
                                P.op("act", lambda e, ps=ps, r_=r_: e.activation(out=r_[:, 0:n], in_=ps[:, 0:n], func=AF.Relu), reads=[Rps], writes=[Rr_])
                                P.op("pool", lambda e, r_=r_, cb=cb: e.tensor_tensor(out=aT[:, cb, 0:n], in0=r_[:, 0:n], in1=r_[:, 0:n], op=ALU.mult), reads=[Rr_], writes=[RaT])

                        def sZ2():
                            for ob in range(8):
                                ps, Rps = PS[4 + (pi_[0] % 2)]
                                pi_[0] += 1

                                def mm(e, ps=ps, ob=ob):
                                    for k in range(32):
                                        ins = e.matmul(ps[:, 0:n], lhsT=W2g[k // 8][0][:, k % 8, ob * 128:(ob + 1) * 128], rhs=aT[:, k, 0:n], start=(k == 0), stop=(k == 31))
                                    return ins
                                P.op("pe", mm, reads=RW2all + [RaT], writes=[Rps])
                                P.op("dve", lambda e, ps=ps, ob=ob: e.scalar_tensor_tensor(
                                    out=hh_[:, ob, 0:n], in0=ps[:, 0:n], scalar=mcol(l, 5, ob, col), in1=hh_[:, ob, 0:n], op0=ALU.mult, op1=ALU.add),
                                    reads=[Rps, Rhh_, Rmod], writes=[Rhh_])
                            if not last:
                                P.dma("sp", lambda e: e.dma_start(out=fm(HT, t0, n), in_=hh_[:, :, 0:n]), reads=[Rhh_])
                            else:
                                P.op("act", lambda e: e.activation(out=sqf[:, :, 0:n], in_=hh_[:, :, 0:n], func=AF.Square), reads=[Rhh_], writes=[Rsqf])

                        def sW():
                            if not last:
                                return

                            def mm(e):
                                for k in range(8):
                                    ins = e.matmul(psf[:, 0:n], lhsT=onesb[:], rhs=sqf[:, k, 0:n], start=(k == 0), stop=(k == 7))
                                return ins
                            P.op("pe", mm, reads=[Rsqf, Ronesb], writes=[Rpsf])
                            P.op("act", lambda e: e.activation(out=rstd[:, 0:n], in_=psf[:, 0:n], func=AF.Sqrt, bias=EPS, scale=1.0 / D), reads=[Rpsf], writes=[Rrstd])
                            P.op("dve", lambda e: e.reciprocal(out=rstd[:, 0:n], in_=rstd[:, 0:n]), reads=[Rrstd], writes=[Rrstd])
                            for k in range(8):
                                P.op("dve", lambda e, k=k: e.scalar_tensor_tensor(
                                    out=hh_[:, k, 0:n], in0=hh_[:, k, 0:n], scalar=fnw[:, k:k + 1], in1=rstd[:, 0:n], op0=ALU.mult, op1=ALU.mult),
                                    reads=[Rhh_, Rrstd, Rfnw], writes=[Rhh_])
                            P.dma("sp", lambda e: e.dma_start(out=fm(outT[b], t0, n), in_=hh_[:, :, 0:n]), reads=[Rhh_])
                        return dict(X=sX, Y=sY, Z1=sZ1, Z2=sZ2, W=sW)
                    nt2 = len(tl2)
                    st2 = {}
                    for t_ in range(-2, nt2 + 1):
                        for (nm_, off_) in (("W", -1), ("Y", 1), ("Z1", 0), ("X", 2), ("Z2", 0)):
                            u = t_ + off_
                            if 0 <= u < nt2:
                                if u not in st2:
                                    st2[u] = mk_t2(u)
                                st2[u][nm_]()
                        if t_ == -1:
                            emit_ffn_weights()
                    P.barrier()
                    P.emit("phD2")
                if b == 0 and l == layers[0]:
                    dbg_dump(P, "HT2", HT, [D, T], F32)
        P.barrier()
        P.emit()
    return nc, dbg_out


def _consts():
    k = np.arange(128)
    ident = np.eye(128, dtype=np.float32)
    triu = (k[:, None] <= k[None, :]).astype(np.float32)
    tril = (k[:, None] >= k[None, :]).astype(np.float32)
    negf1 = np.where(k[None, :] < k[:, None], NEG, 0.0).astype(np.float32)
    negb1 = np.where(k[None, :] > k[:, None], NEG, 0.0).astype(np.float32)
    negf = np.tile(negf1, (1, 4))
    negb = np.tile(negb1, (1, 4))
    t = np.arange(L, dtype=np.int32)
    row = (t // 64).astype(np.float32)
    col = (t % 64).astype(np.float32)
    half = 32
    inv = (np.float32(10000.0) ** (-np.arange(0, half, 2, dtype=np.float32) / np.float32(half))).astype(np.float32)
    ang_r = row[:, None] * inv
    ang_c = col[:, None] * inv
    ang = np.concatenate([ang_r, ang_r, ang_c, ang_c], axis=-1)
    cos = np.cos(ang).astype(np.float32).T
    sin = np.sin(ang).astype(np.float32).T
    cos = np.concatenate([cos, cos], axis=0)
    sin = np.concatenate([sin, sin], axis=0)
    rot = np.zeros((128, 128), np.float32)
    for m in range(128):
        if (m % 32) < 16:
            rot[m + 16, m] = -1.0
        else:
            rot[m - 16, m] = 1.0
    return dict(c_ident=ident, c_triu=triu, c_tril=tril, c_negf=negf, c_negb=negb,
                c_cos=np.ascontiguousarray(cos), c_sin=np.ascontiguousarray(sin), c_rot=rot)


def _shared_inputs(inp):
    f = lambda a: np.ascontiguousarray(np.asarray(a, dtype=np.float32))
    d = {}
    for k_ in ("w_ada", "w_in", "w_br_conv", "w_br_ssd", "w_br_na", "w_out", "w_ff1", "w_ff2"):
        d[k_] = f(inp[k_])
    d["b_ada_p"] = f(np.asarray(inp["b_ada"]).reshape(DEPTH, 48, 128).transpose(0, 2, 1))
    d["norm1_p"] = f(np.asarray(inp["norm1_w"]).reshape(DEPTH, 8, 128).transpose(0, 2, 1))
    d["norm2_p"] = f(np.asarray(inp["norm2_w"]).reshape(DEPTH, 8, 128).transpose(0, 2, 1))
    d["fnorm_p"] = f(np.asarray(inp["final_norm_w"]).reshape(8, 128).T)
    d["convw_p"] = f(np.asarray(inp["conv_mix_w"]).reshape(DEPTH, 3, 8, 128).transpose(0, 3, 2, 1))
    d["sconvw_p"] = f(np.asarray(inp["ssd_conv_w"]).reshape(DEPTH, 3, 12, 128).transpose(0, 3, 2, 1))
    d["sconvb_p"] = f(np.asarray(inp["ssd_conv_b"]).reshape(DEPTH, 12, 128).transpose(0, 2, 1))
    d["dt_bias"] = f(np.asarray(inp["ssd_dt_bias"]).reshape(DEPTH, 32))
    d["a_log"] = f(np.asarray(inp["ssd_a_log"]).reshape(DEPTH, 32))
    d["ssd_d"] = f(np.asarray(inp["ssd_d"]).reshape(DEPTH, 32))
    d["ssd_norm_w"] = f(inp["ssd_norm_w"])
    colv = np.arange(64)
    col_start = np.clip(colv - 8, 0, 48)
    col_ok = (colv[None, :] >= col_start[:, None]) & (colv[None, :] < col_start[:, None] + 16)
    dc_idx = np.clip(colv[None, :] - colv[:, None], -15, 15) + 15
    rpb = np.asarray(inp["na_rpb"], dtype=np.float32)
    g = rpb[:, :, :, dc_idx]
    g = np.where(col_ok[None, None, None], g, np.float32(NEG))
    d["TT"] = f(g.transpose(0, 3, 1, 2, 4))
    d.update(_consts())
    return d


def _core_inputs(inp, shared, b0, nb):
    x = np.asarray(inp["x"], dtype=np.float32)
    ctx = np.asarray(inp["ctx"], dtype=np.float32)
    c = np.asarray(inp["c"], dtype=np.float32)
    c_ctx = np.asarray(inp["c_ctx"], dtype=np.float32)
    xT = np.concatenate([x[b0:b0 + nb].transpose(0, 2, 1), ctx[b0:b0 + nb].transpose(0, 2, 1)], axis=2)
    cm = np.zeros((5, D), np.float32)
    cm[0:nb] = c[b0:b0 + nb]
    cm[4] = c_ctx
    cT = cm.T.reshape(8, 128, 5).transpose(1, 0, 2)
    m = dict(shared)
    m["xT"] = np.ascontiguousarray(xT)
    m["cT"] = np.ascontiguousarray(cT)
    return m


_CACHE = {}


def kernel(**inputs):
    if "nc" not in _CACHE:
        _CACHE["nc"] = build()[0]
    nc = _CACHE["nc"]
    shared = _shared_inputs(inputs)
    in_maps = [_core_inputs(inputs, shared, c * NB, NB) for c in range(NCORES)]
    res = run_bass_kernel_spmd(nc, in_maps, core_ids=list(range(NCORES)))
    outs = [np.asarray(r["outT"]).transpose(0, 2, 1) for r in res.results]
    return np.ascontiguousarray(np.concatenate(outs, axis=0).astype(np.float32))
```

```python
import numpy as np
import concourse.bass as bass
import concourse.mybir as mybir
from concourse.bass_utils import run_bass_kernel_spmd
from contextlib import ExitStack

F32 = mybir.dt.float32
BF16 = mybir.dt.bfloat16
AF = mybir.ActivationFunctionType
ALU = mybir.AluOpType
AX = mybir.AxisListType

NCORES = 8
NB = 4
D = 1024
L = 2048
CT = 256
T = L + CT
DEPTH = 2
INC = 11808
DFF = 4096
EPS = 1e-6
NEG = -30000.0
_DEFER_CVT = True
O_CB, O_CC, O_CX, O_Z, O_XBC, O_DT, O_Q, O_K, O_V, O_G = 0, 1024, 2048, 3072, 4096, 5632, 5664, 6688, 7712, 8736


class Res:
    __slots__ = ("name", "w", "r")

    def __init__(self, name=""):
        self.name = name
        self.w = None
        self.r = {}


class Prog:
    ENG = ("pe", "act", "dve", "pool", "sp")
    NLANE = {"sp": 8, "act": 2, "pool": 6}

    use_scopes = False

    def __init__(self, nc, es):
        self.nc = nc
        self.q = {e: [] for e in self.ENG}
        self.sem = {}
        self.cnt = {}
        for e in self.ENG:
            self.sem[e] = es.enter_context(nc.semaphore("s_" + e))
            self.cnt[e] = 0
        self.lane_rr = {}
        for e, n in self.NLANE.items():
            self.lane_rr[e] = 0
            for i in range(n):
                k = "%s_l%d" % (e, i)
                self.sem[k] = es.enter_context(nc.semaphore("s_" + k))
                self.cnt[k] = 0
        self.seen = {e: {} for e in self.ENG}

    def _deps(self, e, reads, writes):
        deps = {}
        for r in reads:
            if r.w is not None:
                o, i = r.w
                if deps.get(o, 0) < i:
                    deps[o] = i
        for w in writes:
            if w.w is not None:
                o, i = w.w
                if deps.get(o, 0) < i:
                    deps[o] = i
            for o, i in w.r.items():
                if deps.get(o, 0) < i:
                    deps[o] = i
        waits = []
        seen = self.seen[e]
        for o, i in deps.items():
            if seen.get(o, 0) < i:
                seen[o] = i
                waits.append((o, i))
        return waits

    def op(self, e, fn, reads=(), writes=()):
        waits = self._deps(e, reads, writes)
        self.cnt[e] += 1
        idx = self.cnt[e]
        self.q[e].append((fn, waits, e, 1))
        for r in reads:
            r.r[e] = idx
        for w in writes:
            w.w = (e, idx)
            w.r = {}

    def dma(self, e, fn, reads=(), writes=()):
        n = self.NLANE[e]
        li = self.lane_rr[e]
        self.lane_rr[e] = (li + 1) % n
        k = "%s_l%d" % (e, li)
        waits = self._deps(e, reads, writes)
        prev = self.cnt[k]
        if prev > 0 and self.seen[e].get(k, 0) < prev:
            self.seen[e][k] = prev
            waits.append((k, prev))
        self.cnt[k] += 16
        val = self.cnt[k]
        self.q[e].append((fn, waits, k, 16))
        for r in reads:
            r.r[k] = val
        for w in writes:
            w.w = (k, val)
            w.r = {}

    def barrier(self):
        for e in self.ENG:
            waits = []
            for k, c in self.cnt.items():
                if k != e and c > 0 and self.seen[e].get(k, 0) < c:
                    self.seen[e][k] = c
                    waits.append((k, c))
            if waits:
                self.q[e].append((None, waits, None, 0))

    def emit(self, scope=None):
        nc = self.nc
        engs = {"pe": "tensor", "act": "scalar", "dve": "vector", "pool": "gpsimd", "sp": "sync"}
        with ExitStack() as _es:
            if scope is not None and self.use_scopes:
                _es.enter_context(nc.named_scope(scope))
            block = _es.enter_context(nc.Block())
            for e in self.ENG:
                def body(eng, e=e):
                    for fn, waits, k, inc in self.q[e]:
                        for o, i in waits:
                            eng.wait_ge(self.sem[o], i)
                        if fn is not None:
                            ins = fn(eng)
                            ins.then_inc(self.sem[k], inc)
                getattr(block, engs[e])(body)
        self.q = {e: [] for e in self.ENG}


class TB:
    gid = 0

    def __init__(self, nc, es):
        self.nc = nc
        self.es = es
        self.n = 0

    def __call__(self, shape, dt=F32, name=None):
        self.n += 1
        TB.gid += 1
        t = self.es.enter_context(self.nc.sbuf_tensor("%s_%d" % (name or "t", TB.gid), list(shape), dt))
        return t, Res(name or "t")


def build(nb=NB, layers=(0, 1), dbg=()):
    nc = bass.Bass("TRN2", target_bir_lowering=False)

    def din(name, shape, dt=F32):
        return nc.dram_tensor(name, list(shape), dt, kind="ExternalInput").ap()

    def dscr(name, shape, dt=BF16):
        return nc.dram_tensor(name, list(shape), dt, kind="Internal").ap()

    xT = din("xT", [nb, D, T])
    cT = din("cT", [128, 8, 5])
    w_ada = din("w_ada", [DEPTH, D, 6 * D])
    w_in = din("w_in", [DEPTH, D, INC])
    w_brc = din("w_br_conv", [DEPTH, D, D])
    w_brs = din("w_br_ssd", [DEPTH, D, D])
    w_brn = din("w_br_na", [DEPTH, D, D])
    w_out = din("w_out", [DEPTH, D, D])
    w_ff1 = din("w_ff1", [DEPTH, D, DFF])
    w_ff2 = din("w_ff2", [DEPTH, DFF, D])
    b_ada_p = din("b_ada_p", [DEPTH, 128, 48])
    norm1_p = din("norm1_p", [DEPTH, 128, 8])
    norm2_p = din("norm2_p", [DEPTH, 128, 8])
    fnorm_p = din("fnorm_p", [128, 8])
    convw_p = din("convw_p", [DEPTH, 128, 8, 3])
    sconvw_p = din("sconvw_p", [DEPTH, 128, 12, 3])
    sconvb_p = din("sconvb_p", [DEPTH, 128, 12])
    dtb_d = din("dt_bias", [DEPTH, 32])
    alog_d = din("a_log", [DEPTH, 32])
    dsk_d = din("ssd_d", [DEPTH, 32])
    snw_d = din("ssd_norm_w", [DEPTH, D])
    TT_d = din("TT", [DEPTH, 64, 16, 15, 64])
    cident = din("c_ident", [128, 128])
    ctriu = din("c_triu", [128, 128])
    ctril = din("c_tril", [128, 128])
    cnegf = din("c_negf", [128, 512])
    cnegb = din("c_negb", [128, 512])
    ccos = din("c_cos", [128, L])
    csin = din("c_sin", [128, L])
    crot = din("c_rot", [128, 128])
    outT = nc.dram_tensor("outT", [nb, D, L], F32, kind="ExternalOutput").ap()

    HT = dscr("HT", [D, T], F32)
    S_gate = dscr("S_gate", [3 * D, T])
    S_xbc = dscr("S_xbc", [1536, T])
    S_dt = dscr("S_dt", [T, 32], F32)
    S_q = dscr("S_q", [D, T])
    S_k = dscr("S_k", [D, T])
    S_v = dscr("S_v", [T, D])
    S_z = dscr("S_z", [T, D])
    Y_conv = dscr("Y_conv", [D, T])
    Y_ssd = dscr("Y_ssd", [D, T])
    Y_na = dscr("Y_na", [D, T])
    YF = dscr("YF", [18, 128, D], F32)
    NBLK = 76
    Wb_in = dscr("Wb_in", [DEPTH, NBLK, 128, 8, 128])
    Wb_zv = dscr("Wb_zv", [DEPTH, D, 2048])
    Wb_dt = dscr("Wb_dt", [DEPTH, D, 32])
    Wb_br = dscr("Wb_br", [DEPTH, 4, D, D])
    Wb_f1 = dscr("Wb_f1", [DEPTH, D, DFF])
    Wb_f2 = dscr("Wb_f2", [DEPTH, DFF, D])
    BLK_COL = [j * 128 for j in range(24)] + [O_XBC + j * 128 for j in range(12)] + [O_Q + j * 128 for j in range(16)] + [O_G + j * 128 for j in range(24)]
    BLK_CB, BLK_CC, BLK_CX, BLK_XBC, BLK_Q, BLK_K, BLK_G = 0, 8, 16, 24, 36, 44, 52

    dbg_out = {}

    def dbg_dump(P, name, src_ap, shape, dt):
        if name in dbg and name not in dbg_out:
            o = nc.dram_tensor("dbg_" + name, list(shape), dt, kind="ExternalOutput").ap()
            dbg_out[name] = o
            P.barrier()
            P.dma("sp", lambda e, o=o: e.dma_start(out=o, in_=src_ap))
            P.barrier()

    def fm(ap2d, t0, n):
        return ap2d[:, t0:t0 + n].rearrange("(k p) t -> p k t", p=128)

    with ExitStack() as es:
        P = Prog(nc, es)
        G = TB(nc, es)
        identf, Ridentf = G([128, 128], F32, "identf")
        ident, Rident = G([128, 128], BF16, "ident")
        onesf, Ronesf = G([128, 128], F32, "onesf")
        onesb, Ronesb = G([128, 128], BF16, "onesb")
        triu, Rtriu = G([128, 128], F32, "triu")
        tril, Rtril = G([128, 128], F32, "tril")
        negf, Rnegf = G([128, 512], F32, "negf")
        negb, Rnegb = G([128, 512], F32, "negb")
        modt, Rmod = G([128, DEPTH, 48, 5], F32, "mod")
        A1, RA1 = G([128, DEPTH, 8, 5], F32, "A1")
        A2, RA2 = G([128, DEPTH, 8, 5], F32, "A2")
        fnw, Rfnw = G([128, 8], F32, "fnw")
        def psum_alloc(scope, nf, nb16):
            TB.gid += 1
            ps_ = [(scope.enter_context(nc.psum_tensor("ps%d_%d" % (i, TB.gid), [128, 512], F32)), Res("ps%d" % i)) for i in range(nf)]
            pb_ = [(scope.enter_context(nc.psum_tensor("psb%d_%d" % (i, TB.gid), [128, 1024], BF16)), Res("psb%d" % i)) for i in range(nb16)]
            return ps_, pb_

        P.dma("sp", lambda e: e.dma_start(out=identf[:], in_=cident), writes=[Ridentf])
        P.dma("sp", lambda e: e.dma_start(out=triu[:], in_=ctriu), writes=[Rtriu])
        P.dma("sp", lambda e: e.dma_start(out=tril[:], in_=ctril), writes=[Rtril])
        P.dma("sp", lambda e: e.dma_start(out=negf[:], in_=cnegf), writes=[Rnegf])
        P.dma("sp", lambda e: e.dma_start(out=negb[:], in_=cnegb), writes=[Rnegb])
        P.dma("sp", lambda e: e.dma_start(out=fnw[:], in_=fnorm_p), writes=[Rfnw])
        P.op("dve", lambda e: e.tensor_copy(out=ident[:], in_=identf[:]), reads=[Ridentf], writes=[Rident])
        P.op("pool", lambda e: e.memset(onesf[:], 1.0), writes=[Ronesf])
        P.op("pool", lambda e: e.memset(onesb[:], 1.0), writes=[Ronesb])

        with ExitStack() as ms:
            M = TB(nc, ms)
            PS, PSB = psum_alloc(ms, 2, 0)
            sc, Rsc = M([128, 8, 5], F32, "sc")
            wa = [M([128, 8, 768], F32, "wa") for _ in range(2)]
            bap, Rbap = M([128, DEPTH, 48], F32, "bap")
            n1, Rn1 = M([128, DEPTH, 8], F32, "n1")
            n2, Rn2 = M([128, DEPTH, 8], F32, "n2")
            P.dma("sp", lambda e: e.dma_start(out=sc[:], in_=cT), writes=[Rsc])
            P.op("act", lambda e: e.activation(out=sc[:], in_=sc[:], func=AF.Silu), reads=[Rsc], writes=[Rsc])
            for l in range(DEPTH):
                P.dma("sp", lambda e, l=l: e.dma_start(out=bap[:, l, :], in_=b_ada_p[l]), writes=[Rbap])
                P.dma("sp", lambda e, l=l: e.dma_start(out=n1[:, l, :], in_=norm1_p[l]), writes=[Rn1])
                P.dma("sp", lambda e, l=l: e.dma_start(out=n2[:, l, :], in_=norm2_p[l]), writes=[Rn2])
            it = 0
            for l in range(DEPTH):
                for g8 in range(8):
                    wt, Rwt = wa[it % 2]
                    it += 1
                    P.dma("sp", lambda e, l=l, g8=g8, wt=wt: e.dma_start(
                        out=wt[:], in_=w_ada[l, :, g8 * 768:(g8 + 1) * 768].rearrange("(k p) c -> p k c", p=128)), writes=[Rwt])
                    ps, Rps = PS[g8 % 2]

                    def mm(e, wt=wt, ps=ps):
                        for j in range(6):
                            for k in range(8):
                                ins = e.matmul(ps[:, j * 8:j * 8 + 5], lhsT=wt[:, k, j * 128:(j + 1) * 128], rhs=sc[:, k, :],
                                               start=(k == 0), stop=(k == 7))
                        return ins
                    P.op("pe", mm, reads=[Rwt, Rsc], writes=[Rps])
                    P.op("dve", lambda e, l=l, g8=g8, ps=ps: e.tensor_tensor(
                        out=modt[:, l, g8 * 6:(g8 + 1) * 6, :], in0=ps[:, 0:48].rearrange("p (j c) -> p j c", c=8)[:, :, 0:5],
                        in1=bap[:, l, g8 * 6:(g8 + 1) * 6].unsqueeze(2).to_broadcast([128, 6, 5]), op=ALU.add),
                        reads=[Rps, Rbap], writes=[Rmod])
            for l in range(DEPTH):
                for (At, RAt, nn, Rnn, j) in ((A1, RA1, n1, Rn1, 1), (A2, RA2, n2, Rn2, 4)):
                    P.op("dve", lambda e, At=At, j=j, l=l: e.tensor_scalar(
                        out=At[:, l, :, :], in0=modt[:, l, j * 8:(j + 1) * 8, :], scalar1=1.0, scalar2=None, op0=ALU.add),
                        reads=[Rmod], writes=[RAt])
                    P.op("dve", lambda e, At=At, nn=nn, l=l: e.tensor_tensor(
                        out=At[:, l, :, :], in0=At[:, l, :, :], in1=nn[:, l, :].unsqueeze(2).to_broadcast([128, 8, 5]), op=ALU.mult),
                        reads=[RAt, Rnn], writes=[RAt])
            P.barrier()
            P.emit()
        if "mod" in dbg:
            o = nc.dram_tensor("dbg_mod", [128, DEPTH, 48, 5], F32, kind="ExternalOutput").ap()
            dbg_out["mod"] = o
            P.dma("sp", lambda e, o=o: e.dma_start(out=o, in_=modt[:]), reads=[Rmod])

        def mcol(l, j, fc, col):
            return modt[:, l, j * 8 + fc, col:col + 1]

        with ExitStack() as cs0:
            Cv = TB(nc, cs0)
            NCB = 4
            cin = [Cv([128, 2048], F32, "cin") for _ in range(NCB)]
            cout = [Cv([128, 2048], BF16, "cout") for _ in range(NCB)]
            cvi = [0]

            deferred = []
            defer_mode = [False]

            def cvt(src_ap, dst_ap, shp):
                if defer_mode[0]:
                    deferred.append((src_ap, dst_ap, shp))
                    return
                i = cvi[0]
                cvi[0] += 1
                ti, Rti = cin[i % NCB]
                to, Rto = cout[i % NCB]
                n = 1
                for d_ in shp[1:]:
                    n *= d_
                if len(shp) == 3:
                    tv = ti[:, 0:n].rearrange("p (a b) -> p a b", b=shp[2])
                    ov = to[:, 0:n].rearrange("p (a b) -> p a b", b=shp[2])
                else:
                    tv = ti[:, 0:n]
                    ov = to[:, 0:n]
                P.dma("sp", lambda e: e.dma_start(out=tv, in_=src_ap), writes=[Rti])
                eng = ("act", "dve", "pool")[i % 3]
                if eng == "act":
                    P.op("act", lambda e: e.copy(out=to[:, 0:n], in_=ti[:, 0:n]), reads=[Rti], writes=[Rto])
                else:
                    P.op(eng, lambda e: e.tensor_copy(out=to[:, 0:n], in_=ti[:, 0:n]), reads=[Rti], writes=[Rto])
                P.dma("sp", lambda e: e.dma_start(out=dst_ap, in_=ov), reads=[Rto])
            for l in range(DEPTH):
                if l not in layers:
                    continue
                defer_mode[0] = (l != layers[0]) and _DEFER_CVT
                for k in range(8):
                    rows = slice(k * 128, (k + 1) * 128)
                    for (b0, nb_) in ((0, 16), (16, 8), (24, 12), (36, 16), (52, 16), (68, 8)):
                        c0 = BLK_COL[b0]
                        cvt(w_in[l, rows, c0:c0 + nb_ * 128].rearrange("p (b c) -> p b c", c=128),
                            Wb_in[l, b0:b0 + nb_, :, k, :].rearrange("b p c -> p b c"), [128, nb_, 128])
                    cvt(w_in[l, rows, O_Z:O_Z + 1024], Wb_zv[l, rows, 0:1024], [128, 1024])
                    cvt(w_in[l, rows, O_V:O_V + 1024], Wb_zv[l, rows, 1024:2048], [128, 1024])
                    cvt(w_in[l, rows, O_DT:O_DT + 32], Wb_dt[l, rows, :], [128, 32])
                    dm_ = defer_mode[0]
                    defer_mode[0] = _DEFER_CVT
                    for c2 in range(2):
                        cvt(w_ff1[l, rows, c2 * 2048:(c2 + 1) * 2048], Wb_f1[l, rows, c2 * 2048:(c2 + 1) * 2048], [128, 2048])
                    defer_mode[0] = dm_
                defer_mode[0] = _DEFER_CVT
                for wi, wd in enumerate((w_brc, w_brs, w_brn, w_out)):
                    for k2 in range(4):
                        cvt(wd[l, k2 * 256:(k2 + 1) * 256, :].rearrange("(k p) c -> p k c", p=128),
                            Wb_br[l, wi, k2 * 256:(k2 + 1) * 256, :].rearrange("(k p) c -> p k c", p=128), [128, 2, 1024])
                for k2 in range(16):
                    cvt(w_ff2[l, k2 * 256:(k2 + 1) * 256, :].rearrange("(k p) c -> p k c", p=128),
                        Wb_f2[l, k2 * 256:(k2 + 1) * 256, :].rearrange("(k p) c -> p k c", p=128), [128, 2, 1024])
            P.barrier()
            P.emit("cvt")

        bg = {"i": 0, "bufs": None}

        def bg_alloc(Sx):
            bg["bufs"] = ([Sx([128, 2048], F32, "bgin") for _ in range(3)], [Sx([128, 2048], BF16, "bgout") for _ in range(3)])

        def bg_cvt(ntask):
            for _ in range(ntask):
                if bg["i"] >= len(deferred) or bg["bufs"] is None:
                    return
                src_ap, dst_ap, shp = deferred[bg["i"]]
                i = bg["i"]
                bg["i"] += 1
                ti, Rti = bg["bufs"][0][i % 3]
                to, Rto = bg["bufs"][1][i % 3]
                n = 1
                for d_ in shp[1:]:
                    n *= d_
                if len(shp) == 3:
                    tv = ti[:, 0:n].rearrange("p (a b) -> p a b", b=shp[2])
                    ov = to[:, 0:n].rearrange("p (a b) -> p a b", b=shp[2])
                else:
                    tv = ti[:, 0:n]
                    ov = to[:, 0:n]
                P.dma("sp", lambda e, tv=tv, src_ap=src_ap: e.dma_start(out=tv, in_=src_ap), writes=[Rti])
                P.op("act", lambda e, to=to, ti=ti, n=n: e.copy(out=to[:, 0:n], in_=ti[:, 0:n]), reads=[Rti], writes=[Rto])
                P.dma("act", lambda e, ov=ov, dst_ap=dst_ap: e.dma_start(out=dst_ap, in_=ov), reads=[Rto])

        def rms_tile(l_or_none, At, col, htile, Rht, n, sq, Rsq, rstd, Rrstd, tmp, Rtmp, outT_tile, Rout, shift_j, psb):
            ps, Rps = psb
            P.op("act", lambda e: e.activation(out=sq[:, :, 0:n], in_=htile[:, :, 0:n], func=AF.Square), reads=[Rht], writes=[Rsq])

            def mm(e):
                for k in range(8):
                    ins = e.matmul(ps[:, 0:n], lhsT=onesf[:], rhs=sq[:, k, 0:n], start=(k == 0), stop=(k == 7))
                return ins
            P.op("pe", mm, reads=[Rsq, Ronesf], writes=[Rps])
            P.op("act", lambda e: e.activation(out=rstd[:, 0:n], in_=ps[:, 0:n], func=AF.Sqrt, bias=EPS, scale=1.0 / D), reads=[Rps], writes=[Rrstd])
            P.op("dve", lambda e: e.reciprocal(out=rstd[:, 0:n], in_=rstd[:, 0:n]), reads=[Rrstd], writes=[Rrstd])
            for k in range(8):
                if l_or_none is None:
                    sc_ap = fnw[:, k:k + 1]
                else:
                    sc_ap = At[:, l_or_none, k, col:col + 1]
                if shift_j is None:
                    P.op("dve", lambda e, k=k, sc_ap=sc_ap: e.scalar_tensor_tensor(
                        out=outT_tile[:, k, 0:n], in0=htile[:, k, 0:n], scalar=sc_ap, in1=rstd[:, 0:n], op0=ALU.mult, op1=ALU.mult),
                        reads=[Rht, Rrstd, RA1, RA2, Rfnw], writes=[Rout])
                else:
                    P.op("dve", lambda e, k=k, sc_ap=sc_ap: e.scalar_tensor_tensor(
                        out=tmp[:, 0:n], in0=htile[:, k, 0:n], scalar=sc_ap, in1=rstd[:, 0:n], op0=ALU.mult, op1=ALU.mult),
                        reads=[Rht, Rrstd, RA1, RA2], writes=[Rtmp])
                    P.op("act", lambda e, k=k: e.activation(out=outT_tile[:, k, 0:n], in_=tmp[:, 0:n], func=AF.Identity,
                                                           bias=mcol(l_or_none, shift_j, k, col), scale=1.0),
                         reads=[Rtmp, Rmod], writes=[Rout])

        TOK_TILES = [(0, 512), (512, 512), (1024, 512), (1536, 512), (2048, 256)]

        def wsrc(wd, l, c0, cw, kc=8):
            return wd[l, :, c0:c0 + cw].rearrange("(k p) c -> p k c", p=128)

        for b in range(nb):
            for l in layers:
                src_h = xT[b] if l == layers[0] else HT
                last = (l == DEPTH - 1)
                P.barrier()
                with ExitStack() as s1:
                    S = TB(nc, s1)
                    PS, PSB = psum_alloc(s1, 6, 0)
                    hmT, RhmT_all = S([128, 8, T], BF16, "hmT")
                    RhmT_t = {t0_: Res("hmT%d" % t0_) for (t0_, n_) in TOK_TILES}
                    hld = [S([128, 8, 512], F32, "hld") for _ in range(2)]
                    sq, Rsq = S([128, 8, 512], BF16, "sq")
                    rstd, Rrstd = S([128, 512], F32, "rstd")
                    tmp, Rtmp = S([128, 512], F32, "tmp")
                    tmpsA = [(tmp, Rtmp)] + [S([128, 512], F32, "tmpA2")]

                    def norm_tile(ti):
                        t0, n = TOK_TILES[ti]
                        ht, Rht = hld[ti % 2]
                        P.dma("sp", lambda e, ht=ht, t0=t0, n=n: e.dma_start(out=ht[:, :, 0:n], in_=fm(src_h, t0, n)), writes=[Rht])
                        col = b if t0 < L else 4
                        class _V:
                            pass
                        hv = hmT[:, :, t0:t0 + n]
                        ps_b = PS[ti % 2]
                        P.op("act", lambda e, ht=ht, n=n: e.activation(out=sq[:, :, 0:n], in_=ht[:, :, 0:n], func=AF.Square), reads=[Rht], writes=[Rsq])
                        ps, Rps = ps_b

                        def mm(e, ps=ps, n=n):
                            for k in range(8):
                                ins = e.matmul(ps[:, 0:n], lhsT=onesb[:], rhs=sq[:, k, 0:n], start=(k == 0), stop=(k == 7))
                            return ins
                        P.op("pe", mm, reads=[Rsq, Ronesb], writes=[Rps])
                        P.op("act", lambda e, ps=ps, n=n: e.activation(out=rstd[:, 0:n], in_=ps[:, 0:n], func=AF.Sqrt, bias=EPS, scale=1.0 / D), reads=[Rps], writes=[Rrstd])
                        P.op("dve", lambda e, n=n: e.reciprocal(out=rstd[:, 0:n], in_=rstd[:, 0:n]), reads=[Rrstd], writes=[Rrstd])
                        for k in range(8):
                            tmp_, Rtmp_ = tmpsA[k % 2]
                            P.op("dve", lambda e, k=k, ht=ht, n=n, col=col, tmp_=tmp_: e.scalar_tensor_tensor(
                                out=tmp_[:, 0:n], in0=ht[:, k, 0:n], scalar=A1[:, l, k, col:col + 1], in1=rstd[:, 0:n], op0=ALU.mult, op1=ALU.mult),
                                reads=[Rht, Rrstd, RA1], writes=[Rtmp_])
                            P.op("act", lambda e, k=k, t0=t0, n=n, col=col, tmp_=tmp_: e.activation(
                                out=hmT[:, k, t0:t0 + n], in_=tmp_[:, 0:n], func=AF.Identity, bias=mcol(l, 0, k, col), scale=1.0),
                                reads=[Rtmp_, Rmod], writes=[RhmT_t[t0]])
                    wb = [S([128, 8, 128], BF16, "wb") for _ in range(7)]
                    wbi = [0]

                    blk_order = ([x for j in range(8) for x in (BLK_CB + j, BLK_CC + j, BLK_CX + j)] + [BLK_XBC + j for j in range(12)]
                                 + [BLK_G + j for j in range(24)] + [BLK_Q + j for j in range(16)])
                    wq_issued = [0]
                    NWB = 7
                    PREF = 5

                    def _issue_until(nmax):
                        while wq_issued[0] < min(nmax, len(blk_order)):
                            i_ = wq_issued[0]
                            wt, Rwt = wb[i_ % NWB]
                            P.dma("sp", lambda e, wt=wt, blk=blk_order[i_]: e.dma_start(out=wt[:], in_=Wb_in[l, blk]), writes=[Rwt])
                            wq_issued[0] += 1

                    def load_w(blk, cw=128):
                        i_ = wbi[0]
                        assert blk_order[i_] == blk, (i_, blk, blk_order[i_])
                        wbi[0] += 1
                        _issue_until(i_ + PREF)
                        return wb[i_ % NWB]
                    psi = [0]

                    def mm_tile(wt, Rwt, t0, n, cw=128):
                        ps, Rps = PS[psi[0] % 4]
                        psi[0] += 1

                        def mm(e):
                            for k in range(8):
                                ins = e.matmul(ps[0:cw, 0:n], lhsT=wt[:, k, 0:cw], rhs=hmT[:, k, t0:t0 + n], start=(k == 0), stop=(k == 7))
                            return ins
                        P.op("pe", mm, reads=[Rwt, RhmT_t[t0]], writes=[Rps])
                        return ps, Rps

                    cxp, Rcxp = S([128, L + 2], F32, "cxp")
                    cxc, Rcxc = S([128, CT + 2], F32, "cxc")
                    cbs, Rcbs = S([128, T], BF16, "cbs")
                    acc, Racc = S([128, T], F32, "acc")
                    ybf, Rybf = S([128, T], BF16, "ybf")
                    csb, Rcsb = S([128, 512], F32, "csb")
                    cw_t, Rcw = S([128, 8, 3], F32, "cw")
                    sw_t, Rsw = S([128, 12, 3], F32, "sw")
                    sb_t, Rsb = S([128, 12], F32, "sb")
                    P.dma("sp", lambda e: e.dma_start(out=cw_t[:], in_=convw_p[l]), writes=[Rcw])
                    P.dma("sp", lambda e: e.dma_start(out=sw_t[:], in_=sconvw_p[l]), writes=[Rsw])
                    P.dma("sp", lambda e: e.dma_start(out=sb_t[:], in_=sconvb_p[l]), writes=[Rsb])
                    P.op("pool", lambda e: e.memset(cxp[:], 0.0), writes=[Rcxp])
                    P.op("pool", lambda e: e.memset(cxc[:], 0.0), writes=[Rcxc])

                    def pad_dst(t0, n):
                        if t0 < L:
                            return cxp[:, 1 + t0:1 + t0 + n], Rcxp
                        return cxc[:, 1:1 + n], Rcxc

                    def conv3(wtile, j, bias_ap):
                        for (pad, Rpad, o0, n) in ((cxp, Rcxp, 0, L), (cxc, Rcxc, L, CT)):
                            if bias_ap is None:
                                P.op("dve", lambda e, pad=pad, o0=o0, n=n: e.tensor_scalar(
                                    out=acc[:, o0:o0 + n], in0=pad[:, 0:n], scalar1=wtile[:, j, 0:1], scalar2=None, op0=ALU.mult),
                                    reads=[Rpad, Rcw, Rsw], writes=[Racc])
                            else:
                                P.op("dve", lambda e, pad=pad, o0=o0, n=n: e.tensor_scalar(
                                    out=acc[:, o0:o0 + n], in0=pad[:, 0:n], scalar1=wtile[:, j, 0:1], scalar2=bias_ap, op0=ALU.mult, op1=ALU.add),
                                    reads=[Rpad, Rcw, Rsw, Rsb], writes=[Racc])
                            for tap in (1, 2):
                                P.op("dve", lambda e, pad=pad, o0=o0, n=n, tap=tap: e.scalar_tensor_tensor(
                                    out=acc[:, o0:o0 + n], in0=pad[:, tap:tap + n], scalar=wtile[:, j, tap:tap + 1], in1=acc[:, o0:o0 + n],
                                    op0=ALU.mult, op1=ALU.add), reads=[Rpad, Racc, Rcw, Rsw], writes=[Racc])

                    cbs2, Rcbs2 = S([128, T], BF16, "cbs2")
                    padsets = [((cxp, Rcxp), (cxc, Rcxc), (cbs, Rcbs)), ((cxp, Rcxp), (cxc, Rcxc), (cbs2, Rcbs2))]

                    def conv_tile(j, wB, wC, wX, t0, n):
                        (cxp_, Rcxp_), (cxc_, Rcxc_), (cbs_, Rcbs_) = padsets[j % 2]
                        pb, Rpb = mm_tile(*wB, t0, n)
                        P.op("act", lambda e: e.copy(out=cbs_[:, t0:t0 + n], in_=pb[:, 0:n]), reads=[Rpb], writes=[Rcbs_])
                        pc, Rpc = mm_tile(*wC, t0, n)
                        P.op("act", lambda e: e.copy(out=csb[:, 0:n], in_=pc[:, 0:n]), reads=[Rpc], writes=[Rcsb])
                        px, Rpx = mm_tile(*wX, t0, n)
                        if t0 < L:
                            dst, Rdst = cxp_[:, 1 + t0:1 + t0 + n], Rcxp_
                        else:
                            dst, Rdst = cxc_[:, 1:1 + n], Rcxc_
                        P.op("dve", lambda e: e.tensor_tensor(out=dst, in0=px[:, 0:n], in1=csb[:, 0:n], op=ALU.mult), reads=[Rpx, Rcsb], writes=[Rdst])

                    def conv_fin(j):
                        (cxp_, Rcxp_), (cxc_, Rcxc_), (cbs_, Rcbs_) = padsets[j % 2]
                        for (pad, Rpad, o0, n) in (((cxp_, Rcxp_, 0, L), (cxc_, Rcxc_, L, CT)) if not last else ((cxp_, Rcxp_, 0, L),)):
                            P.op("dve", lambda e, pad=pad, o0=o0, n=n: e.tensor_scalar(
                                out=acc[:, o0:o0 + n], in0=pad[:, 0:n], scalar1=cw_t[:, j, 0:1], scalar2=None, op0=ALU.mult),
                                reads=[Rpad, Rcw], writes=[Racc])
                            for tap in (1, 2):
                                P.op("dve", lambda e, pad=pad, o0=o0, n=n, tap=tap: e.scalar_tensor_tensor(
                                    out=acc[:, o0:o0 + n], in0=pad[:, tap:tap + n], scalar=cw_t[:, j, tap:tap + 1], in1=acc[:, o0:o0 + n],
                                    op0=ALU.mult, op1=ALU.add), reads=[Rpad, Racc, Rcw], writes=[Racc])
                        P.op("pool", lambda e: e.tensor_tensor(out=ybf[:], in0=acc[:], in1=cbs_[:], op=ALU.mult), reads=[Racc, Rcbs_], writes=[Rybf])
                        P.dma("sp", lambda e: e.dma_start(out=Y_conv[j * 128:(j + 1) * 128, :], in_=ybf[:]), reads=[Rybf])
                    w0 = (load_w(BLK_CB + 0), load_w(BLK_CC + 0), load_w(BLK_CX + 0))
                    TILES_X = TOK_TILES if not last else TOK_TILES[:4]
                    for ti in range(len(TOK_TILES)):
                        norm_tile(ti)
                        if ti >= 1:
                            conv_tile(0, *w0, *TOK_TILES[ti - 1])
                    if not last:
                        conv_tile(0, *w0, *TOK_TILES[-1])
                    conv_fin(0)
                    for j in range(1, 8):
                        wj = (load_w(BLK_CB + j), load_w(BLK_CC + j), load_w(BLK_CX + j))
                        for (t0, n) in TILES_X:
                            conv_tile(j, *wj, t0, n)
                        conv_fin(j)
                    for j in range(12):
                        wX = load_w(BLK_XBC + j)
                        for (t0, n) in TOK_TILES:
                            px, Rpx = mm_tile(*wX, t0, n)
                            dst, Rdst = pad_dst(t0, n)
                            P.op("act", lambda e, px=px, n=n, dst=dst: e.copy(out=dst, in_=px[:, 0:n]), reads=[Rpx], writes=[Rdst])
                        conv3(sw_t, j, sb_t[:, j:j + 1])
                        P.op("act", lambda e: e.activation(out=ybf[:], in_=acc[:], func=AF.Silu), reads=[Racc], writes=[Rybf])
                        P.dma("sp", lambda e, j=j: e.dma_start(out=S_xbc[j * 128:(j + 1) * 128, :], in_=ybf[:]), reads=[Rybf])
                    for j in range(24):
                        wX = load_w(BLK_G + j)
                        for (t0, n) in TILES_X:
                            px, Rpx = mm_tile(*wX, t0, n)
                            P.op("act", lambda e, px=px, t0=t0, n=n: e.activation(out=ybf[:, t0:t0 + n], in_=px[:, 0:n], func=AF.Sigmoid),
                                 reads=[Rpx], writes=[Rybf])
                        P.dma("sp", lambda e, j=j: e.dma_start(out=S_gate[j * 128:(j + 1) * 128, :], in_=ybf[:]), reads=[Rybf])
                    cos_t, Rcos = S([128, L], F32, "cos")
                    sin_t, Rsin = S([128, L], F32, "sin")
                    rot_t, Rrot = S([128, 128], BF16, "rot")
                    rotf, Rrotf = S([128, 128], F32, "rotf")
                    ub, Rub = S([128, 512], BF16, "ub")
                    t1, Rt1 = S([128, 512], F32, "t1")
                    t2, Rt2 = S([128, 512], F32, "t2")
                    P.dma("sp", lambda e: e.dma_start(out=cos_t[:], in_=ccos), writes=[Rcos])
                    P.dma("sp", lambda e: e.dma_start(out=sin_t[:], in_=csin), writes=[Rsin])
                    P.dma("sp", lambda e: e.dma_start(out=rotf[:], in_=crot), writes=[Rrotf])
                    P.op("dve", lambda e: e.tensor_copy(out=rot_t[:], in_=rotf[:]), reads=[Rrotf], writes=[Rrot])
                    ubs = [(ub, Rub)] + [S([128, 512], BF16, "ub2")]
                    t1s = [(t1, Rt1), (t1, Rt1)]
                    t2s = [(t2, Rt2), (t2, Rt2)]
                    ybq = [(ybf, Rybf)] + [S([128, T], BF16, "ybf2")]
                    rix = [0]
                    pend = [None]
                    bix = 0
                    for (o_col, dstS, scl) in ((BLK_Q, S_q, 0.125), (BLK_K, S_k, 1.0)):
                        for j in range(8):
                            wX = load_w(o_col + j)
                            yb_, Ryb_ = ybq[bix % 2]
                            bix += 1
                            for (t0, n) in (TILES_X if o_col == BLK_Q else TOK_TILES):
                                px, Rpx = mm_tile(*wX, t0, n)
                                if t0 >= L:
                                    P.op("act", lambda e, px=px, t0=t0, n=n, scl=scl, yb_=yb_: e.activation(
                                        out=yb_[:, t0:t0 + n], in_=px[:, 0:n], func=AF.Copy, scale=scl), reads=[Rpx], writes=[Ryb_])
                                    continue
                                ub_, Rub_ = ubs[rix[0] % 2]
                                t1_, Rt1_ = t1s[rix[0] % 2]
                                t2_, Rt2_ = t2s[rix[0] % 2]
                                pr, Rpr = PS[4 + (rix[0] % 2)]
                                rix[0] += 1
                                P.op("act", lambda e, px=px, n=n, scl=scl, ub_=ub_: e.activation(out=ub_[:, 0:n], in_=px[:, 0:n], func=AF.Copy, scale=scl),
                                     reads=[Rpx], writes=[Rub_])
                                if pend[0] is not None:
                                    pend[0]()

                                def post(pr=pr, Rpr=Rpr, ub_=ub_, Rub_=Rub_, t1_=t1_, Rt1_=Rt1_, t2_=t2_, Rt2_=Rt2_, t0=t0, n=n, yb_=yb_, Ryb_=Ryb_):
                                    P.op("pe", lambda e: e.matmul(pr[:, 0:n], lhsT=rot_t[:], rhs=ub_[:, 0:n], start=True, stop=True),
                                         reads=[Rrot, Rub_], writes=[Rpr])
                                    P.op("pool", lambda e: e.tensor_tensor(out=t1_[:, 0:n], in0=ub_[:, 0:n], in1=cos_t[:, t0:t0 + n], op=ALU.mult),
                                         reads=[Rub_, Rcos], writes=[Rt1_])
                                    P.op("dve", lambda e: e.tensor_tensor(out=t2_[:, 0:n], in0=pr[:, 0:n], in1=sin_t[:, t0:t0 + n], op=ALU.mult),
                                         reads=[Rpr, Rsin], writes=[Rt2_])
                                    P.op("pool", lambda e: e.tensor_tensor(out=yb_[:, t0:t0 + n], in0=t1_[:, 0:n], in1=t2_[:, 0:n], op=ALU.add),
                                         reads=[Rt1_, Rt2_], writes=[Ryb_])
                                pend[0] = post

                            def fin(j=j, dstS=dstS, yb_=yb_, Ryb_=Ryb_):
                                P.dma("sp", lambda e: e.dma_start(out=dstS[j * 128:(j + 1) * 128, :], in_=yb_[:]), reads=[Ryb_])
                            prev_post = pend[0]

                            def post_and_fin(prev_post=prev_post, fin=fin):
                                prev_post()
                                fin()
                            pend[0] = post_and_fin
                    if pend[0] is not None:
                        pend[0]()
                        pend[0] = None
                    wz, Rwz = S([128, 8, D], BF16, "wz")
                    wv, Rwv = S([128, 8, D], BF16, "wv")
                    wdt, Rwdt = S([128, 8, 32], BF16, "wdt")
                    dtb, Rdtb = S([128, 32], F32, "dtb")
                    P.dma("sp", lambda e: e.dma_start(out=wz[:], in_=Wb_zv[l, :, 0:1024].rearrange("(k p) c -> p k c", p=128)), writes=[Rwz])
                    P.dma("sp", lambda e: e.dma_start(out=wv[:], in_=Wb_zv[l, :, 1024:2048].rearrange("(k p) c -> p k c", p=128)), writes=[Rwv])
                    P.dma("sp", lambda e: e.dma_start(out=wdt[:], in_=Wb_dt[l].rearrange("(k p) c -> p k c", p=128)), writes=[Rwdt])
                    P.dma("sp", lambda e: e.dma_start(out=dtb[:], in_=dtb_d[l:l + 1, :].partition_broadcast(128)), writes=[Rdtb])
                    ztm = [S([128, D], BF16, "ztm") for _ in range(2)]
                    vtm = [S([128, D], BF16, "vtm") for _ in range(2)]
                    dx, Rdx = S([128, 32], F32, "dx")
                    dax, Rdax = S([128, 32], F32, "dax")
                    dout = [S([128, 32], F32, "dout") for _ in range(2)]
                    for tt in range(18):
                        t0 = tt * 128
                        zt, Rzt = ztm[tt % 2]
                        vt, Rvt = vtm[tt % 2]
                        for (wt_, Rw_, dstt, Rd_, fn_) in ((wz, Rwz, zt, Rzt, AF.Silu), (wv, Rwv, vt, Rvt, AF.Copy)):
                            if last and tt >= 16 and fn_ == AF.Silu:
                                continue
                            for hh in range(2):
                                ps, Rps = PS[psi[0] % 4]
                                psi[0] += 1

                                def mm(e, ps=ps, wt_=wt_, hh=hh, t0=t0):
                                    for k in range(8):
                                        ins = e.matmul(ps[:, :], lhsT=hmT[:, k, t0:t0 + 128], rhs=wt_[:, k, hh * 512:(hh + 1) * 512], start=(k == 0), stop=(k == 7))
                                    return ins
                                P.op("pe", mm, reads=[Rw_, RhmT_t[(t0 // 512) * 512]], writes=[Rps])
                                P.op("act", lambda e, ps=ps, dstt=dstt, hh=hh, fn_=fn_: e.activation(out=dstt[:, hh * 512:(hh + 1) * 512], in_=ps[:, :], func=fn_),
                                     reads=[Rps], writes=[Rd_])
                        if not (last and tt >= 16):
                            P.dma("sp", lambda e, zt=zt, t0=t0: e.dma_start(out=S_z[t0:t0 + 128, :], in_=zt[:]), reads=[Rzt])
                        P.dma("sp", lambda e, vt=vt, t0=t0: e.dma_start(out=S_v[t0:t0 + 128, :], in_=vt[:]), reads=[Rvt])
                        ps, Rps = PS[psi[0] % 4]
                        psi[0] += 1

                        def mmd(e, ps=ps, t0=t0):
                            for k in range(8):
                                ins = e.matmul(ps[:, 0:32], lhsT=hmT[:, k, t0:t0 + 128], rhs=wdt[:, k, :], start=(k == 0), stop=(k == 7))
                            return ins
                        P.op("pe", mmd, reads=[Rwdt, RhmT_t[(t0 // 512) * 512]], writes=[Rps])
                        do, Rdo = dout[tt % 2]
                        P.op("dve", lambda e, ps=ps: e.tensor_tensor(out=dx[:], in0=ps[:, 0:32], in1=dtb[:], op=ALU.add), reads=[Rps, Rdtb], writes=[Rdx])
                        P.op("act", lambda e: e.activation(out=dax[:], in_=dx[:], func=AF.Abs), reads=[Rdx], writes=[Rdax])
                        P.op("act", lambda e: e.activation(out=dax[:], in_=dax[:], func=AF.Exp, scale=-1.0), reads=[Rdax], writes=[Rdax])
                        P.op("act", lambda e: e.activation(out=dax[:], in_=dax[:], func=AF.Ln, bias=1.0, scale=1.0), reads=[Rdax], writes=[Rdax])
                        P.op("dve", lambda e, do=do: e.scalar_tensor_tensor(out=do[:], in0=dx[:], scalar=0.0, in1=dax[:], op0=ALU.max, op1=ALU.add),
                             reads=[Rdx, Rdax], writes=[Rdo])
                        P.dma("sp", lambda e, do=do, t0=t0: e.dma_start(out=S_dt[t0:t0 + 128, :], in_=do[:]), reads=[Rdo])
                    P.barrier()
                    P.emit("phA")
                if b == 0 and l == layers[0]:
                    dbg_dump(P, "Y_conv", Y_conv, [D, T], BF16)
                    dbg_dump(P, "S_xbc", S_xbc, [1536, T], BF16)
                    dbg_dump(P, "S_gate", S_gate, [3 * D, T], BF16)
                    dbg_dump(P, "S_q", S_q, [D, T], BF16)
                    dbg_dump(P, "S_k", S_k, [D, T], BF16)
                    dbg_dump(P, "S_v", S_v, [T, D], BF16)
                    dbg_dump(P, "S_z", S_z, [T, D], BF16)
                    dbg_dump(P, "S_dt", S_dt, [T, 32], F32)
                if "stopA" in dbg:
                    continue

                P.barrier()
                with ExitStack() as s2:
                    S = TB(nc, s2)
                    PS, PSB = psum_alloc(s2, 7, 1)
                    psb, Rpsb = PSB[0]
                    alog, Ralog = S([128, 32], F32, "alog")
                    dskt, Rdsk = S([128, 32], F32, "dsk")
                    dsum, Rdsum = S([128, 16], F32, "dsum")
                    snw, Rsnw = S([128, D], F32, "snw")
                    P.dma("sp", lambda e: e.dma_start(out=alog[:], in_=alog_d[l:l + 1, :].partition_broadcast(128)), writes=[Ralog])
                    P.dma("sp", lambda e: e.dma_start(out=dskt[:], in_=dsk_d[l:l + 1, :].partition_broadcast(128)), writes=[Rdsk])
                    P.dma("sp", lambda e: e.dma_start(out=snw[:], in_=snw_d[l:l + 1, :].partition_broadcast(128)), writes=[Rsnw])
                    P.op("act", lambda e: e.activation(out=alog[:], in_=alog[:], func=AF.Exp), reads=[Ralog], writes=[Ralog])
                    P.op("dve", lambda e: e.tensor_scalar(out=alog[:], in0=alog[:], scalar1=-1.0, scalar2=None, op0=ALU.mult), reads=[Ralog], writes=[Ralog])
                    P.op("dve", lambda e: e.tensor_tensor(out=dsum[:], in0=dskt[:, 0:16], in1=dskt[:, 16:32], op=ALU.add), reads=[Rdsk], writes=[Rdsum])
                    hst, Rhst = S([128, D], F32, "hst")
                    hbf, Rhbf = S([128, D], BF16, "hbf")
                    bg_on = (b == 0 and l == layers[0] and bg["i"] < len(deferred))
                    if bg_on:
                        bg_alloc(S)
                    NR = 2

                    def rot(shape, dt, nm, n=NR):
                        return [S(shape, dt, nm) for _ in range(n)]
                    xbcT = rot([128, 12, 128], BF16, "xbcT", 5)
                    dtt = rot([128, 32], F32, "dtt", 5)
                    ztm_ = rot([128, D], BF16, "zt", 6)
                    yfl = rot([128, D], F32, "yfl", 3)
                    xs_tms = rot([128, D], BF16, "xs_tm", 3)
                    B_tms = rot([128, 256], BF16, "B_tm", 3)
                    a_ts = rot([128, 16], F32, "a", 3)
                    cs_ts = rot([128, 16], F32, "cs", 3)
                    ncs_ts = rot([128, 16], F32, "ncs", 3)
                    dout_ts = rot([128, 16], F32, "dout", 3)
                    dst_ts = rot([128, 16], F32, "dst", 3)
                    cdec_ts = rot([128, 16], F32, "cdec", 3)
                    Xds = rot([128, D], BF16, "Xd", 3)
                    Xss = rot([128, D], BF16, "Xs", 3)
                    stsbs = rot([128, D], F32, "stsb", 3)
                    segrs = rot([128, 16, 128], F32, "segr")
                    Lms = rot([128, 16, 128], BF16, "Lm", 2)
                    scTs = rot([128, 256], BF16, "scT", 3)
                    MTs = rot([128, 16, 128], BF16, "MT")
                    yaccs = rot([128, D], F32, "yacc", 3)
                    ytmps = rot([128, D], F32, "ytmp", 1)
                    y3s = rot([128, D], BF16, "y3", 2)
                    pres = rot([128, D], F32, "pre", 4)
                    gsums = rot([128, 2], F32, "gsum", 3)
                    yTs_ = rot([128, 8, 128], BF16, "yT")
                    RYF = [Res("YF%d" % c) for c in range(18)]
                    passes = []
                    for d_ in range(2):
                        order = [16, 17] + list(range(16)) if d_ == 0 else [17, 16] + list(range(15, -1, -1))
                        for oi, c in enumerate(order):
                            passes.append((d_, c, oi == 0))

                    def mk_pass(pi):
                        d_, c, first = passes[pi]
                        tok0 = c * 128
                        tri_d, Rtri_d = (triu, Rtriu) if d_ == 0 else (tril, Rtril)
                        neg_d, Rneg_d = (negf, Rnegf) if d_ == 0 else (negb, Rnegb)
                        xb, Rxb = xbcT[pi % 5]
                        dt_, Rdt_ = dtt[pi % 5]
                        zt, Rzt = ztm_[pi % 6]
                        yf, Ryf = yfl[pi % 3]
                        xs_tm, Rxs = xs_tms[pi % 3]
                        B_tm, RBtm = B_tms[pi % 3]
                        a_t, Ra = a_ts[pi % 3]
                        cs_t, Rcs = cs_ts[pi % 3]
                        ncs_t, Rncs = ncs_ts[pi % 3]
                        dout_t, Rdout = dout_ts[pi % 3]
                        dst_t, Rdst_ = dst_ts[pi % 3]
                        cdec_t, Rcdec = cdec_ts[pi % 3]
                        Xd, RXd = Xds[pi % 3]
                        Xs, RXs = Xss[pi % 3]
                        stsb, Rstsb = stsbs[pi % 3]
                        segr, Rsegr = segrs[pi % NR]
                        Lm, RLm = Lms[pi % 2]
                        scT, RscT = scTs[pi % 3]
                        MT, RMT = MTs[pi % NR]
                        yacc, Ryacc = yaccs[pi % 3]
                        ytmp, Rytmp = ytmps[0]
                        y3, Ry3 = y3s[pi % 2]
                        pre, Rpre = pres[pi % 4]
                        gsum, Rgsum = gsums[pi % 3]
                        yT, RyT = yTs_[pi % NR]
                        p4, Rp4 = PS[4]

                        def ld():
                            P.dma("sp", lambda e: e.dma_start(out=xb[:], in_=S_xbc[:, tok0:tok0 + 128].rearrange("(k p) t -> p k t", p=128)), writes=[Rxb])
                            P.dma("sp", lambda e: e.dma_start(out=dt_[:], in_=S_dt[tok0:tok0 + 128, :]), writes=[Rdt_])
                            if d_ == 1:
                                P.dma("sp", lambda e: e.dma_start(out=zt[:], in_=S_z[tok0:tok0 + 128, :]), writes=[Rzt])

                        def noop():
                            pass

                        def early():
                            if d_ == 1:
                                P.dma("sp", lambda e: e.dma_start(out=yf[:], in_=YF[c]), reads=[RYF[c]], writes=[Ryf])

                            def trx(e):
                                for k in range(8):
                                    ins = e.transpose(psb[:, k * 128:(k + 1) * 128], xb[:, k, :], ident[:])
                                return ins
                            P.op("pe", trx, reads=[Rxb, Rident], writes=[Rpsb])
                            P.op("act", lambda e: e.copy(out=xs_tm[:], in_=psb[:, :]), reads=[Rpsb], writes=[Rxs])
                            P.op("dve", lambda e: e.tensor_tensor(out=a_t[:], in0=dt_[:, d_ * 16:(d_ + 1) * 16], in1=alog[:, d_ * 16:(d_ + 1) * 16], op=ALU.mult),
                                 reads=[Rdt_, Ralog], writes=[Ra])

                            def mcs(e):
                                e.matmul(p4[:, 0:16], lhsT=tri_d[:], rhs=a_t[:], start=True, stop=True)
                                e.matmul(p4[:, 16:32], lhsT=onesf[:], rhs=a_t[:], start=True, stop=True)
                                for g in range(2):
                                    ins = e.matmul(p4[:, 128 + g * 128:256 + g * 128], lhsT=xb[:, 8 + g, :], rhs=xb[:, 10 + g, :], start=True, stop=True)
                                return ins
                            P.op("pe", mcs, reads=[Rtri_d, Ronesf, Ra, Rxb], writes=[Rp4])
                            P.op("dve", lambda e: e.tensor_copy(out=cs_t[:], in_=p4[:, 0:16]), reads=[Rp4], writes=[Rcs, Rp4])
                            P.op("dve", lambda e: e.tensor_tensor(out=dst_t[:], in0=p4[:, 16:32], in1=cs_t[:], op=ALU.subtract), reads=[Rp4, Rcs], writes=[Rdst_, Rp4])
                            P.op("act", lambda e: e.activation(out=cdec_t[:], in_=p4[:, 16:32], func=AF.Exp), reads=[Rp4], writes=[Rcdec, Rp4])
                            P.op("act", lambda e: e.copy(out=scT[:], in_=p4[:, 128:384]), reads=[Rp4], writes=[RscT, Rp4])
                            P.op("dve", lambda e: e.tensor_scalar(out=ncs_t[:], in0=cs_t[:], scalar1=-1.0, scalar2=None, op0=ALU.mult), reads=[Rcs], writes=[Rncs])
                            P.op("act", lambda e: e.activation(out=dout_t[:], in_=cs_t[:], func=AF.Exp), reads=[Rcs], writes=[Rdout])
                            P.op("act", lambda e: e.activation(out=dst_t[:], in_=dst_t[:], func=AF.Exp), reads=[Rdst_], writes=[Rdst_])

                            def trb(e):
                                for g in range(2):
                                    ins = e.transpose(psb[:, g * 128:(g + 1) * 128], xb[:, 8 + g, :], ident[:])
                                return ins
                            P.op("pe", trb, reads=[Rxb, Rident], writes=[Rpsb])
                            P.op("dve", lambda e: e.tensor_copy(out=B_tm[:], in_=psb[:, 0:256]), reads=[Rpsb], writes=[RBtm])
                            P.op("dve", lambda e: e.tensor_tensor(
                                out=Xd[:].rearrange("p (h q) -> p h q", q=64), in0=xs_tm[:].rearrange("p (h q) -> p h q", q=64),
                                in1=dt_[:, d_ * 16:(d_ + 1) * 16].unsqueeze(2).to_broadcast([128, 16, 64]), op=ALU.mult), reads=[Rxs, Rdt_], writes=[RXd])
                            P.op("dve", lambda e: e.tensor_tensor(
                                out=Xs[:].rearrange("p (h q) -> p h q", q=64), in0=Xd[:].rearrange("p (h q) -> p h q", q=64),
                                in1=dst_t[:].unsqueeze(2).to_broadcast([128, 16, 64]), op=ALU.mult), reads=[RXd, Rdst_], writes=[RXs])
                            if d_ == 1:
                                P.op("pool", lambda e: e.tensor_tensor(
                                    out=pre[:].rearrange("p (h q) -> p h q", q=64), in0=xs_tm[:].rearrange("p (h q) -> p h q", q=64),
                                    in1=dsum[:].unsqueeze(2).to_broadcast([128, 16, 64]), op=ALU.mult), reads=[Rxs, Rdsum], writes=[Rpre])
                                P.op("pool", lambda e: e.tensor_tensor(out=pre[:], in0=pre[:], in1=yf[:], op=ALU.add), reads=[Rpre, Ryf], writes=[Rpre])
                            P.op("pool", lambda e: e.tensor_tensor(
                                out=segr[:], in0=tri_d[:].unsqueeze(1).to_broadcast([128, 16, 128]), in1=a_t[:].unsqueeze(2).to_broadcast([128, 16, 128]), op=ALU.mult),
                                reads=[Rtri_d, Ra], writes=[Rsegr])
                            for g in range(2):
                                pst_, Rpst = PS[5 + g]
                                P.op("pe", lambda e, pst_=pst_, g=g: e.matmul(pst_[:, :], lhsT=B_tm[:, g * 128:(g + 1) * 128], rhs=Xs[:, g * 512:(g + 1) * 512], start=True, stop=True),
                                     reads=[RBtm, RXs], writes=[Rpst])
                                P.op("act", lambda e, pst_=pst_, g=g: e.copy(out=stsb[:, g * 512:(g + 1) * 512], in_=pst_[:, :]), reads=[Rpst], writes=[Rstsb, Rpst])

                        def mid():
                            for q4 in range(4):
                                pq, Rpq = PS[q4 % 2]

                                def mseg(e, pq=pq, q4=q4):
                                    e.matmul(pq[:, :], lhsT=onesf[:], rhs=segr[:, q4 * 4:(q4 + 1) * 4, :].rearrange("p h l -> p (h l)"), start=True, stop=False)
                                    return e.matmul(pq[:, :], lhsT=identf[:], rhs=neg_d[:], start=False, stop=True)
                                P.op("pe", mseg, reads=[Rsegr, Ronesf, Ridentf, Rneg_d], writes=[Rpq])

                                def lexp(e, pq=pq, q4=q4):
                                    for hh in range(4):
                                        h_ = q4 * 4 + hh
                                        ins = e.activation(out=Lm[:, h_, :], in_=pq[:, hh * 128:(hh + 1) * 128], func=AF.Exp, bias=ncs_t[:, h_:h_ + 1], scale=1.0)
                                    return ins
                                P.op("act", lexp, reads=[Rpq, Rncs], writes=[RLm])
                            for g in range(2):
                                P.op("dve", lambda e, g=g: e.tensor_tensor(
                                    out=MT[:, g * 8:(g + 1) * 8, :], in0=Lm[:, g * 8:(g + 1) * 8, :],
                                    in1=scT[:, g * 128:(g + 1) * 128].unsqueeze(1).to_broadcast([128, 8, 128]), op=ALU.mult), reads=[RLm, RscT], writes=[RMT])
                            for g in range(2):
                                pyd, Rpyd = PS[2 + g]

                                def myd(e, pyd=pyd, g=g):
                                    for hh in range(8):
                                        h_ = g * 8 + hh
                                        ins = e.matmul(pyd[:, hh * 64:(hh + 1) * 64], lhsT=MT[:, h_, :], rhs=Xd[:, h_ * 64:(h_ + 1) * 64], start=True, stop=True)
                                    return ins
                                P.op("pe", myd, reads=[RMT, RXd], writes=[Rpyd])

                        def late():
                            if first:
                                P.op("pool", lambda e: e.memset(hst[:], 0.0), writes=[Rhst])
                                P.op("pool", lambda e: e.memset(hbf[:], 0.0), writes=[Rhbf])
                            for g in range(2):
                                pyo, Rpyo = PS[5 + g]
                                P.op("pe", lambda e, pyo=pyo, g=g: e.matmul(pyo[:, :], lhsT=xb[:, 10 + g, :], rhs=hbf[:, g * 512:(g + 1) * 512], start=True, stop=True),
                                     reads=[Rxb, Rhbf], writes=[Rpyo])
                            for g in range(2):
                                pyo, Rpyo = PS[5 + g]
                                P.op("dve", lambda e, pyo=pyo, g=g: e.tensor_tensor(
                                    out=yacc[:, g * 512:(g + 1) * 512].rearrange("p (h q) -> p h q", q=64), in0=pyo[:, :].rearrange("p (h q) -> p h q", q=64),
                                    in1=dout_t[:, g * 8:(g + 1) * 8].unsqueeze(2).to_broadcast([128, 8, 64]), op=ALU.mult), reads=[Rpyo, Rdout], writes=[Ryacc, Rpyo])
                                P.op("pool", lambda e, g=g: e.tensor_tensor(
                                    out=hst[:, g * 512:(g + 1) * 512].rearrange("p (h q) -> p h q", q=64), in0=hst[:, g * 512:(g + 1) * 512].rearrange("p (h q) -> p h q", q=64),
                                    in1=cdec_t[:, g * 8:(g + 1) * 8].unsqueeze(2).to_broadcast([128, 8, 64]), op=ALU.mult), reads=[Rhst, Rcdec], writes=[Rhst])
                                P.op("pool", lambda e, g=g: e.tensor_tensor(out=hst[:, g * 512:(g + 1) * 512], in0=stsb[:, g * 512:(g + 1) * 512], in1=hst[:, g * 512:(g + 1) * 512], op=ALU.add),
                                     reads=[Rstsb, Rhst], writes=[Rhst])
                            P.op("act", lambda e: e.copy(out=hbf[:], in_=hst[:]), reads=[Rhst], writes=[Rhbf])
                            for g in range(2):
                                pyd, Rpyd = PS[2 + g]
                                P.op("dve", lambda e, pyd=pyd, g=g: e.tensor_tensor(out=yacc[:, g * 512:(g + 1) * 512], in0=pyd[:, :], in1=yacc[:, g * 512:(g + 1) * 512], op=ALU.add),
                                     reads=[Rpyd, Ryacc], writes=[Ryacc, Rpyd])
                            if d_ == 0:
                                P.dma("sp", lambda e: e.dma_start(out=YF[c], in_=yacc[:]), reads=[Ryacc], writes=[RYF[c]])
                            else:
                                pass

                        def late2():
                            if d_ == 0:
                                return
                            if True:
                                P.op("pool", lambda e: e.tensor_tensor(out=yacc[:], in0=yacc[:], in1=pre[:], op=ALU.add), reads=[Ryacc, Rpre], writes=[Ryacc])
                                P.op("dve", lambda e: e.tensor_tensor(out=yacc[:], in0=yacc[:], in1=zt[:], op=ALU.mult), reads=[Ryacc, Rzt], writes=[Ryacc])
                                for g in range(2):
                                    P.op("act", lambda e, g=g: e.activation(out=ytmp[:, g * 512:(g + 1) * 512], in_=yacc[:, g * 512:(g + 1) * 512], func=AF.Square,
                                                                            accum_out=gsum[:, g:g + 1]), reads=[Ryacc], writes=[Rytmp, Rgsum])

                        def late3():
                            if d_ == 0:
                                return
                            if True:
                                P.op("dve", lambda e: e.tensor_scalar(out=gsum[:], in0=gsum[:], scalar1=1.0 / 512, scalar2=EPS, op0=ALU.mult, op1=ALU.add), reads=[Rgsum], writes=[Rgsum])
                                P.op("act", lambda e: e.activation(out=gsum[:], in_=gsum[:], func=AF.Ln), reads=[Rgsum], writes=[Rgsum])
                                P.op("act", lambda e: e.activation(out=gsum[:], in_=gsum[:], func=AF.Exp, scale=-0.5), reads=[Rgsum], writes=[Rgsum])
                                for g in range(2):
                                    P.op("dve", lambda e, g=g: e.scalar_tensor_tensor(
                                        out=y3[:, g * 512:(g + 1) * 512], in0=yacc[:, g * 512:(g + 1) * 512], scalar=gsum[:, g:g + 1], in1=snw[:, g * 512:(g + 1) * 512],
                                        op0=ALU.mult, op1=ALU.mult), reads=[Ryacc, Rgsum, Rsnw], writes=[Ry3])

                        def fin():
                            if d_ == 0:
                                return

                            def try_(e):
                                for k in range(8):
                                    ins = e.transpose(psb[:, k * 128:(k + 1) * 128], y3[:, k * 128:(k + 1) * 128], ident[:])
                                return ins
                            P.op("pe", try_, reads=[Ry3, Rident], writes=[Rpsb])
                            P.op("act", lambda e: e.copy(out=yT[:].rearrange("p k t -> p (k t)"), in_=psb[:, :]), reads=[Rpsb], writes=[RyT])
                            P.dma("sp", lambda e: e.dma_start(out=Y_ssd[:, tok0:tok0 + 128].rearrange("(k p) t -> p k t", p=128), in_=yT[:]), reads=[RyT])
                        return [ld, noop, early, mid, late, late2, late3, fin]
                    NSTB = 8
                    liveb = {}
                    npass = len(passes)
                    for t_ in range(npass + NSTB - 1):
                        if t_ < npass:
                            liveb[t_] = mk_pass(t_)
                        for k in range(NSTB - 1, -1, -1):
                            u = t_ - k
                            if 0 <= u < npass:
                                liveb[u][k]()
                        liveb.pop(t_ - (NSTB - 1), None)
                        if bg_on:
                            bg_cvt(2)
                    bg["bufs"] = None
                    P.barrier()
                    P.emit("phB")
                if b == 0 and l == layers[0]:
                    dbg_dump(P, "Y_ssd", Y_ssd, [D, T], BF16)
                if "stopB" in dbg:
                    continue
                P.barrier()
                with ExitStack() as s3:
                    S = TB(nc, s3)

                    def pst(shape, dt, nm):
                        TB.gid += 1
                        return (s3.enter_context(nc.psum_tensor("%s_%d" % (nm, TB.gid), shape, dt)), Res(nm))
                    NPX = 2
                    pxs = [pst([128, 512], F32, "px") for _ in range(NPX)]
                    pys = [pst([128, 512], F32, "py") for _ in range(NPX)]
                    ptrs = [pst([128, 1024], BF16, "ptr") for _ in range(2)]
                    pos_ = [pst([128, 512], F32, "po") for _ in range(2)]
                    Vev, RVev = S([128, 18, D], BF16, "Vev")
                    ynat, Rynat = S([128, 18, D], BF16, "ynat")
                    qTs = [S([64, T], BF16, "qT") for _ in range(2)]
                    kTs = [S([64, T], BF16, "kT") for _ in range(2)]
                    TTs = [S([128, 15, 64], F32, "TTh") for _ in range(2)]
                    tbls = [S([128, 5, 832], F32, "tbl") for _ in range(2)]
                    NBUF = 4
                    sls = [S([128, 832], F32, "sl") for _ in range(NBUF)]
                    pbs = [S([128, 832], BF16, "pb") for _ in range(NBUF)]
                    pTs = [S([128, 7, 128], BF16, "pT") for _ in range(NBUF)]
                    sms = [S([128, 4], F32, "sm") for _ in range(NBUF)]
                    units = []
                    for h_ in range(16):
                        for r in range(0, 32, 2):
                            units.append((h_, "lat", r))
                        if not last:
                            units += [(h_, "ctx", 0), (h_, "ctx", 1)]
                    nun = len(units)
                    CASE = {0: 1, 2: 2, 28: 3, 30: 4}

                    def head_load(h_):
                        qT, RqT = qTs[h_ % 2]
                        kT, RkT = kTs[h_ % 2]
                        TTh, RTTh = TTs[h_ % 2]
                        tbl, Rtbl = tbls[h_ % 2]
                        hc0 = h_ * 64
                        P.dma("sp", lambda e: e.dma_start(out=qT[:], in_=S_q[hc0:hc0 + 64, :]), writes=[RqT])
                        P.dma("sp", lambda e: e.dma_start(out=kT[:], in_=S_k[hc0:hc0 + 64, :]), writes=[RkT])
                        P.dma("sp", lambda e: e.dma_start(out=TTh[0:64], in_=TT_d[l, :, h_, :, :]), writes=[RTTh])
                        P.dma("sp", lambda e: e.dma_start(out=TTh[64:128], in_=TT_d[l, :, h_, :, :]), writes=[RTTh])
                        P.op("pool", lambda e: e.memset(tbl[:, :, 0:256], 0.0), writes=[Rtbl])
                        P.op("pool", lambda e: e.memset(tbl[:, :, 256:832], NEG), writes=[Rtbl])
                        for rr, cs_ in ((4, 0), (0, 1), (2, 2), (28, 3), (30, 4)):
                            Rb = min(max(rr - 4, 0), 24)
                            for hf in range(2):
                                row = rr + hf
                                r0row = min(max(row - 4, 0), 24)
                                kr0 = r0row - Rb
                                drs = r0row - row + 7
                                P.op("pool", lambda e, hf=hf, cs_=cs_, kr0=kr0, drs=drs: e.tensor_copy(
                                    out=tbl[hf * 64:(hf + 1) * 64, cs_, 256 + kr0 * 64:256 + kr0 * 64 + 512],
                                    in_=TTh[hf * 64:(hf + 1) * 64, drs:drs + 8, :].rearrange("p a b -> p (a b)")), reads=[RTTh], writes=[Rtbl])

                    def mk_unit(ui):
                        h_, kind, r = units[ui]
                        qT, RqT = qTs[h_ % 2]
                        kT, RkT = kTs[h_ % 2]
                        tbl, Rtbl = tbls[h_ % 2]
                        hc0 = h_ * 64
                        sl, Rsl = sls[ui % NBUF]
                        pb_, Rpb_ = pbs[ui % NBUF]
                        pT, RpT = pTs[ui % NBUF]
                        sm, Rsm = sms[ui % NBUF]
                        px, Rpx = pxs[ui % NPX]
                        py, Rpy = pys[ui % NPX]
                        ptr, Rptr = ptrs[ui % 2]
                        pot, Rpo = pos_[ui % 2]
                        po = pot[:, 0:64]
                        if kind == "lat":
                            Rb = min(max(r - 4, 0), 24)
                            cs_ = CASE.get(r, 0)
                            q0, nk, tix = r * 64, 832, r // 2
                            full9 = (Rb + 8 <= 31)
                            vl = [Vev[:, 16, hc0:hc0 + 64], Vev[:, 17, hc0:hc0 + 64]]
                            for j in range(4):
                                vl.append(Vev[:, Rb // 2 + j, hc0:hc0 + 64])
                            vl.append(Vev[0:64, (Rb + 8) // 2 if full9 else 0, hc0:hc0 + 64])
                        else:
                            q0, nk, tix = L + r * 128, 256, 16 + r
                            vl = [Vev[:, 16, hc0:hc0 + 64], Vev[:, 17, hc0:hc0 + 64]]
                        nch = (nk + 127) // 128

                        def st0():
                            if kind == "lat" and r == 0:
                                if h_ == 0:
                                    head_load(0)
                                    P.dma("sp", lambda e: e.dma_start(out=Vev[:], in_=S_v[:, :].rearrange("(i p) d -> p i d", p=128)), writes=[RVev])
                                if h_ + 1 < 16:
                                    head_load(h_ + 1)

                            def msc(e):
                                ins = e.matmul(px[:, 0:256], lhsT=qT[:, q0:q0 + 128], rhs=kT[:, L:T], start=True, stop=True)
                                if kind == "lat":
                                    ins = e.matmul(px[:, 256:512], lhsT=qT[:, q0:q0 + 128], rhs=kT[:, Rb * 64:Rb * 64 + 256], start=True, stop=True)
                                return ins
                            P.op("pe", msc, reads=[RqT, RkT], writes=[Rpx])
                            if kind == "lat":
                                def msc2(e):
                                    if full9:
                                        return e.matmul(py[:, 0:320], lhsT=qT[:, q0:q0 + 128], rhs=kT[:, Rb * 64 + 256:Rb * 64 + 576], start=True, stop=True)
                                    e.matmul(py[:, 0:256], lhsT=qT[:, q0:q0 + 128], rhs=kT[:, Rb * 64 + 256:Rb * 64 + 512], start=True, stop=True)
                                    return e.matmul(py[:, 256:320], lhsT=qT[:, q0:q0 + 128], rhs=kT[:, 0:64], start=True, stop=True)
                                P.op("pe", msc2, reads=[RqT, RkT], writes=[Rpy])

                        def st1():
                            if kind == "lat":
                                P.op("dve", lambda e: e.tensor_tensor(out=sl[:, 0:512], in0=px[:, 0:512], in1=tbl[:, cs_, 0:512], op=ALU.add),
                                     reads=[Rpx, Rtbl], writes=[Rsl])
                                P.op("dve", lambda e: e.tensor_tensor(out=sl[:, 512:832], in0=py[:, 0:320], in1=tbl[:, cs_, 512:832], op=ALU.add),
                                     reads=[Rpy, Rtbl], writes=[Rsl])
                            else:
                                P.op("dve", lambda e: e.tensor_copy(out=sl[:, 0:256], in_=px[:, 0:256]), reads=[Rpx], writes=[Rsl])
                            P.op("dve", lambda e: e.reduce_max(out=sm[:, 1:2], in_=sl[:, 0:nk], axis=AX.X, negate=True), reads=[Rsl], writes=[Rsm])

                        def st2():
                            P.op("act", lambda e: e.activation(
                                out=pb_[:, 0:nk], in_=sl[:, 0:nk], func=AF.Exp, bias=sm[:, 1:2], scale=1.0, accum_out=sm[:, 2:3]), reads=[Rsl, Rsm], writes=[Rpb_, Rsm])

                        def st3():
                            def trp(e):
                                for j in range(nch):
                                    w_ = min(128, nk - j * 128)
                                    ins = e.transpose(ptr[0:w_, j * 128:(j + 1) * 128], pb_[:, j * 128:j * 128 + w_], ident[:])
                                return ins
                            P.op("pe", trp, reads=[Rpb_, Rident], writes=[Rptr])

                        def st4():
                            def cp(e):
                                nfull = nk // 128
                                ins = e.copy(out=pT[:, 0:nfull, :], in_=ptr[:, 0:nfull * 128].rearrange("p (j q) -> p j q", q=128))
                                if nk % 128:
                                    ins = e.copy(out=pT[0:64, nfull, :], in_=ptr[0:64, nfull * 128:(nfull + 1) * 128])
                                return ins
                            P.op("act", cp, reads=[Rptr], writes=[RpT])

                        def st5():
                            def mpv(e):
                                for j, v in enumerate(vl):
                                    kk = 64 if (kind == "lat" and j == 6) else 128
                                    ins = e.matmul(po, lhsT=pT[0:kk, j, :], rhs=v, start=(j == 0), stop=(j == len(vl) - 1))
                                return ins
                            P.op("pe", mpv, reads=[RpT, RVev], writes=[Rpo])

                        def st6():
                            P.op("dve", lambda e: e.reciprocal(out=sm[:, 3:4], in_=sm[:, 2:3]), reads=[Rsm], writes=[Rsm])
                            P.op("act", lambda e: e.activation(out=ynat[:, tix, hc0:hc0 + 64], in_=po, func=AF.Copy, scale=sm[:, 3:4]),
                                 reads=[Rpo, Rsm], writes=[Rynat, Rpo])
                        return [st0, st1, st2, st3, st4, st5, st6]
                    NST = 7
                    live = {}
                    bg_on_c = (b == 0 and l == layers[0] and bg["i"] < len(deferred))
                    if bg_on_c:
                        bg_alloc(S)
                    for t_ in range(nun + NST - 1):
                        if t_ < nun:
                            live[t_] = mk_unit(t_)
                        for k in range(NST - 1, -1, -1):
                            u = t_ - k
                            if 0 <= u < nun:
                                live[u][k]()
                        live.pop(t_ - (NST - 1), None)
                        if bg_on_c and t_ % 3 == 0:
                            bg_cvt(1)
                    if bg_on_c:
                        bg_cvt(len(deferred))
                        bg["bufs"] = None
                    ntile = 18 if not last else 16
                    yTb = [S([128, 8, 128], BF16, "yTb") for _ in range(2)]
                    for ti in range(ntile):
                        ptr, Rptr = ptrs[ti % 2]
                        yT_, RyT_ = yTb[ti % 2]

                        def try_(e, ptr=ptr, ti=ti):
                            for k in range(8):
                                ins = e.transpose(ptr[:, k * 128:(k + 1) * 128], ynat[:, ti, k * 128:(k + 1) * 128], ident[:])
                            return ins
                        P.op("pe", try_, reads=[Rynat, Rident], writes=[Rptr])
                        P.op("act", lambda e, ptr=ptr, yT_=yT_: e.copy(out=yT_[:].rearrange("p k t -> p (k t)"), in_=ptr[:, :]), reads=[Rptr], writes=[RyT_])
                        P.dma("sp", lambda e, yT_=yT_, ti=ti: e.dma_start(out=Y_na[:, ti * 128:(ti + 1) * 128].rearrange("(k p) t -> p k t", p=128), in_=yT_[:]), reads=[RyT_])
                    P.barrier()
                    P.emit("phC")
                if b == 0 and l == layers[0]:
                    dbg_dump(P, "Y_na", Y_na, [D, T], BF16)
                if "stopC" in dbg:
                    continue
                tiles_d = TOK_TILES if not last else TOK_TILES[:4]
                P.barrier()
                with ExitStack() as s4:
                    S = TB(nc, s4)
                    PS, PSB = psum_alloc(s4, 4, 0)
                    wbr = []
                    for wi in range(4):
                        wt, Rwt = S([128, 8, D], BF16, "wbr")
                        wbr.append((wt, Rwt))

                    def emit_d1_weights():
                        for wi in range(4):
                            wt, Rwt = wbr[wi]
                            P.dma("sp", lambda e, wt=wt, wi=wi: e.dma_start(out=wt[:], in_=Wb_br[l, wi].rearrange("(k p) c -> p k c", p=128)), writes=[Rwt])
                    NT1 = 256
                    NBF1 = 2
                    yss = [[S([128, 8, NT1], BF16, "ys") for _ in range(3)] for _ in range(NBF1)]
                    gts = [S([128, 24, NT1], BF16, "gt") for _ in range(NBF1)]
                    hhs1 = [S([128, 8, NT1], F32, "hh") for _ in range(NBF1 + 1)]
                    mg, Rmg = S([128, 8, NT1], F32, "mg")
                    mb, Rmb = S([128, 8, NT1], BF16, "mb")
                    tmps = [S([128, NT1], F32, "tmpd") for _ in range(2)]
                    ntok1 = T if not last else L
                    tl1 = list(range(0, ntok1, NT1))

                    def d1_load(ti1):
                        t0 = tl1[ti1]
                        n = NT1
                        for bi, Ysrc in enumerate((Y_conv, Y_ssd, Y_na)):
                            yt_, Ryt_ = yss[ti1 % NBF1][bi]
                            P.dma("sp", lambda e, yt_=yt_, Ysrc=Ysrc: e.dma_start(out=yt_[:, :, 0:n], in_=fm(Ysrc, t0, n)), writes=[Ryt_])
                        gt_, Rgt_ = gts[ti1 % NBF1]
                        hh_, Rhh_ = hhs1[ti1 % (NBF1 + 1)]
                        P.dma("sp", lambda e: e.dma_start(out=gt_[:, :, 0:n], in_=fm(S_gate, t0, n)), writes=[Rgt_])
                        P.dma("sp", lambda e: e.dma_start(out=hh_[:, :, 0:n], in_=fm(src_h, t0, n)), writes=[Rhh_])
                    pi = 0
                    d1_load(0)
                    emit_d1_weights()
                    for ti1, t0 in enumerate(tl1):
                        n = NT1
                        if ti1 + 1 < len(tl1):
                            d1_load(ti1 + 1)
                        col = b if t0 < L else 4
                        gt_, Rgt_ = gts[ti1 % NBF1]
                        hh_, Rhh_ = hhs1[ti1 % (NBF1 + 1)]
                        for bi in range(3):
                            yt_, Ryt_ = yss[ti1 % NBF1][bi]
                            wt, Rwt = wbr[bi]
                            for ob in range(8):
                                ps, Rps = PS[pi % 4]
                                tmpd, Rtmpd = tmps[pi % 2]
                                pi += 1

                                def mm(e, ps=ps, wt=wt, yt_=yt_, ob=ob, n=n):
                                    for k in range(8):
                                        ins = e.matmul(ps[:, 0:n], lhsT=wt[:, k, ob * 128:(ob + 1) * 128], rhs=yt_[:, k, 0:n], start=(k == 0), stop=(k == 7))
                                    return ins
                                P.op("pe", mm, reads=[Rwt, Ryt_], writes=[Rps])
                                if bi == 0:
                                    P.op("dve", lambda e, ps=ps, ob=ob, n=n, gt_=gt_: e.tensor_tensor(out=mg[:, ob, 0:n], in0=ps[:, 0:n], in1=gt_[:, ob, 0:n], op=ALU.mult),
                                         reads=[Rps, Rgt_], writes=[Rmg])
                                else:
                                    P.op("dve", lambda e, ps=ps, ob=ob, n=n, bi=bi, tmpd=tmpd, gt_=gt_: e.tensor_tensor(out=tmpd[:, 0:n], in0=ps[:, 0:n], in1=gt_[:, bi * 8 + ob, 0:n], op=ALU.mult),
                                         reads=[Rps, Rgt_], writes=[Rtmpd])
                                    P.op("pool", lambda e, ob=ob, n=n, tmpd=tmpd: e.tensor_tensor(out=mg[:, ob, 0:n], in0=mg[:, ob, 0:n], in1=tmpd[:, 0:n], op=ALU.add),
                                         reads=[Rmg, Rtmpd], writes=[Rmg])
                        P.op("act", lambda e, n=n: e.copy(out=mb[:, :, 0:n], in_=mg[:, :, 0:n]), reads=[Rmg], writes=[Rmb])
                        wt, Rwt = wbr[3]
                        for ob in range(8):
                            ps, Rps = PS[pi % 4]
                            pi += 1

                            def mm(e, ps=ps, wt=wt, ob=ob, n=n):
                                for k in range(8):
                                    ins = e.matmul(ps[:, 0:n], lhsT=wt[:, k, ob * 128:(ob + 1) * 128], rhs=mb[:, k, 0:n], start=(k == 0), stop=(k == 7))
                                return ins
                            P.op("pe", mm, reads=[Rwt, Rmb], writes=[Rps])
                            P.op("dve", lambda e, ps=ps, ob=ob, n=n, col=col, hh_=hh_: e.scalar_tensor_tensor(
                                out=hh_[:, ob, 0:n], in0=ps[:, 0:n], scalar=mcol(l, 2, ob, col), in1=hh_[:, ob, 0:n], op0=ALU.mult, op1=ALU.add),
                                reads=[Rps, Rhh_, Rmod], writes=[Rhh_])
                        P.dma("sp", lambda e, t0=t0, n=n, hh_=hh_: e.dma_start(out=fm(HT, t0, n), in_=hh_[:, :, 0:n]), reads=[Rhh_])
                    P.barrier()
                    P.emit("phD1")
                if b == 0 and l == layers[0]:
                    dbg_dump(P, "HT1", HT, [D, T], F32)
                if "stopD1" in dbg:
                    continue
                P.barrier()
                with ExitStack() as s5:
                    S = TB(nc, s5)
                    PS, PSB = psum_alloc(s5, 8, 0)
                    W1g = [S([128, 8, 512], BF16, "W1g") for _ in range(8)]
                    W2g = [S([128, 8, D], BF16, "W2g") for _ in range(4)]
                    def emit_ffn_weights():
                        for g8 in range(8):
                            w1, Rw1 = W1g[g8]
                            P.dma("sp", lambda e, w1=w1, g8=g8: e.dma_start(out=w1[:], in_=Wb_f1[l, :, g8 * 512:(g8 + 1) * 512].rearrange("(k p) c -> p k c", p=128)), writes=[Rw1])
                        for q4 in range(4):
                            w2, Rw2 = W2g[q4]
                            P.dma("sp", lambda e, w2=w2, q4=q4: e.dma_start(out=w2[:], in_=Wb_f2[l, q4 * 1024:(q4 + 1) * 1024, :].rearrange("(k p) c -> p k c", p=128)), writes=[Rw2])
                    RW2all = [r_ for (_, r_) in W2g]
                    NT2 = 256
                    hhs = [S([128, 8, NT2], F32, "hh2") for _ in range(3)]
                    sqs = [S([128, 8, NT2], BF16, "sq2") for _ in range(2)]
                    rstds = [S([128, NT2], F32, "rstd2") for _ in range(2)]
                    tmps2 = [S([128, NT2], F32, "tmp2") for _ in range(2)]
                    xTs = [S([128, 8, NT2], BF16, "xT2") for _ in range(2)]
                    aT, RaT = S([128, 32, NT2], BF16, "aT")
                    sqf, Rsqf = S([128, 8, NT2], BF16, "sqf")
                    rl = [S([128, NT2], F32, "rl") for _ in range(2)]
                    pi_ = [0]
                    ntok = T if not last else L
                    tl2 = list(range(0, ntok, NT2))
                    n = NT2

                    def mk_t2(ti2):
                        t0 = tl2[ti2]
                        col = b if t0 < L else 4
                        hh_, Rhh_ = hhs[ti2 % 3]
                        sq, Rsq = sqs[ti2 % 2]
                        rstd, Rrstd = rstds[ti2 % 2]
                        tmp, Rtmp = tmps2[ti2 % 2]
                        xTt, RxTt = xTs[ti2 % 2]
                        psn, Rpsn = PS[6]
                        psf, Rpsf = PS[7]

                        def sX():
                            P.dma("sp", lambda e: e.dma_start(out=hh_[:, :, 0:n], in_=fm(HT, t0, n)), writes=[Rhh_])
                            P.op("act", lambda e: e.activation(out=sq[:, :, 0:n], in_=hh_[:, :, 0:n], func=AF.Square), reads=[Rhh_], writes=[Rsq])

                        def sY():
                            def mm(e):
                                for k in range(8):
                                    ins = e.matmul(psn[:, 0:n], lhsT=onesb[:], rhs=sq[:, k, 0:n], start=(k == 0), stop=(k == 7))
                                return ins
                            P.op("pe", mm, reads=[Rsq, Ronesb], writes=[Rpsn])
                            P.op("act", lambda e: e.activation(out=rstd[:, 0:n], in_=psn[:, 0:n], func=AF.Sqrt, bias=EPS, scale=1.0 / D), reads=[Rpsn], writes=[Rrstd])
                            P.op("dve", lambda e: e.reciprocal(out=rstd[:, 0:n], in_=rstd[:, 0:n]), reads=[Rrstd], writes=[Rrstd])
                            for k in range(8):
                                P.op("dve", lambda e, k=k: e.scalar_tensor_tensor(
                                    out=tmp[:, 0:n], in0=hh_[:, k, 0:n], scalar=A2[:, l, k, col:col + 1], in1=rstd[:, 0:n], op0=ALU.mult, op1=ALU.mult),
                                    reads=[Rhh_, Rrstd, RA2], writes=[Rtmp])
                                P.op("act", lambda e, k=k: e.activation(out=xTt[:, k, 0:n], in_=tmp[:, 0:n], func=AF.Identity, bias=mcol(l, 3, k, col), scale=1.0),
                                     reads=[Rtmp, Rmod], writes=[RxTt])

                        def sZ1():
                            for cb in range(32):
                                w1, Rw1 = W1g[cb // 4]
                                c4 = cb % 4
                                ps, Rps = PS[pi_[0] % 4]
                                r_, Rr_ = rl[pi_[0] % 2]
                                pi_[0] += 1

                                def mm(e, ps=ps, w1=w1, c4=c4):
                                    for k in range(8):
                                        ins = e.matmul(ps[:, 0:n], lhsT=w1[:, k, c4 * 128:(c4 + 1) * 128], rhs=xTt[:, k, 0:n], start=(k == 0), stop=(k == 7))
                                    return ins
                                P.op("pe", mm, reads=[Rw1, RxTt], writes=[Rps])
                                P.op("act", lambda e, ps=ps, r_=r_: e.activation(out=r_[:, 0:n], in_=ps[:, 0:n], func=AF.Relu), reads=[Rps], writes=[Rr_])
                                P.op("pool", lambda e, r_=r_, cb=cb: e.tensor_tensor(out=aT[:, cb, 0:n], in0=r_[:, 0:n], in1=r_[:, 0:n], op=ALU.mult), reads=[Rr_], writes=[RaT])

                        def sZ2():
                            for ob in range(8):
                                ps, Rps = PS[4 + (pi_[0] % 2)]
                                pi_[0] += 1

                                def mm(e, ps=ps, ob=ob):
                                    for k in range(32):
                                        ins = e.matmul(ps[:, 0:n], lhsT=W2g[k // 8][0][:, k % 8, ob * 128:(ob + 1) * 128], rhs=aT[:, k, 0:n], start=(k == 0), stop=(k == 31))
                                    return ins
                                P.op("pe", mm, reads=RW2all + [RaT], writes=[Rps])
                                P.op("dve", lambda e, ps=ps, ob=ob: e.scalar_tensor_tensor(
                                    out=hh_[:, ob, 0:n], in0=ps[:, 0:n], scalar=mcol(l, 5, ob, col), in1=hh_[:, ob, 0:n], op0=ALU.mult, op1=ALU.add),
                                    reads=[Rps, Rhh_, Rmod], writes=[Rhh_])
                            if not last:
                                P.dma("sp", lambda e: e.dma_start(out=fm(HT, t0, n), in_=hh_[:, :, 0:n]), reads=[Rhh_])
                            else:
                                P.op("act", lambda e: e.activation(out=sqf[:, :, 0:n], in_=hh_[:, :, 0:n], func=AF.Square), reads=[Rhh_], writes=[Rsqf])

                        def sW():
                            if not last:
                                return

                            def mm(e):
                                for k in range(8):
                                    ins = e.matmul(psf[:, 0:n], lhsT=onesb[:], rhs=sqf[:, k, 0:n], start=(k == 0), stop=(k == 7))
                                return ins
                            P.op("pe", mm, reads=[Rsqf, Ronesb], writes=[Rpsf])
                            P.op("act", lambda e: e.activation(out=rstd[:, 0:n], in_=psf[:, 0:n], func=AF.Sqrt, bias=EPS, scale=1.0 / D), reads=[Rpsf], writes=[Rrstd])
                            P.op("dve", lambda e: e.reciprocal(out=rstd[:, 0:n], in_=rstd[:, 0:n]), reads=[Rrstd], writes=[Rrstd])
                            for k in range(8):
                                P.op("dve", lambda e, k=k: e.scalar_tensor_tensor(
                                    out=hh_[:, k, 0:n], in0=hh_[:, k, 0:n], scalar=fnw[:, k:k + 1], in1=rstd[:, 0:n], op0=ALU.mult, op1=ALU.mult),
                                    reads=[Rhh_, Rrstd, Rfnw], writes=[Rhh_])
                            P.dma("sp", lambda e: e.dma_start(out=fm(outT[b], t0, n), in_=hh_[:, :, 0:n]), reads=[Rhh_])
                        return dict(X=sX, Y=sY, Z1=sZ1, Z2=sZ2, W=sW)
                    nt2 = len(tl2)
                    st2 = {}
                    for t_ in range(-2, nt2 + 1):
                        for (nm_, off_) in (("W", -1), ("Y", 1), ("Z1", 0), ("X", 2), ("Z2", 0)):
                            u = t_ + off_
                            if 0 <= u < nt2:
                                if u not in st2:
                                    st2[u] = mk_t2(u)
                                st2[u][nm_]()
                        if t_ == -1:
                            emit_ffn_weights()
                    P.barrier()
                    P.emit("phD2")
                if b == 0 and l == layers[0]:
                    dbg_dump(P, "HT2", HT, [D, T], F32)
        P.barrier()
        P.emit()
    return nc, dbg_out


def _consts():
    k = np.arange(128)
    ident = np.eye(128, dtype=np.float32)
    triu = (k[:, None] <= k[None, :]).astype(np.float32)
    tril = (k[:, None] >= k[None, :]).astype(np.float32)
    negf1 = np.where(k[None, :] < k[:, None], NEG, 0.0).astype(np.float32)
    negb1 = np.where(k[None, :] > k[:, None], NEG, 0.0).astype(np.float32)
    negf = np.tile(negf1, (1, 4))
    negb = np.tile(negb1, (1, 4))
    t = np.arange(L, dtype=np.int32)
    row = (t // 64).astype(np.float32)
    col = (t % 64).astype(np.float32)
    half = 32
    inv = (np.float32(10000.0) ** (-np.arange(0, half, 2, dtype=np.float32) / np.float32(half))).astype(np.float32)
    ang_r = row[:, None] * inv
    ang_c = col[:, None] * inv
    ang = np.concatenate([ang_r, ang_r, ang_c, ang_c], axis=-1)
    cos = np.cos(ang).astype(np.float32).T
    sin = np.sin(ang).astype(np.float32).T
    cos = np.concatenate([cos, cos], axis=0)
    sin = np.concatenate([sin, sin], axis=0)
    rot = np.zeros((128, 128), np.float32)
    for m in range(128):
        if (m % 32) < 16:
            rot[m + 16, m] = -1.0
        else:
            rot[m - 16, m] = 1.0
    return dict(c_ident=ident, c_triu=triu, c_tril=tril, c_negf=negf, c_negb=negb,
                c_cos=np.ascontiguousarray(cos), c_sin=np.ascontiguousarray(sin), c_rot=rot)


def _shared_inputs(inp):
    f = lambda a: np.ascontiguousarray(np.asarray(a, dtype=np.float32))
    d = {}
    for k_ in ("w_ada", "w_in", "w_br_conv", "w_br_ssd", "w_br_na", "w_out", "w_ff1", "w_ff2"):
        d[k_] = f(inp[k_])
    d["b_ada_p"] = f(np.asarray(inp["b_ada"]).reshape(DEPTH, 48, 128).transpose(0, 2, 1))
    d["norm1_p"] = f(np.asarray(inp["norm1_w"]).reshape(DEPTH, 8, 128).transpose(0, 2, 1))
    d["norm2_p"] = f(np.asarray(inp["norm2_w"]).reshape(DEPTH, 8, 128).transpose(0, 2, 1))
    d["fnorm_p"] = f(np.asarray(inp["final_norm_w"]).reshape(8, 128).T)
    d["convw_p"] = f(np.asarray(inp["conv_mix_w"]).reshape(DEPTH, 3, 8, 128).transpose(0, 3, 2, 1))
    d["sconvw_p"] = f(np.asarray(inp["ssd_conv_w"]).reshape(DEPTH, 3, 12, 128).transpose(0, 3, 2, 1))
    d["sconvb_p"] = f(np.asarray(inp["ssd_conv_b"]).reshape(DEPTH, 12, 128).transpose(0, 2, 1))
    d["dt_bias"] = f(np.asarray(inp["ssd_dt_bias"]).reshape(DEPTH, 32))
    d["a_log"] = f(np.asarray(inp["ssd_a_log"]).reshape(DEPTH, 32))
    d["ssd_d"] = f(np.asarray(inp["ssd_d"]).reshape(DEPTH, 32))
    d["ssd_norm_w"] = f(inp["ssd_norm_w"])
    colv = np.arange(64)
    col_start = np.clip(colv - 8, 0, 48)
    col_ok = (colv[None, :] >= col_start[:, None]) & (colv[None, :] < col_start[:, None] + 16)
    dc_idx = np.clip(colv[None, :] - colv[:, None], -15, 15) + 15
    rpb = np.asarray(inp["na_rpb"], dtype=np.float32)
    g = rpb[:, :, :, dc_idx]
    g = np.where(col_ok[None, None, None], g, np.float32(NEG))
    d["TT"] = f(g.transpose(0, 3, 1, 2, 4))
    d.update(_consts())
    return d


def _core_inputs(inp, shared, b0, nb):
    x = np.asarray(inp["x"], dtype=np.float32)
    ctx = np.asarray(inp["ctx"], dtype=np.float32)
    c = np.asarray(inp["c"], dtype=np.float32)
    c_ctx = np.asarray(inp["c_ctx"], dtype=np.float32)
    xT = np.concatenate([x[b0:b0 + nb].transpose(0, 2, 1), ctx[b0:b0 + nb].transpose(0, 2, 1)], axis=2)
    cm = np.zeros((5, D), np.float32)
    cm[0:nb] = c[b0:b0 + nb]
    cm[4] = c_ctx
    cT = cm.T.reshape(8, 128, 5).transpose(1, 0, 2)
    m = dict(shared)
    m["xT"] = np.ascontiguousarray(xT)
    m["cT"] = np.ascontiguousarray(cT)
    return m


_CACHE = {}


def kernel(**inputs):
    if "nc" not in _CACHE:
        _CACHE["nc"] = build()[0]
    nc = _CACHE["nc"]
    shared = _shared_inputs(inputs)
    in_maps = [_core_inputs(inputs, shared, c * NB, NB) for c in range(NCORES)]
    res = run_bass_kernel_spmd(nc, in_maps, core_ids=list(range(NCORES)))
    outs = [np.asarray(r["outT"]).transpose(0, 2, 1) for r in res.results]
    return np.ascontiguousarray(np.concatenate(outs, axis=0).astype(np.float32))
```
